# Optimizing a Trainium2 kernel written in Bass

```python
import jax, jax.numpy as jnp
from jax import lax
import numpy as np


D_MODEL = 1024
BATCH = 4
SEQ = 8192
DEPTH = 2

MEM_TOKENS = 256
N_BRANCH = 4
BRANCH_DIM = 512
CONV_DIM = 512
CONV_WIDTH = 3
DSA_HEADS = 8
DSA_HEAD_DIM = 64
IDX_HEADS = 8
IDX_DIM = 64
TOPK_MAX = 256
Q_BLOCK = 128
HG_HEADS = 4
HG_DK = 128
HG_DV = 128
HG_CHUNK = 64
MEM_HEADS = 4
MEM_HEAD_DIM = 128
ROPE_THETA = 500000.0
ROT_DIM = DSA_HEAD_DIM // 4
FFN_DIM = ((8 * D_MODEL // 3 + 255) // 256) * 256
EPS = 1e-6

SPLIT_SIZES = (CONV_DIM, CONV_DIM, CONV_DIM,
               DSA_HEADS * DSA_HEAD_DIM, DSA_HEAD_DIM, DSA_HEAD_DIM,
               IDX_HEADS * IDX_DIM, IDX_DIM, IDX_HEADS,
               HG_HEADS * HG_DK, HG_HEADS * HG_DK, HG_HEADS * HG_DV, HG_HEADS * HG_DV,
               MEM_HEADS * MEM_HEAD_DIM,
               N_BRANCH * D_MODEL)
IN_COLS = sum(SPLIT_SIZES)

kernel_name = 'hybrid_gated_conv_dsa_hgrn2_mem'


def rmsnorm(x, g):
    xf = x.astype(jnp.float32)
    y = xf * lax.rsqrt(jnp.mean(xf * xf, axis=-1, keepdims=True) + EPS)
    return (y * g.astype(jnp.float32)).astype(x.dtype)


def rope_tables(positions, dtype):
    inv_freq = 1.0 / (ROPE_THETA ** (jnp.arange(0, ROT_DIM, 2, dtype=jnp.float32) / ROT_DIM))
    ang = positions.astype(jnp.float32)[..., None] * inv_freq
    return jnp.cos(ang)[:, :, None, :].astype(dtype), jnp.sin(ang)[:, :, None, :].astype(dtype)


def partial_rope(x, cos, sin):
    half = ROT_DIM // 2
    x1, x2, xp = x[..., :half], x[..., half:ROT_DIM], x[..., ROT_DIM:]
    return jnp.concatenate([x1 * cos - x2 * sin, x1 * sin + x2 * cos, xp], axis=-1)


def split_cols(z):
    idx = [int(i) for i in np.cumsum(SPLIT_SIZES)[:-1]]
    return jnp.split(z, idx, axis=-1)


def causal_conv3(u, w):
    up = jnp.pad(u, ((0, 0), (CONV_WIDTH - 1, 0), (0, 0)))
    s = u.shape[1]
    return up[:, 0:s] * w[0] + up[:, 1:s + 1] * w[1] + up[:, 2:s + 2] * w[2]


def dsa_attention(q, k, v, iq, ik, iw):
    b, s = q.shape[0], q.shape[1]
    nb = s // Q_BLOCK
    topk = min(TOPK_MAX, s // 4)
    key_pos = jnp.arange(s)
    idx_scale = IDX_DIM ** -0.5
    att_scale = DSA_HEAD_DIM ** -0.5

    def blockify(a):
        return a.reshape(b, nb, Q_BLOCK, *a.shape[2:]).swapaxes(0, 1)

    def one_block(args):
        qb, iqb, iwb, qpos = args
        rel = jax.nn.relu(jnp.einsum('bthd,bsd->bths', iqb, ik) * idx_scale)
        score = jnp.einsum('bths,bth->bts', rel, iwb).astype(jnp.float32)
        causal = key_pos[None, :] <= qpos[:, None]
        score = jnp.where(causal[None], score, -jnp.inf)
        _, sel = lax.top_k(score, topk)
        valid = sel <= qpos[None, :, None]
        k_sel = jax.vmap(lambda kb, ib: kb[ib])(k, sel)
        v_sel = jax.vmap(lambda vb, ib: vb[ib])(v, sel)
        logits = jnp.einsum('bthd,btkd->bthk', qb, k_sel).astype(jnp.float32) * att_scale
        logits = jnp.where(valid[:, :, None, :], logits, -jnp.inf)
        p = jax.nn.softmax(logits, axis=-1).astype(v.dtype)
        return jnp.einsum('bthk,btkd->bthd', p, v_sel)

    out = lax.map(one_block, (blockify(q), blockify(iq), blockify(iw),
                              jnp.arange(s).reshape(nb, Q_BLOCK)))
    return out.swapaxes(0, 1).reshape(b, s, q.shape[2], q.shape[3])


def hgrn2_chunkwise(q, k, v, log_f):
    q, k, v, log_f = (a.astype(jnp.float32) for a in (q, k, v, log_f))
    b, s, h, dk = q.shape
    dv = v.shape[-1]
    nc = s // HG_CHUNK

    def to_chunks(a):
        return a.reshape(b, nc, HG_CHUNK, h, a.shape[-1]).transpose(1, 0, 3, 2, 4)

    qc, kc, vc, gc = (to_chunks(a) for a in (q, k, v, log_f))
    cum = jnp.cumsum(gc, axis=3)
    ref = cum[:, :, :, HG_CHUNK // 2 - 1:HG_CHUNK // 2, :]
    last = cum[:, :, :, -1:, :]
    tri = jnp.tril(jnp.ones((HG_CHUNK, HG_CHUNK), dtype=bool))
    a_intra = jnp.einsum('nbhtk,nbhsk->nbhts', qc * jnp.exp(cum - ref), kc * jnp.exp(ref - cum))
    o_intra = jnp.einsum('nbhts,nbhsv->nbhtv', jnp.where(tri, a_intra, 0.0), vc)
    q_inter = qc * jnp.exp(cum)
    k_state = kc * jnp.exp(last - cum)
    decay_last = jnp.exp(last[:, :, :, 0, :])

    def step(state, inp):
        qi, ks, vv, dl = inp
        o = jnp.einsum('bhtk,bhkv->bhtv', qi, state)
        state = dl[..., None] * state + jnp.einsum('bhtk,bhtv->bhkv', ks, vv)
        return state, o

    s0 = jnp.zeros((b, h, dk, dv), jnp.float32)
    _, o_inter = lax.scan(step, s0, (q_inter, k_state, vc, decay_last))
    o = o_intra + o_inter
    return o.transpose(1, 0, 3, 2, 4).reshape(b, s, h, dv)


def hybrid_layer(x, mem_n, cos, sin, lb, norm_mix, w_in, conv_w, dsa_q_norm, dsa_k_norm,
                 hgrn_out_norm, mem_w_kv, mem_q_norm, mem_k_norm, w_lift, w_out,
                 norm_ffn, ffn_w_up, ffn_w_down):
    b, s, _ = x.shape
    h = rmsnorm(x, norm_mix)
    (a_x, a_b, a_c, d_q, d_k, d_v, i_q, i_k, i_w,
     g_q, g_f, g_i, g_g, m_q, gate_logits) = split_cols(h @ w_in)

    y_a = a_b * causal_conv3(a_c * a_x, conv_w)

    q = partial_rope(rmsnorm(d_q.reshape(b, s, DSA_HEADS, DSA_HEAD_DIM), dsa_q_norm), cos, sin)
    k = partial_rope(rmsnorm(d_k.reshape(b, s, 1, DSA_HEAD_DIM), dsa_k_norm), cos, sin)[:, :, 0]
    iq = partial_rope(i_q.reshape(b, s, IDX_HEADS, IDX_DIM), cos, sin)
    ik = partial_rope(i_k.reshape(b, s, 1, IDX_DIM), cos, sin)[:, :, 0]
    iw = i_w * (IDX_HEADS ** -0.5)
    y_b = dsa_attention(q, k, d_v, iq, ik, iw).reshape(b, s, DSA_HEADS * DSA_HEAD_DIM)

    hq = jax.nn.silu(g_q).reshape(b, s, HG_HEADS, HG_DK)
    zf = g_f.astype(jnp.float32).reshape(b, s, HG_HEADS, HG_DK)
    lbh = lb.reshape(HG_HEADS, HG_DK)
    log_f = jnp.logaddexp(jnp.log(lbh), jnp.log1p(-lbh) + jax.nn.log_sigmoid(zf))
    hk = -jnp.expm1(log_f)
    hv = g_i.reshape(b, s, HG_HEADS, HG_DV)
    o_c = hgrn2_chunkwise(hq, hk, hv, log_f).astype(x.dtype)
    o_c = rmsnorm(o_c, hgrn_out_norm) * jax.nn.silu(g_g.reshape(b, s, HG_HEADS, HG_DV))
    y_c = o_c.reshape(b, s, HG_HEADS * HG_DV)

    mk, mv = jnp.split(mem_n @ mem_w_kv, 2, axis=-1)
    mk = rmsnorm(mk.reshape(b, MEM_TOKENS, MEM_HEADS, MEM_HEAD_DIM), mem_k_norm)
    mv = mv.reshape(b, MEM_TOKENS, MEM_HEADS, MEM_HEAD_DIM)
    mq = rmsnorm(m_q.reshape(b, s, MEM_HEADS, MEM_HEAD_DIM), mem_q_norm)
    ml = jnp.einsum('bshd,bmhd->bhsm', mq, mk).astype(jnp.float32) * (MEM_HEAD_DIM ** -0.5)
    mp = jax.nn.softmax(ml, axis=-1).astype(mv.dtype)
    y_m = jnp.einsum('bhsm,bmhd->bshd', mp, mv).reshape(b, s, MEM_HEADS * MEM_HEAD_DIM)

    gates = jax.nn.sigmoid(gate_logits).reshape(b, s, N_BRANCH, D_MODEL)
    branches = (y_a, y_b, y_c, y_m)
    merged = gates[:, :, 0] * (branches[0] @ w_lift[0])
    for n in range(1, N_BRANCH):
        merged = merged + gates[:, :, n] * (branches[n] @ w_lift[n])
    x = x + merged @ w_out

    hf = rmsnorm(x, norm_ffn)
    gate, up = jnp.split(hf @ ffn_w_up, 2, axis=-1)
    return x + (jax.nn.silu(gate) * up) @ ffn_w_down


def setup_inputs(seed: int = 0) -> dict:
    key = jax.random.key(seed)
    ks = jax.random.split(key, 20)
    f32 = jnp.float32

    def nrm(k, shape, scale):
        return jax.random.normal(k, shape, f32) * scale

    def gain(k, shape):
        return 1.0 + 0.02 * jax.random.normal(k, shape, f32)

    return {
        'x': nrm(ks[0], (BATCH, SEQ, D_MODEL), 1.0),
        'mem': nrm(ks[1], (BATCH, MEM_TOKENS, D_MODEL), 1.0),
        'positions': jnp.broadcast_to(jnp.arange(SEQ, dtype=jnp.int32), (BATCH, SEQ)),
        'norm_mix': gain(ks[2], (DEPTH, D_MODEL)),
        'w_in': nrm(ks[3], (DEPTH, D_MODEL, IN_COLS), D_MODEL ** -0.5),
        'conv_w': nrm(ks[4], (DEPTH, CONV_WIDTH, CONV_DIM), CONV_WIDTH ** -0.5),
        'dsa_q_norm': gain(ks[5], (DEPTH, DSA_HEAD_DIM)),
        'dsa_k_norm': gain(ks[6], (DEPTH, DSA_HEAD_DIM)),
        'hgrn_lower_bounds': nrm(ks[7], (DEPTH, HG_HEADS * HG_DK), 0.5),
        'hgrn_out_norm': gain(ks[8], (DEPTH, HG_DV)),
        'mem_norm': gain(ks[9], (D_MODEL,)),
        'mem_w_kv': nrm(ks[10], (DEPTH, D_MODEL, 2 * MEM_HEADS * MEM_HEAD_DIM), D_MODEL ** -0.5),
        'mem_q_norm': gain(ks[11], (DEPTH, MEM_HEAD_DIM)),
        'mem_k_norm': gain(ks[12], (DEPTH, MEM_HEAD_DIM)),
        'w_lift': nrm(ks[13], (DEPTH, N_BRANCH, BRANCH_DIM, D_MODEL), BRANCH_DIM ** -0.5),
        'w_out': nrm(ks[14], (DEPTH, D_MODEL, D_MODEL), D_MODEL ** -0.5),
        'norm_ffn': gain(ks[15], (DEPTH, D_MODEL)),
        'ffn_w_up': nrm(ks[16], (DEPTH, D_MODEL, 2 * FFN_DIM), D_MODEL ** -0.5),
        'ffn_w_down': nrm(ks[17], (DEPTH, FFN_DIM, D_MODEL), FFN_DIM ** -0.5),
    }


def reference(x, mem, positions, norm_mix, w_in, conv_w, dsa_q_norm, dsa_k_norm,
              hgrn_lower_bounds, hgrn_out_norm, mem_norm, mem_w_kv, mem_q_norm, mem_k_norm,
              w_lift, w_out, norm_ffn, ffn_w_up, ffn_w_down):
    cos, sin = rope_tables(positions, x.dtype)
    mem_n = rmsnorm(mem, mem_norm)
    lbs = jnp.cumsum(jax.nn.softmax(hgrn_lower_bounds.astype(jnp.float32), axis=0), axis=0)
    lbs = lbs - lbs[0:1]
    for l in range(DEPTH):
        x = hybrid_layer(x, mem_n, cos, sin, lbs[l], norm_mix[l], w_in[l], conv_w[l],
                         dsa_q_norm[l], dsa_k_norm[l], hgrn_out_norm[l], mem_w_kv[l],
                         mem_q_norm[l], mem_k_norm[l], w_lift[l], w_out[l],
                         norm_ffn[l], ffn_w_up[l], ffn_w_down[l])
    return x
```

```python
import numpy as np
from contextlib import ExitStack
import concourse.bass as bass
import concourse.mybir as mybir
from concourse.bass_utils import run_bass_kernel_spmd

F32 = mybir.dt.float32
BF16 = mybir.dt.bfloat16
I32 = mybir.dt.int32
ALU = mybir.AluOpType
AF = mybir.ActivationFunctionType
AX = mybir.AxisListType

D = 1024
IN_COLS = 9416
FFN = 2816
MEMT = 256
EPS = 1e-6
STAGE = 99
NOCAST = False
C_AX, C_AB, C_AC = 0, 512, 1024
C_DQ, C_DK, C_DV = 1536, 2048, 2112
C_IQ, C_IK, C_IW = 2176, 2688, 2752
C_GQ, C_GF, C_GI, C_GG = 2760, 3272, 3784, 4296
C_MQ = 4808
C_GATE = 5320


class Res:
    __slots__ = ("name", "w", "r", "excl")

    def __init__(self, name, excl=False):
        self.name = name
        self.w = None
        self.r = {}
        self.excl = excl


class Sched:
    def __init__(self, nc, stack):
        self.nc = nc
        self.eng = dict(pe=nc.tensor, act=nc.scalar, dve=nc.vector, pool=nc.gpsimd, sp=nc.sync)
        self.stack = stack
        self.prog = {e: stack.enter_context(nc.semaphore("prog_" + e)) for e in self.eng}
        self.cnt = {e: 0 for e in self.eng}
        self.waited = {e: {} for e in self.eng}
        self.dsem = {}
        self.nwait = 0

    def dma_sem(self, key):
        if key not in self.dsem:
            self.dsem[key] = [self.stack.enter_context(self.nc.semaphore("d_" + key)), 0]
        return key

    def _wait(self, e, tok):
        sem, val, src, key = tok
        if src == e and e == "pe":
            return
        if self.waited[e].get(key, 0) >= val:
            return
        self.eng[e].wait_ge(sem, val)
        self.waited[e][key] = val
        self.nwait += 1

    def _deps(self, e, reads, writes):
        for r in reads:
            if r.w is not None:
                self._wait(e, r.w)
            if r.excl:
                for tok in r.r.values():
                    if tok[2] != e:
                        self._wait(e, tok)
        for w in writes:
            if w.w is not None:
                self._wait(e, w.w)
            for tok in w.r.values():
                self._wait(e, tok)

    def _mark(self, tok, reads, writes):
        key = tok[3]
        for r in reads:
            r.r[key] = tok
        for w in writes:
            w.w = tok
            w.r = {}

    def op(self, e, fn, reads=(), writes=()):
        self._deps(e, reads, writes)
        ins = fn(self.eng[e])
        self.cnt[e] += 1
        ins.then_inc(self.prog[e], 1)
        tok = (self.prog[e], self.cnt[e], e, "prog_" + e)
        self._mark(tok, reads, writes)
        return tok

    def dma(self, q, out, in_, reads, writes, key, **kw):
        self.dma_sem(key)
        self._deps(q, reads, writes)
        ent = self.dsem[key]
        ent[1] += 16
        self.eng[q].dma_start(out=out, in_=in_, **kw).then_inc(ent[0], 16)
        tok = (ent[0], ent[1], None, "d_" + key)
        self._mark(tok, reads, writes)
        return tok

    def wait_tok(self, e, tok):
        self._wait(e, tok)


BRANCHES = "ABCM"
NEG = -1.0e9
MBNEG = -30000.0
NBIS = 18
DVE_FRAC = 1.0
PIPELINE = True


def _const_layout():
    lay = {}
    off = 0

    def add(name, w):
        nonlocal off
        lay[name] = (off, w)
        off += w

    add("ident", 128)
    add("Ublk", 128)
    add("M1", 128)
    add("SL", 128)
    add("triu", 64)
    add("CB", 128)
    add("pow2", 32)
    add("invf", 8)
    add("memn", 8)
    for l in range(2):
        add(f"nmix{l}", 8)
        add(f"nffn{l}", 8)
        add(f"convw{l}", 12)
        add(f"hgn{l}", 1)
        add(f"mqn{l}", 1)
        add(f"mkn{l}", 1)
        add(f"dqn{l}", 64)
        add(f"dkn{l}", 64)
    return lay, off


def build(S, n_layers=2, dbg=False):
    nc = bass.Bass("TRN2", target_bir_lowering=False)
    NT = S // 512
    NQ = S // 128
    lay, NCONST = _const_layout()
    BR = BRANCHES

    def din(name, shape, dt=F32):
        return nc.dram_tensor(name, list(shape), dt, kind="ExternalInput").ap()

    x_d = din("x", [S, D])
    mem_d = din("mem", [MEMT, D])
    pos_d = din("pos", [128, NQ], I32)
    consts_d = din("consts", [128, NCONST])
    cbig_d = din("cbig", [128, 1536])
    w_in_d = din("w_in", [2, D, IN_COLS])
    w_kv_d = din("mem_w_kv", [2, D, D])
    w_lift_d = din("w_lift", [2, 4 * 512, D])
    w_out_d = din("w_out", [2, D, D])
    w_up_d = din("ffn_w_up", [2, D, 2 * FFN])
    w_dn_d = din("ffn_w_down", [2, FFN, D])
    y_d = nc.dram_tensor("y", [S, D], F32, kind="ExternalOutput").ap()

    def dscr(name, shape, dt):
        return nc.dram_tensor(name, list(shape), dt, kind="Internal").ap()

    wb_in = dscr("wb_in", [2, D, IN_COLS], BF16)
    wb_kv = dscr("wb_kv", [2, D, D], BF16)
    wb_lift = dscr("wb_lift", [2, 2048, D], BF16)
    wb_out = dscr("wb_out", [2, D, D], BF16)
    wb_up = dscr("wb_up", [2, D, 2 * FFN], BF16)
    wb_dn = dscr("wb_dn", [2, FFN, D], BF16)
    x1_d = dscr("x1", [S, D], F32)

    stack = ExitStack()
    with stack:
        sc = Sched(nc, stack)

        sbtot = [0]

        def sb(name, shape, dt=F32):
            n_ = 1
            for d_ in shape[1:]:
                n_ *= d_
            sbtot[0] += n_ * (2 if dt == BF16 else 4)
            return nc.alloc_sbuf_tensor("sb_" + name, list(shape), dt)

        def T(name, shape, dt=F32):
            t_ = sb(name, shape, dt)
            return t_[tuple(slice(None) for _ in shape)], Res(name)

        consts, r_consts = T("consts", [128, NCONST])
        sc.dma("sp", consts[:, :], consts_d[:, :], [], [r_consts], "const")

        def cc(name, j=0, w=None):
            o, ww = lay[name]
            if w is None:
                w = ww - j
            return consts[:, o + j:o + j + w]

        ident_bf, r_identbf = T("ident_bf", [128, 128], BF16)
        sc.op("dve", lambda e: e.tensor_copy(out=ident_bf[:, :], in_=cc("ident")), [r_consts], [r_identbf])
        ones_bf, r_ones = T("ones_bf", [128, 128], BF16)
        sc.op("pool", lambda e: e.memset(ones_bf[:, :], 1.0), [], [r_ones])
        E4_bf, r_E4 = T("E4_bf", [128, 512], BF16)
        epsc, r_epsc = T("epsc", [128, 1])
        sc.op("pool", lambda e: e.memset(epsc[:, :], EPS), [], [r_epsc])
        negpi, r_negpi = T("negpi", [128, 1])
        sc.op("pool", lambda e: e.memset(negpi[:, :], -float(np.pi)), [], [r_negpi])

        def rsqrt(out, r_out, in_, r_in, scale):
            np_ = out.shape[0]
            sc.op("act", lambda e: e.activation(out=out, in_=in_, func=AF.Sqrt, bias=epsc[0:np_, :],
                                                scale=scale), [r_in, r_epsc], [r_out])
            sc.op("dve", lambda e: e.reciprocal(out=out, in_=out), [r_out], [r_out])

        CW = 2048
        cast_toks = []
        with nc.sbuf_tensor("stg_all", [128, 4, CW], F32) as stg_all, \
                nc.sbuf_tensor("stgb_all", [128, 4, CW], BF16) as stgb_all:
            r_stg = [Res(f"stg{i}") for i in range(4)]
            r_stgb = [Res(f"stgb{i}") for i in range(4)]
            ci = [0]
            cast_eng = ["dve", "act", "dve", "act"]

            def cast_matrix(src, dst, rows, cols):
                for r0 in range(0, rows, 128):
                    for c0 in range(0, cols, CW):
                        w = min(CW, cols - c0)
                        k = ci[0] % 4
                        sc.dma("sp", stg_all[:, k, 0:w], src[r0:r0 + 128, c0:c0 + w], [], [r_stg[k]], f"stg{k}")
                        e = cast_eng[k]
                        if e == "act":
                            sc.op("act", lambda en: en.copy(out=stgb_all[:, k, 0:w], in_=stg_all[:, k, 0:w]),
                                  [r_stg[k]], [r_stgb[k]])
                        else:
                            sc.op(e, lambda en: en.tensor_copy(out=stgb_all[:, k, 0:w], in_=stg_all[:, k, 0:w]),
                                  [r_stg[k]], [r_stgb[k]])
                        t = sc.dma("pool", dst[r0:r0 + 128, c0:c0 + w], stgb_all[:, k, 0:w], [r_stgb[k]], [],
                                   f"stgb{k}")
                        cast_toks.append(t)
                        ci[0] += 1

            for l in range(0 if NOCAST else n_layers):
                cast_matrix(w_in_d[l], wb_in[l], D, IN_COLS)
                cast_matrix(w_kv_d[l], wb_kv[l], D, D)
                cast_matrix(w_lift_d[l], wb_lift[l], 2048, D)
                cast_matrix(w_out_d[l], wb_out[l], D, D)
                cast_matrix(w_up_d[l], wb_up[l], D, 2 * FFN)
                cast_matrix(w_dn_d[l], wb_dn[l], FFN, D)
            last = {}
            for t in cast_toks:
                last[t[3]] = t
            for t in last.values():
                for e in ("sp", "act", "dve", "pool", "pe"):
                    sc.wait_tok(e, t)

        NPS = 8
        ps = [nc.alloc_psum_tensor(f"ps{i}", [128, 512], F32) for i in range(NPS)]
        r_ps = [Res(f"ps{i}", excl=True) for i in range(NPS)]
        psi = [0]
        pti = [0]

        pinned = set()

        def next_ps(pin=False):
            while True:
                i = psi[0] % NPS
                psi[0] += 1
                if i not in pinned:
                    break
            if pin:
                pinned.add(i)
            return ps[i], r_ps[i]

        def unpin(p):
            pinned.discard(ps.index(p))

        def next_pst():
            p_, rp_ = next_ps()
            return p_[:, :].bitcast(BF16)[:, 0:512], rp_

        NW = 2
        wslot = [sb(f"wslot{i}", [128, 8, 512], BF16) for i in range(NW)]
        r_wslot = [Res(f"wslot{i}") for i in range(NW)]
        wi = [0]

        def load_w(src, c0, ncols, nk, krow0=0):
            i = wi[0] % NW
            wi[0] += 1
            v = src[krow0:krow0 + nk * 128, c0:c0 + ncols].rearrange("(kc p) c -> p kc c", p=128)
            sc.dma("sp", wslot[i][:, 0:nk, 0:ncols], v, [], [r_wslot[i]], f"wslot{i}")
            return wslot[i], r_wslot[i]

        xt, r_xt = T("xt", [128, 4, D])
        hT, r_hT = T("hT", [128, 8, 512], BF16)
        ssq, r_ssq = T("ssq", [128, 4])
        rstd, r_rstd = T("rstd", [128, 4])
        aT, r_aT = T("aT", [128, 24, 512], BF16)
        yA, yB, yC, yM = aT[:, 0:4, :], aT[:, 4:8, :], aT[:, 8:12, :], aT[:, 12:16, :]
        r_yA, r_yB, r_yC, r_yM = Res("yA"), Res("yB"), Res("yC"), Res("yM")
        mergedT = aT[:, 16:24, :]
        r_merged = Res("merged")
        xn = aT[:, 16:24, :].rearrange("p (s a) c -> p s (a c)", a=2)
        r_xn = r_merged
        uhalo, r_uhalo = T("uhalo", [128, 4, 2])
        G = [T(f"g{i}", [128, 514]) for i in range(6)]

        def g(i):
            return G[i][0][:, 0:512], G[i][1]

        memT = aT[:, 0:8, 0:256]
        r_memT = Res("memT")
        for i_, c0_ in ((0, 0), (1, 512), (2, 1024)):
            sc.dma("sp", G[i_][0][:, 0:512], cbig_d[:, c0_:c0_ + 512], [], [G[i_][1]], f"cbig{i_}")
        sc.op("dve", lambda e: e.tensor_copy(out=E4_bf[:, :], in_=G[0][0][:, 0:512]), [G[0][1]], [r_E4])

        def rmsnorm_to_F(src, r_src, nsub, gname, dst, r_dst):
            junk, r_junk = g(3)
            for s in range(nsub):
                for hf in range(2):
                    sc.op("act", lambda e: e.activation(out=junk, in_=src[:, s, hf * 512:(hf + 1) * 512],
                                                        func=AF.Square, accum_out=ssq2[:, 2 * s + hf:2 * s + hf + 1]),
                          [r_src], [r_junk, r_ssq2])
            sc.op("dve", lambda e: e.tensor_reduce(out=ssq[:, 0:nsub],
                                                   in_=ssq2[:, 0:2 * nsub].rearrange("p (s two) -> p s two", two=2),
                                                   axis=AX.X, op=ALU.add), [r_ssq2], [r_ssq])
            rsqrt(rstd[:, 0:nsub], r_rstd, ssq[:, 0:nsub], r_ssq, 1.0 / D)
            for s in range(nsub):
                if s % 2 == 0:
                    sc.op("dve", lambda e: e.tensor_scalar(out=xn[:, s, :], in0=src[:, s, :],
                                                           scalar1=rstd[:, s:s + 1], scalar2=None, op0=ALU.mult),
                          [r_src, r_rstd], [r_xn])
                else:
                    sc.op("act", lambda e: e.activation(out=xn[:, s, :], in_=src[:, s, :], func=AF.Copy,
                                                        scale=rstd[:, s:s + 1]), [r_src, r_rstd], [r_xn])
            for kc in range(8):
                pt, rpt = next_pst()
                for s in range(nsub):
                    sc.op("pe", lambda e: e.transpose(out=pt[:, s * 128:(s + 1) * 128],
                                                      in_=xn[:, s, kc * 128:(kc + 1) * 128], identity=ident_bf[:, :]),
                          [r_xn, r_identbf], [rpt])
                sc.op("act", lambda e: e.activation(out=dst[:, kc, 0:nsub * 128], in_=pt[:, 0:nsub * 128],
                                                    func=AF.Copy, scale=cc(gname, kc, 1)),
                      [rpt, r_consts], r_dst if isinstance(r_dst, list) else [r_dst])

        ssq2, r_ssq2 = T("ssq2", [128, 8])

        def proj_F(wsrc, c0, n128, consume, rhs, r_rhs, nk=8, krow0=0, ntok=512):
            for g0 in range(0, n128, 4):
                gn = min(4, n128 - g0)
                wt, rw = load_w(wsrc, c0 + g0 * 128, gn * 128, nk, krow0)
                for j in range(gn):
                    p, rp = next_ps()
                    for kc in range(nk):
                        sc.op("pe", lambda e: e.matmul(p[:, 0:ntok], lhsT=wt[:, kc, j * 128:(j + 1) * 128],
                                                       rhs=rhs[:, kc, 0:ntok], start=(kc == 0), stop=(kc == nk - 1)),
                              [rw, r_rhs], [rp])
                    consume(g0 + j, p, rp)

        def proj_T(wsrc, c0, ncols, consume, subs=(0, 1, 2, 3)):
            wt, rw = load_w(wsrc, c0, ncols, 8)
            for s in subs:
                p, rp = next_ps()
                for kc in range(8):
                    sc.op("pe", lambda e: e.matmul(p[:, 0:ncols], lhsT=hT[:, kc, s * 128:(s + 1) * 128],
                                                   rhs=wt[:, kc, 0:ncols], start=(kc == 0), stop=(kc == 7)),
                          [rw, r_hT], [rp])
                consume(s, p, rp)

        ident_f = cc("ident")
        if "B" in BR:
            posi, r_posi = T("posi", [128, NQ], I32)
            sc.dma("sp", posi[:, :], pos_d[:, :], [], [r_posi], "posi")
            posf, r_posf = T("posf", [128, NQ])
            sc.op("dve", lambda e: e.tensor_copy(out=posf[:, :], in_=posi[:, :]), [r_posi], [r_posf])
            cosT, r_cos = T("cosT", [128, NQ, 8])
            sinT, r_sin = T("sinT", [128, NQ, 8])
            ang, r_ang = G[3][0][:, 0:NQ * 8].rearrange("p (n j) -> p n j", j=8), G[3][1]
            sc.op("dve", lambda e: e.tensor_tensor(out=ang[:, :, :],
                                                   in0=posf[:, :].unsqueeze(2).to_broadcast([128, NQ, 8]),
                                                   in1=cc("invf").unsqueeze(1).to_broadcast([128, NQ, 8]),
                                                   op=ALU.mult), [r_posf, r_consts], [r_ang])
            TWO_PI = float(2 * np.pi)
            MAGIC = 12582912.0
            nrd, r_nrd = G[4][0][:, 0:NQ * 8].rearrange("p (n j) -> p n j", j=8), G[4][1]
            for (dst, rdst, shift) in ((sinT, r_sin, 0.0), (cosT, r_cos, 0.25)):
                sc.op("dve", lambda e: e.tensor_scalar(out=dst[:, :, :], in0=ang[:, :, :], scalar1=1.0 / TWO_PI,
                                                       scalar2=shift, op0=ALU.mult, op1=ALU.add), [r_ang], [rdst])
                sc.op("dve", lambda e: e.tensor_scalar(out=nrd[:, :, :], in0=dst[:, :, :], scalar1=MAGIC,
                                                       scalar2=None, op0=ALU.add), [rdst], [r_nrd])
                sc.op("dve", lambda e: e.tensor_scalar(out=nrd[:, :, :], in0=nrd[:, :, :], scalar1=MAGIC,
                                                       scalar2=None, op0=ALU.subtract), [r_nrd], [r_nrd])
                sc.op("dve", lambda e: e.tensor_tensor(out=dst[:, :, :], in0=dst[:, :, :], in1=nrd[:, :, :],
                                                       op=ALU.subtract), [rdst, r_nrd], [rdst])
                sc.op("dve", lambda e: e.tensor_scalar(out=dst[:, :, :], in0=dst[:, :, :], scalar1=-0.49999,
                                                       scalar2=0.49999, op0=ALU.max, op1=ALU.min), [rdst], [rdst])
                sc.op("act", lambda e: e.activation(out=dst[:, :, :], in_=dst[:, :, :], func=AF.Sin,
                                                    scale=TWO_PI), [rdst], [rdst])
            kikT, r_kik = T("kikT", [128, S], BF16)
            r_kikc = [Res(f"kik{c}") for c in range(NQ)]
            Vall, r_V = T("Vall", [128, NQ, 65], BF16)
            r_Vc = [Res(f"V{c}") for c in range(NQ)]
            sc.op("pool", lambda e: e.memset(Vall[:, :, 64:65], 1.0), [], [r_V])
            score, r_score = T("score", [128, S])
            qiq = [T(f"qiq{i}", [128, 1, 8, 128], BF16) for i in range(3)]
            iqTb = [q_[0] for q_ in qiq]
            r_iqTb = [Res(f"iqTb{i}") for i in range(3)]
            wq, r_wq = T("wq", [128, 4, 8])
            wabs, r_wabs = T("wabs", [128, 4, 8])
            wsgn, r_wsgn = T("wsgn", [128, 4, 8])
            rbb = [T(f"rbb{i}", [128, 512], BF16) for i in range(3)]
            Dm = [T(f"Dm{i}", [128, 8, 128], BF16) for i in range(2)]
            stg_iq, r_stgiq = T("stg_iq", [128, 8, 128])
            sc.op("pool", lambda e: e.memset(stg_iq[:, :, :], 0.0), [], [r_stgiq])
            PTb = [T(f"PT{i}", [128, 512], BF16) for i in range(3)]
            rt = [T(f"rt{i}", [128, 8, 8]) for i in range(4)]
            thr, r_thr = T("thr", [128, 1])
            thr0, r_thr0 = T("thr0", [128, 1])
            sc.op("pool", lambda e: e.memset(thr0[:, :], -1.0e8), [], [r_thr0])

        if "C" in BR:
            lb1, r_lb1 = T("lb1", [128, 512])
            sc.op("dve", lambda e: e.tensor_tensor(out=lb1[:, :], in0=G[2][0][:, 0:512], in1=G[1][0][:, 0:512],
                                                   op=ALU.subtract), [G[1][1], G[2][1]], [r_lb1])
            sc.op("act", lambda e: e.activation(out=lb1[:, :], in_=lb1[:, :], func=AF.Sigmoid), [r_lb1], [r_lb1])
            Sst, r_S = T("Sst", [128, 4, 128])
            sgg, r_sgg = aT[:, 16:20, :], r_merged
            ex, r_ex = T("ex", [128, 512])
            prod, r_prod = T("prod", [128, 512])
            Am, r_Am = T("Am", [128, 4, 64])
            vT_, r_vT = T("vTl", [128, 512])
            qTl, r_qTl = T("qTl", [128, 512])
            kTl, r_kTl = T("kTl", [128, 512])
            lfT, r_lfT = T("lfT", [128, 512])
            dl, r_dl = T("dl", [128, 4, 2])
            OSQ = None
            QD = [g(1 + h) for h in range(4)]


        if "B" in BR:
            RB = [g(3), g(4)]
            cjunk, r_cjunk = T("cjunk", [128, 1024], BF16)
            if DVE_FRAC < 1.0:
                ajunk, r_ajunk = T("ajunk", [128, 1024], BF16)
                asum, r_asum = T("asum", [128, 4])
            mid, r_mid = T("mid", [128, 1])
            MBf, r_MB = T("MBf", [128, S], BF16)
            qTb = [q_[0] for q_ in qiq]
            r_qTb = [q_[1] for q_ in qiq]
            ob, r_ob = g(2)
            stg_k4, r_stgk4 = T("stg_k4", [128, 4, 128])
            sm, r_sm = T("sm", [128, 8])
            wk, r_wk = T("wk", [128, 32])
            sA, r_sA = T("sA", [128, 16])
            sB, r_sB = T("sB", [128, 16])
            sC, r_sC = T("sC", [128, 4])
            rec8, r_rec8 = T("rec8", [128, 8])

            def rope(v3, nh, tq, r_v):
                cb = cosT[:, tq, :].unsqueeze(1).to_broadcast([128, nh, 8])
                sb_ = sinT[:, tq, :].unsqueeze(1).to_broadcast([128, nh, 8])
                x1 = v3[:, :, 0:8]
                x2 = v3[:, :, 8:16]
                tt = [rt[i][0][:, 0:nh, :] for i in range(4)]
                rr = [rt[i][1] for i in range(4)]
                sc.op("pool", lambda e: e.tensor_tensor(out=tt[0], in0=x1, in1=cb, op=ALU.mult), [r_v, r_cos], [rr[0]])
                sc.op("pool", lambda e: e.tensor_tensor(out=tt[1], in0=x2, in1=sb_, op=ALU.mult), [r_v, r_sin], [rr[1]])
                sc.op("pool", lambda e: e.tensor_tensor(out=tt[2], in0=x1, in1=sb_, op=ALU.mult), [r_v, r_sin], [rr[2]])
                sc.op("pool", lambda e: e.tensor_tensor(out=tt[3], in0=x2, in1=cb, op=ALU.mult), [r_v, r_cos], [rr[3]])
                sc.op("pool", lambda e: e.tensor_tensor(out=x1, in0=tt[0], in1=tt[1], op=ALU.subtract),
                      [rr[0], rr[1]], [r_v])
                sc.op("pool", lambda e: e.tensor_tensor(out=x2, in0=tt[2], in1=tt[3], op=ALU.add),
                      [rr[2], rr[3]], [r_v])

            def dsa_tile(l, t):
                def cons_q(s, p, rp):
                    tq = 4 * t + s
                    sq, r_sq = g(0)
                    sc.op("act", lambda e: e.activation(out=sq, in_=p[:, :], func=AF.Square), [rp], [r_sq])
                    sc.op("dve", lambda e: e.tensor_reduce(out=sA[:, 0:8], in_=sq.rearrange("p (h d) -> p h d", h=8),
                                                           axis=AX.X, op=ALU.add), [r_sq], [r_sA])
                    rsqrt(sA[:, 8:16], r_sB, sA[:, 0:8], r_sA, 1.0 / 64)
                    qn, r_qn = g(1)
                    qn3 = qn.rearrange("p (h d) -> p h d", h=8)
                    sc.op("dve", lambda e: e.tensor_tensor(out=qn3, in0=p[:, :].rearrange("p (h d) -> p h d", h=8),
                                                           in1=sA[:, 8:16].unsqueeze(2).to_broadcast([128, 8, 64]),
                                                           op=ALU.mult), [rp, r_sB], [r_qn])
                    sc.op("pool", lambda e: e.tensor_tensor(out=qn3, in0=qn3,
                                                            in1=cc(f"dqn{l}").unsqueeze(1).to_broadcast([128, 8, 64]),
                                                            op=ALU.mult), [r_qn, r_consts], [r_qn])
                    rope(qn3, 8, tq, r_qn)

                def q_b(s):
                    qn, r_qn = g(1)
                    for half in range(2):
                        pt, rpt = next_ps()
                        for hh in range(4):
                            h = half * 4 + hh
                            sc.op("pe", lambda e: e.transpose(out=pt[0:64, hh * 128:(hh + 1) * 128],
                                                              in_=qn[:, h * 64:(h + 1) * 64], identity=ident_f),
                                  [r_qn, r_consts], [rpt])
                        sc.op("act", lambda e: e.copy(out=qTb[s % 3][0:64, 0, half * 4:(half + 1) * 4, :],
                                                      in_=pt[0:64, :].rearrange("p (h t) -> p h t", h=4)),
                              [rpt], [r_qTb[s % 3]])

                def cons_iq(s, p, rp):
                    tq = 4 * t + s
                    sc.op("act", lambda e: e.copy(out=stg_iq[:, :, 64:128],
                                                  in_=p[:, :].rearrange("p (h d) -> p h d", h=8)), [rp], [r_stgiq])
                    sc.op("pool", lambda e: e.tensor_tensor(
                        out=stg_iq[:, :, 64:128], in0=stg_iq[:, :, 64:128],
                        in1=wabs[:, s, :].unsqueeze(2).to_broadcast([128, 8, 64]), op=ALU.mult),
                        [r_stgiq, r_wabs], [r_stgiq])
                    rope(stg_iq[:, :, 64:128], 8, tq, r_stgiq)

                def iq_b(s):
                    for half in range(2):
                        pt, rpt = next_ps()
                        for hh in range(4):
                            h = half * 4 + hh
                            sc.op("pe", lambda e: e.transpose(out=pt[:, hh * 128:(hh + 1) * 128],
                                                              in_=stg_iq[:, h, :], identity=ident_f),
                                  [r_stgiq, r_consts], [rpt])
                        sc.op("act", lambda e: e.copy(out=iqTb[s % 3][64:128, 0, half * 4:(half + 1) * 4, :],
                                                      in_=pt[64:128, :].rearrange("p (h t) -> p h t", h=4)),
                              [rpt], [r_iqTb[s % 3]])

                def gen_proj(s, phase):
                    if phase == 0:
                        proj_T(wb_in[l], C_DQ, 512, cons_q, subs=(s,))
                        proj_T(wb_in[l], C_IQ, 512, cons_iq, subs=(s,))
                        return
                    q_b(s)
                    iq_b(s)
                    dm_, r_dm_ = Dm[s % 2]
                    for h in range(8):
                        sc.op("pool", lambda e: e.tensor_scalar(out=dm_[:, h, :], in0=ident_bf[:, :],
                                                                scalar1=wsgn[:, s, h:h + 1], scalar2=None,
                                                                op0=ALU.mult), [r_identbf, r_wsgn], [r_dm_])
                    return
                    yield

                def cons_kv(s, p, rp):
                    tq = 4 * t + s
                    jk, r_jk = g(2)
                    sc.op("act", lambda e: e.activation(out=jk[:, 0:64], in_=p[:, 0:64], func=AF.Square,
                                                        accum_out=sC[:, 0:1]), [rp], [r_jk, r_sC])
                    rsqrt(sC[:, 1:2], r_sC, sC[:, 0:1], r_sC, 1.0 / 64)
                    sc.op("dve", lambda e: e.tensor_scalar(out=stg_k4[:, s, 0:64], in0=p[:, 0:64], scalar1=sC[:, 1:2],
                                                           scalar2=None, op0=ALU.mult), [rp, r_sC], [r_stgk4])
                    sc.op("pool", lambda e: e.tensor_tensor(out=stg_k4[:, s, 0:64], in0=stg_k4[:, s, 0:64],
                                                            in1=cc(f"dkn{l}"), op=ALU.mult),
                          [r_stgk4, r_consts], [r_stgk4])
                    rope(stg_k4[:, s:s + 1, 0:64], 1, tq, r_stgk4)
                    sc.op("act", lambda e: e.copy(out=Vall[:, tq, 0:64], in_=p[:, 64:128]), [rp], [r_Vc[tq], r_V])

                def cons_ik(s, p, rp):
                    tq = 4 * t + s
                    sc.op("act", lambda e: e.copy(out=stg_k4[:, s, 64:128], in_=p[:, 0:64]), [rp], [r_stgk4])
                    rope(stg_k4[:, s:s + 1, 64:128], 1, tq, r_stgk4)
                    sc.op("dve", lambda e: e.tensor_scalar(out=wq[:, s, :], in0=p[:, 64:72],
                                                           scalar1=float(8 ** -0.5 * 64 ** -0.5), scalar2=None,
                                                           op0=ALU.mult), [rp], [r_wq])
                    sc.op("act", lambda e: e.activation(out=wsgn[:, s, :], in_=wq[:, s, :], func=AF.Sign),
                          [r_wq], [r_wsgn])
                    sc.op("dve", lambda e: e.tensor_tensor(out=wabs[:, s, :], in0=wq[:, s, :], in1=wsgn[:, s, :],
                                                           op=ALU.mult), [r_wq, r_wsgn], [r_wabs])
                    pt, rpt = next_ps()
                    sc.op("pe", lambda e: e.transpose(out=pt[:, 0:128], in_=stg_k4[:, s, :], identity=ident_f),
                          [r_stgk4, r_consts], [rpt])
                    sc.op("act", lambda e: e.copy(out=kikT[:, tq * 128:(tq + 1) * 128], in_=pt[:, 0:128]),
                          [rpt], [r_kikc[tq]])

                proj_T(wb_in[l], C_DK, 128, cons_kv)
                proj_T(wb_in[l], C_IK, 72, cons_ik)

                def gen_idx(s, phase):
                    j = 4 * t + s
                    L = 128 * (j + 1)
                    iqT_, r_iqT_ = iqTb[s % 3], r_iqTb[s % 3]
                    if phase == 0:
                        dm_, r_dm_ = Dm[s % 2]
                        for c in range(t + 1):
                            W = 512 if c < t else (s + 1) * 128
                            kres = [r_kikc[4 * c + i] for i in range(W // 128)]
                            psc, rpsc = next_ps(pin=True)

                            def emit_x(h):
                                px, rpx = next_ps()
                                sc.op("pe", lambda e: e.matmul(px[:, 0:W], lhsT=iqT_[64:128, 0, h, :],
                                                               rhs=kikT[64:128, c * 512:c * 512 + W], start=True,
                                                               stop=True), [r_iqT_] + kres, [rpx])
                                rb, r_rb = rbb[h % 3]
                                sc.op("act", lambda e: e.activation(out=rb[:, 0:W], in_=px[:, 0:W], func=AF.Relu),
                                      [rpx], [r_rb])

                            def emit_acc(h):
                                rb, r_rb = rbb[h % 3]
                                sc.op("pe", lambda e: e.matmul(psc[:, 0:W], lhsT=dm_[:, h, :], rhs=rb[:, 0:W],
                                                               start=(h == 0), stop=(h == 7)), [r_dm_, r_rb], [rpsc])

                            emit_x(0)
                            emit_x(1)
                            for h in range(8):
                                if h + 2 < 8:
                                    emit_x(h + 2)
                                emit_acc(h)
                            sc.op("dve", lambda e: e.tensor_copy(out=score[:, c * 512:c * 512 + W], in_=psc[:, 0:W]),
                                  [rpsc], [r_score])
                            unpin(psc)
                            yield
                        sc.op("dve", lambda e: e.tensor_tensor(out=score[:, j * 128:(j + 1) * 128],
                                                               in0=score[:, j * 128:(j + 1) * 128], in1=cc("CB"),
                                                               op=ALU.add), [r_score, r_consts], [r_score])
                        return
                    if j >= 2 and phase == 1:
                        sc.op("dve", lambda e: e.tensor_reduce(out=sm[:, 0:1], in_=score[:, 0:256], axis=AX.X,
                                                               op=ALU.min), [r_score], [r_sm])
                        sc.op("dve", lambda e: e.tensor_reduce(out=sm[:, 1:2], in_=score[:, 0:L], axis=AX.X,
                                                               op=ALU.max), [r_score], [r_sm])
                        sc.op("dve", lambda e: e.tensor_scalar(out=sm[:, 2:3], in0=sm[:, 1:2], scalar1=sm[:, 0:1],
                                                               scalar2=0.5, op0=ALU.subtract, op1=ALU.mult),
                              [r_sm], [r_sm])
                        sc.op("dve", lambda e: e.tensor_scalar(out=wk[:, :], in0=cc("pow2"), scalar1=sm[:, 2:3],
                                                               scalar2=None, op0=ALU.mult), [r_sm, r_consts], [r_wk])
                        sc.op("dve", lambda e: e.tensor_tensor(out=mid[:, :], in0=sm[:, 0:1], in1=sm[:, 2:3],
                                                               op=ALU.add), [r_sm], [r_mid])
                        Ld = L if DVE_FRAC >= 1.0 else min(L, max(256, int(round(L * DVE_FRAC / 256.0)) * 256))
                        nact = L - Ld
                        for k in range(NBIS):
                            if nact > 0:
                                for ci_, c0_ in enumerate(range(Ld, L, 1024)):
                                    w_ = min(1024, L - c0_)
                                    sc.op("act", lambda e: e.activation(
                                        out=ajunk[:, 0:w_], in_=score[:, c0_:c0_ + w_], func=AF.Sign,
                                        bias=mid[:, 0:1], scale=-1.0, accum_out=asum[:, ci_:ci_ + 1]),
                                        [r_score, r_mid], [r_ajunk, r_asum])
                                nac = ci_ + 1
                            ccol = 4
                            for ci_, c0_ in enumerate(range(0, Ld, 1024)):
                                w_ = min(1024, Ld - c0_)
                                prev = ccol
                                ccol = 6 + (ci_ % 2)
                                sc.op("dve", lambda e: e.tensor_scalar(
                                    out=cjunk[:, 0:w_], in0=score[:, c0_:c0_ + w_], scalar1=mid[:, 0:1],
                                    scalar2=(None if ci_ == 0 else sm[:, prev:prev + 1]), op0=ALU.is_ge,
                                    op1=ALU.add, accum_out=sm[:, ccol:ccol + 1]),
                                    [r_score, r_mid, r_sm], [r_cjunk, r_sm])
                            if nact > 0:
                                for a_ in range(nac):
                                    sc.op("dve", lambda e: e.scalar_tensor_tensor(
                                        out=sm[:, ccol:ccol + 1], in0=asum[:, a_:a_ + 1], scalar=-0.5,
                                        in1=sm[:, ccol:ccol + 1], op0=ALU.mult, op1=ALU.add),
                                        [r_asum, r_sm], [r_sm])
                            sc.op("dve", lambda e: e.tensor_scalar(out=sm[:, 5:6], in0=sm[:, ccol:ccol + 1],
                                                                   scalar1=256.0 - nact / 2.0,
                                                                   scalar2=0.5, op0=ALU.is_ge, op1=ALU.subtract),
                                  [r_sm], [r_sm])
                            sc.op("dve", lambda e: e.scalar_tensor_tensor(out=mid[:, :], in0=sm[:, 5:6],
                                                                          scalar=wk[:, k:k + 1], in1=mid[:, :],
                                                                          op0=ALU.mult, op1=ALU.add),
                                  [r_sm, r_wk, r_mid], [r_mid])
                            yield
                        sc.op("dve", lambda e: e.tensor_tensor(out=thr[:, :], in0=mid[:, :],
                                                               in1=wk[:, NBIS:NBIS + 1], op=ALU.subtract),
                              [r_mid, r_wk], [r_thr])
                        return
                    if phase == 1:
                        return
                    th, r_th = (thr, r_thr) if j >= 2 else (thr0, r_thr0)
                    for c0_ in range(0, L, 2048):
                        w_ = min(2048, L - c0_)
                        sc.op("dve", lambda e: e.tensor_scalar(out=MBf[:, c0_:c0_ + w_], in0=score[:, c0_:c0_ + w_],
                                                               scalar1=th[:, 0:1], scalar2=MBNEG, op0=ALU.is_lt,
                                                               op1=ALU.mult), [r_score, r_th], [r_MB])
                        yield

                def gen_att(s):
                    j = 4 * t + s
                    qb = qTb[s % 3]
                    r_qb = r_qTb[s % 3]
                    pacc = [next_ps(pin=True), next_ps(pin=True)]
                    units = [(kc, gi) for kc in range(j + 1) for gi in range(2)]
                    LA = 2

                    def emit_logits(i):
                        kc, gi = units[i]
                        pl, rpl = next_ps()
                        sc.op("pe", lambda e: e.matmul(pl[:, :], lhsT=kikT[0:64, kc * 128:(kc + 1) * 128],
                                                       rhs=qb[0:64, 0, 4 * gi:4 * gi + 4, :], start=True,
                                                       stop=False), [r_kikc[kc], r_qb], [rpl])
                        sc.op("pe", lambda e: e.matmul(pl[:, :], lhsT=MBf[:, kc * 128:(kc + 1) * 128],
                                                       rhs=E4_bf[:, :], start=False, stop=True),
                              [r_MB, r_E4], [rpl])
                        PT, r_PT = PTb[i % 3]
                        sc.op("act", lambda e: e.activation(out=PT, in_=pl[:, :], func=AF.Exp, scale=0.125),
                              [rpl], [r_PT])

                    def emit_pv(i):
                        kc, gi = units[i]
                        PT, r_PT = PTb[i % 3]
                        pa_, rpa_ = pacc[gi]
                        for hh in range(4):
                            sc.op("pe", lambda e: e.matmul(pa_[:, hh * 65:(hh + 1) * 65],
                                                           lhsT=PT[:, hh * 128:(hh + 1) * 128],
                                                           rhs=Vall[:, kc, :], start=(kc == 0 and hh == 0),
                                                           stop=(kc == j), skip_group_check=True),
                                  [r_PT, r_Vc[kc], r_V], [rpa_])

                    for i in range(min(LA, len(units))):
                        emit_logits(i)
                    for i in range(len(units)):
                        if i + LA < len(units):
                            emit_logits(i + LA)
                        emit_pv(i)
                        if i % 2 == 1:
                            yield
                    for gi in range(2):
                        pa_, rpa_ = pacc[gi]
                        a3 = pa_[:, 0:260].rearrange("p (h d) -> p h d", h=4)
                        sc.op("dve", lambda e: e.reciprocal(out=rec8[:, 4 * gi:4 * gi + 4], in_=a3[:, :, 64]),
                              [rpa_], [r_rec8])
                        sc.op("dve", lambda e: e.tensor_tensor(
                            out=ob[:, gi * 256:(gi + 1) * 256].rearrange("p (h d) -> p h d", h=4), in0=a3[:, :, 0:64],
                            in1=rec8[:, 4 * gi:4 * gi + 4].unsqueeze(2).to_broadcast([128, 4, 64]), op=ALU.mult),
                            [rpa_, r_rec8], [r_ob])
                        unpin(pa_)
                    pt, rpt = next_ps()
                    for k4 in range(4):
                        sc.op("pe", lambda e: e.transpose(out=pt[:, k4 * 128:(k4 + 1) * 128],
                                                          in_=ob[:, k4 * 128:(k4 + 1) * 128], identity=ident_f),
                              [r_ob, r_consts], [rpt])
                    sc.op("act", lambda e: e.copy(out=yB[:, :, s * 128:(s + 1) * 128],
                                                  in_=pt[:, :].rearrange("p (k t) -> p k t", k=4)), [rpt], [r_yB])
                    yield

                def drain(gen):
                    for _ in gen:
                        pass

                def interleave_n(items):
                    prog = [0] * len(items)
                    done = [False] * len(items)
                    while not all(done):
                        best = None
                        for i_, (g_, ex_) in enumerate(items):
                            if done[i_]:
                                continue
                            r_ = prog[i_] / float(ex_)
                            if best is None or r_ < best[0]:
                                best = (r_, i_)
                        i_ = best[1]
                        try:
                            next(items[i_][0])
                            prog[i_] += 1
                        except StopIteration:
                            done[i_] = True

                drain(gen_proj(0, 0))
                drain(gen_proj(0, 1))
                for s in range(4):
                    drain(gen_idx(s, 0))
                    if s < 3:
                        drain(gen_proj(s + 1, 0))
                    drain(gen_idx(s, 1))
                    if s > 0:
                        drain(gen_att(s - 1))
                    drain(gen_idx(s, 2))
                    if s < 3:
                        drain(gen_proj(s + 1, 1))
                drain(gen_att(3))
        for l in range(n_layers):
            src_d = x_d if l == 0 else x1_d
            dst_d = y_d if l == n_layers - 1 else x1_d

            if "M" in BR:
                sc.dma("sp", xt[:, 0:2, :], mem_d[:, :].rearrange("(s p) d -> p s d", p=128), [], [r_xt], "xt")
                rmsnorm_to_F(xt, r_xt, 2, "memn", memT, [r_memT, r_yA, r_yB])
                if l == 0:
                    mkT, r_mkT = T("mkT", [128, 4, 256], BF16)
                    mv, r_mv = T("mv", [128, 2, 512], BF16)
                    gkq, r_gkq = T("gkq", [128, 1])
                sc.op("dve", lambda e: e.tensor_tensor(out=gkq[:, :], in0=cc(f"mqn{l}"), in1=cc(f"mkn{l}"),
                                                       op=ALU.mult), [r_consts], [r_gkq])
                sc.op("dve", lambda e: e.tensor_scalar(out=gkq[:, :], in0=gkq[:, :], scalar1=128.0 ** -0.5,
                                                       scalar2=None, op0=ALU.mult), [r_gkq], [r_gkq])

                def cons_mk(j, p, rp):
                    sq, r_sq = g(0)
                    sc.op("act", lambda e: e.activation(out=sq[:, 0:256], in_=p[:, 0:256], func=AF.Square),
                          [rp], [r_sq])
                    sqb, r_sqb = MQB
                    sc.op("pool", lambda e: e.tensor_copy(out=sqb[:, 0:256], in_=sq[:, 0:256]), [r_sq], [r_sqb])
                    p2, rp2 = next_ps()
                    sc.op("pe", lambda e: e.matmul(p2[:, 0:256], lhsT=ones_bf[:, :], rhs=sqb[:, 0:256], start=True,
                                                   stop=True), [r_ones, r_sqb], [rp2])
                    rs, r_rs = g(1)
                    rsqrt(rs[:, 0:256], r_rs, p2[:, 0:256], rp2, 1.0 / 128)
                    sc.op("dve", lambda e: e.tensor_tensor(out=rs[:, 0:256], in0=rs[:, 0:256], in1=p[:, 0:256],
                                                           op=ALU.mult), [r_rs, rp], [r_rs])
                    sc.op("dve", lambda e: e.tensor_scalar(out=mkT[:, j, :], in0=rs[:, 0:256], scalar1=gkq[:, 0:1],
                                                           scalar2=None, op0=ALU.mult), [r_rs, r_gkq], [r_mkT])

                if l == 0:
                    MQB = PTb[2] if "B" in BR else T("mqb", [128, 512], BF16)
                proj_F(wb_kv[l], 0, 4, cons_mk, memT, r_memT, ntok=256)
                wt, rw = load_w(wb_kv[l], 512, 512, 8)
                for mc in range(2):
                    p, rp = next_ps()
                    for kc in range(8):
                        sc.op("pe", lambda e: e.matmul(p[:, :], lhsT=memT[:, kc, mc * 128:(mc + 1) * 128],
                                                       rhs=wt[:, kc, :], start=(kc == 0), stop=(kc == 7)),
                              [rw, r_memT], [rp])
                    sc.op("act", lambda e: e.copy(out=mv[:, mc, :], in_=p[:, :]), [rp], [r_mv])
                for r_ in (r_yA, r_yB):
                    r_.r.update(r_memT.r)

            if "C" in BR:
                sc.op("pool", lambda e: e.memset(Sst[:, :, :], 0.0), [], [r_S])
                if l == 0:
                    OSQ_ = rbb[0] if "B" in BR else T("osq", [128, 512], BF16)

            for t in range(NT):
                t0 = t * 512
                sc.dma("sp", xt[:, :, :], src_d[t0:t0 + 512, :].rearrange("(s p) d -> p s d", p=128),
                       [], [r_xt], "xt")
                rmsnorm_to_F(xt, r_xt, 4, f"nmix{l}", hT, r_hT)
                branches = []

                if "A" in BR:
                    U = [g(i) for i in range(4)]
                    UT = [G[i][0] for i in range(4)]
                    for c in range(4):
                        if t == 0:
                            sc.op("pool", lambda e: e.memset(UT[c][:, 0:2], 0.0), [], [G[c][1]])
                        else:
                            sc.op("pool", lambda e: e.tensor_copy(out=UT[c][:, 0:2], in_=uhalo[:, c, :]),
                                  [r_uhalo], [G[c][1]])

                    def cons_ax(j, p, rp):
                        sc.op("act", lambda e: e.copy(out=UT[j][:, 2:514], in_=p[:, :]), [rp], [G[j][1]])

                    proj_F(wb_in[l], C_AX, 4, cons_ax, hT, r_hT)

                    def cons_ac(j, p, rp):
                        sc.op("dve", lambda e: e.tensor_tensor(out=UT[j][:, 2:514], in0=UT[j][:, 2:514], in1=p[:, :],
                                                               op=ALU.mult), [rp, G[j][1]], [G[j][1]])
                        sc.op("pool", lambda e: e.tensor_copy(out=uhalo[:, j, :], in_=UT[j][:, 512:514]),
                              [G[j][1]], [r_uhalo])

                    proj_F(wb_in[l], C_AC, 4, cons_ac, hT, r_hT)

                    def cons_ab(j, p, rp):
                        cw = lay[f"convw{l}"][0]
                        ctmp, r_ctmp = g(4 + (j % 2))
                        sc.op("pool", lambda e: e.tensor_scalar(out=ctmp, in0=UT[j][:, 0:512],
                                                                scalar1=consts[:, cw + j:cw + j + 1], scalar2=None,
                                                                op0=ALU.mult), [G[j][1], r_consts], [r_ctmp])
                        sc.op("dve", lambda e: e.scalar_tensor_tensor(out=ctmp, in0=UT[j][:, 1:513],
                                                                      scalar=consts[:, cw + 4 + j:cw + 5 + j],
                                                                      in1=ctmp, op0=ALU.mult, op1=ALU.add),
                              [G[j][1], r_consts, r_ctmp], [r_ctmp])
                        sc.op("dve", lambda e: e.scalar_tensor_tensor(out=ctmp, in0=UT[j][:, 2:514],
                                                                      scalar=consts[:, cw + 8 + j:cw + 9 + j],
                                                                      in1=ctmp, op0=ALU.mult, op1=ALU.add),
                              [G[j][1], r_consts, r_ctmp], [r_ctmp])
                        sc.op("dve", lambda e: e.tensor_tensor(out=yA[:, j, :], in0=ctmp, in1=p[:, :], op=ALU.mult),
                              [rp, r_ctmp], [r_yA])

                    proj_F(wb_in[l], C_AB, 4, cons_ab, hT, r_hT)
                    branches.append((0, yA, r_yA))

                if "M" in BR:
                    def cons_mq(j, p, rp):
                        sq, r_sq = MQB
                        sc.op("act", lambda e: e.activation(out=sq, in_=p[:, :], func=AF.Square), [rp], [r_sq])
                        p2, rp2 = next_ps()
                        sc.op("pe", lambda e: e.matmul(p2[:, :], lhsT=ones_bf[:, :], rhs=sq, start=True, stop=True),
                              [r_ones, r_sq], [rp2])
                        rs, r_rs = g(4)
                        rsqrt(rs, r_rs, p2[:, :], rp2, 1.0 / 128)
                        mqn, r_mqn = MQN
                        sc.op("dve", lambda e: e.tensor_tensor(out=mqn, in0=rs, in1=p[:, :], op=ALU.mult),
                              [r_rs, rp], [r_mqn])
                        pts = []
                        for mc in range(2):
                            pl, rpl = next_ps()
                            sc.op("pe", lambda e: e.matmul(pl[:, :], lhsT=mkT[:, j, mc * 128:(mc + 1) * 128], rhs=mqn,
                                                           start=True, stop=True), [r_mkT, r_mqn], [rpl])
                            PT, r_PT = MPT[mc]
                            sc.op("act", lambda e: e.activation(out=PT, in_=pl[:, :], func=AF.Exp), [rpl], [r_PT])
                            pts.append((PT, r_PT))
                        py, rpy = next_ps()
                        psm, rpsm = next_ps()
                        for mc in range(2):
                            PT, r_PT = pts[mc]
                            sc.op("pe", lambda e: e.matmul(py[:, :], lhsT=mv[:, mc, j * 128:(j + 1) * 128], rhs=PT,
                                                           start=(mc == 0), stop=(mc == 1)), [r_mv, r_PT], [rpy])
                        for mc in range(2):
                            PT, r_PT = pts[mc]
                            sc.op("pe", lambda e: e.matmul(psm[:, :], lhsT=ones_bf[:, :], rhs=PT,
                                                           start=(mc == 0), stop=(mc == 1)), [r_ones, r_PT], [rpsm])
                        rc, r_rc = g(5)
                        sc.op("dve", lambda e: e.reciprocal(out=rc, in_=psm[:, :]), [rpsm], [r_rc])
                        sc.op("dve", lambda e: e.tensor_tensor(out=yM[:, j, :], in0=rc, in1=py[:, :], op=ALU.mult),
                              [r_rc, rpy], [r_yM])

                    if l == 0 and t == 0:
                        if "B" in BR:
                            MQN = (cjunk[:, 0:512], r_cjunk)
                            MPT = [PTb[0], PTb[1]]
                        else:
                            MQN = T("mqn", [128, 512], BF16)
                            MPT = [T(f"mpt{i}", [128, 512], BF16) for i in range(2)]
                    proj_F(wb_in[l], C_MQ, 4, cons_mq, hT, r_hT)
                    branches.append((3, yM, r_yM))

                if "C" in BR:
                    def cons_gg(j, p, rp):
                        sc.op("act", lambda e: e.activation(out=sgg[:, j, :], in_=p[:, :], func=AF.Silu), [rp], [r_sgg])

                    proj_F(wb_in[l], C_GG, 4, cons_gg, hT, r_hT)
                    for s in range(4):
                        tk = slice(s * 128, (s + 1) * 128)

                        def tproj(c0):
                            wt, rw = load_w(wb_in[l], c0, 512, 8)
                            p, rp = next_ps()
                            for kc in range(8):
                                sc.op("pe", lambda e: e.matmul(p[:, :], lhsT=hT[:, kc, tk], rhs=wt[:, kc, :],
                                                               start=(kc == 0), stop=(kc == 7)), [rw, r_hT], [rp])
                            return p, rp

                        p, rp = tproj(C_GQ)
                        sc.op("act", lambda e: e.activation(out=qTl[:, :], in_=p[:, :], func=AF.Silu), [rp], [r_qTl])
                        p, rp = tproj(C_GI)
                        sc.op("act", lambda e: e.copy(out=vT_[:, :], in_=p[:, :]), [rp], [r_vT])
                        p, rp = tproj(C_GF)
                        sc.op("act", lambda e: e.activation(out=lfT[:, :], in_=p[:, :], func=AF.Sigmoid),
                              [rp], [r_lfT])
                        if l == 1:
                            sc.op("dve", lambda e: e.tensor_tensor(out=kTl[:, :], in0=lfT[:, :], in1=lb1[:, :],
                                                                   op=ALU.mult), [r_lfT, r_lb1], [r_kTl])
                            sc.op("dve", lambda e: e.tensor_tensor(out=lfT[:, :], in0=lfT[:, :], in1=kTl[:, :],
                                                                   op=ALU.subtract), [r_lfT, r_kTl], [r_lfT])
                            sc.op("dve", lambda e: e.tensor_tensor(out=lfT[:, :], in0=lfT[:, :], in1=lb1[:, :],
                                                                   op=ALU.add), [r_lfT, r_lb1], [r_lfT])
                        sc.op("dve", lambda e: e.tensor_scalar(out=kTl[:, :], in0=lfT[:, :], scalar1=-1.0, scalar2=1.0,
                                                               op0=ALU.mult, op1=ALU.add), [r_lfT], [r_kTl])
                        sc.op("act", lambda e: e.activation(out=lfT[:, :], in_=lfT[:, :], func=AF.Ln), [r_lfT], [r_lfT])
                        for h in range(4):
                            hc = slice(h * 128, (h + 1) * 128)
                            pc, rpc = next_ps()
                            sc.op("pe", lambda e: e.matmul(pc[:, 0:128], lhsT=lfT[:, hc], rhs=cc("M1"), start=True,
                                                           stop=True), [r_lfT, r_consts], [rpc])
                            sc.op("pe", lambda e: e.matmul(pc[:, 128:256], lhsT=lfT[:, hc], rhs=cc("Ublk"), start=True,
                                                           stop=True), [r_lfT, r_consts], [rpc])
                            sc.op("pe", lambda e: e.matmul(pc[:, 256:384], lhsT=cc("SL"), rhs=lfT[:, hc], start=True,
                                                           stop=True), [r_lfT, r_consts], [rpc])
                            sc.op("act", lambda e: e.activation(out=ex[:, 0:128], in_=pc[:, 0:128], func=AF.Exp),
                                  [rpc], [r_ex])
                            sc.op("act", lambda e: e.activation(out=ex[:, 128:256], in_=pc[:, 0:128], func=AF.Exp,
                                                                scale=-1.0), [rpc], [r_ex])
                            sc.op("act", lambda e: e.activation(out=ex[:, 256:512], in_=pc[:, 128:384], func=AF.Exp),
                                  [rpc], [r_ex])
                            ptq, rptq = next_ps()
                            sc.op("pe", lambda e: e.transpose(out=ptq[:, 0:128], in_=qTl[:, hc], identity=ident_f),
                                  [r_qTl, r_consts], [rptq])
                            sc.op("pe", lambda e: e.transpose(out=ptq[:, 128:256], in_=kTl[:, hc], identity=ident_f),
                                  [r_kTl, r_consts], [rptq])
                            qd, r_qd = QD[h]
                            sc.op("dve", lambda e: e.tensor_tensor(out=qd[:, 0:128], in0=ptq[:, 0:128],
                                                                   in1=ex[:, 0:128], op=ALU.mult),
                                  [rptq, r_ex], [r_qd])
                            sc.op("dve", lambda e: e.tensor_tensor(out=qd[:, 128:256], in0=ptq[:, 128:256],
                                                                   in1=ex[:, 128:256], op=ALU.mult),
                                  [rptq, r_ex], [r_qd])
                            sc.op("dve", lambda e: e.tensor_tensor(out=qd[:, 256:384], in0=ptq[:, 0:128],
                                                                   in1=ex[:, 256:384], op=ALU.mult),
                                  [rptq, r_ex], [r_qd])
                            sc.op("pool", lambda e: e.tensor_tensor(out=qd[:, 384:512], in0=kTl[:, hc],
                                                                    in1=ex[:, 384:512], op=ALU.mult),
                                  [r_kTl, r_ex], [r_qd])
                            sc.op("pool", lambda e: e.tensor_copy(out=dl[:, h, 0:1], in_=ex[:, 256 + 63:256 + 64]),
                                  [r_ex], [r_dl])
                            sc.op("pool", lambda e: e.tensor_copy(out=dl[:, h, 1:2], in_=ex[:, 256 + 127:256 + 128]),
                                  [r_ex], [r_dl])
                        po, rpo = next_ps(pin=True)
                        for c in range(2):
                            rows = slice(64 * c, 64 * c + 64)
                            cs = slice(64 * c, 64 * c + 64)
                            pa, rpa = next_ps()
                            pd, rpd = next_ps()
                            for h in range(4):
                                qd, r_qd = QD[h]
                                hc = slice(h * 128, (h + 1) * 128)
                                sc.op("pe", lambda e: e.matmul(pa[rows, h * 64:(h + 1) * 64],
                                                               lhsT=qd[:, 128 + 64 * c:128 + 64 * c + 64],
                                                               rhs=qd[:, 64 * c:64 * c + 64], start=True, stop=True),
                                      [r_qd], [rpa])
                                sc.op("pe", lambda e: e.matmul(po[:, h * 128 + 64 * c:h * 128 + 64 * c + 64],
                                                               lhsT=Sst[:, h, :],
                                                               rhs=qd[:, 256 + 64 * c:256 + 64 * c + 64],
                                                               start=(c == 0 and h == 0), stop=False,
                                                               skip_group_check=True), [r_S, r_qd], [rpo])
                            sc.op("dve", lambda e: e.tensor_tensor(
                                out=Am[rows, :, :], in0=pa[rows, 0:256].rearrange("p (h t) -> p h t", h=4),
                                in1=cc("triu")[rows, :].unsqueeze(1).to_broadcast([64, 4, 64]), op=ALU.mult),
                                [rpa, r_consts], [r_Am])
                            for h in range(4):
                                qd, r_qd = QD[h]
                                hc = slice(h * 128, (h + 1) * 128)
                                sc.op("pe", lambda e: e.matmul(po[:, h * 128 + 64 * c:h * 128 + 64 * c + 64],
                                                               lhsT=vT_[rows, hc], rhs=Am[rows, h, :],
                                                               start=False, stop=True, skip_group_check=True),
                                      [r_vT, r_Am], [rpo])
                                sc.op("pe", lambda e: e.matmul(pd[:, hc], lhsT=qd[rows, 384:512], rhs=vT_[rows, hc],
                                                               start=True, stop=True), [r_qd, r_vT], [rpd])
                            for h in range(4):
                                hc = slice(h * 128, (h + 1) * 128)
                                sc.op("dve", lambda e: e.scalar_tensor_tensor(out=Sst[:, h, :], in0=Sst[:, h, :],
                                                                              scalar=dl[:, h, c:c + 1], in1=pd[:, hc],
                                                                              op0=ALU.mult, op1=ALU.add),
                                      [r_S, r_dl, rpd], [r_S])
                        sqb, r_sqb = OSQ_
                        sc.op("act", lambda e: e.activation(out=sqb, in_=po[:, :], func=AF.Square), [rpo], [r_sqb])
                        p2, rp2 = next_ps()
                        sc.op("pe", lambda e: e.matmul(p2[:, :], lhsT=ones_bf[:, :], rhs=sqb, start=True, stop=True),
                              [r_ones, r_sqb], [rp2])
                        rs, r_rs = g(0)
                        rsqrt(rs, r_rs, p2[:, :], rp2, 1.0 / 128)
                        sc.op("dve", lambda e: e.tensor_tensor(out=rs, in0=rs, in1=po[:, :], op=ALU.mult),
                              [r_rs, rpo], [r_rs])
                        sc.op("dve", lambda e: e.scalar_tensor_tensor(
                            out=yC[:, :, tk], in0=rs.rearrange("p (h t) -> p h t", h=4), scalar=cc(f"hgn{l}"),
                            in1=sgg[:, :, tk], op0=ALU.mult, op1=ALU.mult), [r_rs, r_consts, r_sgg], [r_yC])
                        unpin(po)
                    branches.append((2, yC, r_yC))

                if "B" in BR:
                    dsa_tile(l, t)
                    branches.append((1, yB, r_yB))

                branches.sort(key=lambda b: b[0])
                gsig, r_gsig = g(4)
                gl, r_gl = g(5)
                for mg in range(2):
                    for bi, (n, yb, ryb) in enumerate(branches):
                        wt, rw = load_w(wb_in[l], C_GATE + n * D + mg * 512, 512, 8)
                        wl, rwl = load_w(wb_lift[l], mg * 512, 512, 4, krow0=n * 512)
                        last_n = (bi == len(branches) - 1)
                        for mm in range(4):
                            m = mg * 4 + mm
                            macc, r_macc = g(mm)
                            pg, rpg = next_ps()
                            for kc in range(8):
                                sc.op("pe", lambda e: e.matmul(pg[:, :], lhsT=wt[:, kc, mm * 128:(mm + 1) * 128],
                                                               rhs=hT[:, kc, :], start=(kc == 0), stop=(kc == 7)),
                                      [rw, r_hT], [rpg])
                            pl, rpl = next_ps()
                            for kc in range(4):
                                sc.op("pe", lambda e: e.matmul(pl[:, :], lhsT=wl[:, kc, mm * 128:(mm + 1) * 128],
                                                               rhs=yb[:, kc, :], start=(kc == 0), stop=(kc == 3)),
                                      [rwl, ryb], [rpl])
                            sc.op("act", lambda e: e.activation(out=gsig, in_=pg[:, :], func=AF.Sigmoid),
                                  [rpg], [r_gsig])
                            dst, rdst = (mergedT[:, m, :], r_merged) if last_n else (macc, r_macc)
                            if bi == 0:
                                sc.op("dve", lambda e: e.tensor_tensor(out=dst, in0=gsig, in1=pl[:, :], op=ALU.mult),
                                      [r_gsig, rpl], [rdst])
                            else:
                                sc.op("dve", lambda e: e.tensor_tensor(out=gl, in0=gsig, in1=pl[:, :], op=ALU.mult),
                                      [r_gsig, rpl], [r_gl])
                                sc.op("pool", lambda e: e.tensor_tensor(out=dst, in0=gl, in1=macc, op=ALU.add),
                                      [r_gl, r_macc], [rdst])

                for half in range(2):
                    wt, rw = load_w(wb_out[l], half * 512, 512, 8)
                    for s in range(4):
                        p, rp = next_ps()
                        for kc in range(8):
                            sc.op("pe", lambda e: e.matmul(p[:, :], lhsT=mergedT[:, kc, s * 128:(s + 1) * 128],
                                                           rhs=wt[:, kc, :], start=(kc == 0), stop=(kc == 7)),
                                  [rw, r_merged], [rp])
                        sc.op("dve", lambda e: e.tensor_tensor(out=xt[:, s, half * 512:(half + 1) * 512],
                                                               in0=xt[:, s, half * 512:(half + 1) * 512], in1=p[:, :],
                                                               op=ALU.add), [rp, r_xt], [r_xt])

                rmsnorm_to_F(xt, r_xt, 4, f"nffn{l}", hT, r_hT)
                allres = [r_aT, r_yA, r_yB, r_yC, r_yM, r_merged]

                def cons_gate(j, p, rp):
                    sc.op("act", lambda e: e.activation(out=aT[:, j, :], in_=p[:, :], func=AF.Silu), [rp], allres)

                proj_F(wb_up[l], 0, 22, cons_gate, hT, r_hT)

                def cons_up(j, p, rp):
                    sc.op("dve", lambda e: e.tensor_tensor(out=aT[:, j, :], in0=aT[:, j, :], in1=p[:, :], op=ALU.mult),
                          [rp, r_aT], allres)

                proj_F(wb_up[l], FFN, 22, cons_up, hT, r_hT)

                for half in range(2):
                    pss = [next_ps() for _ in range(4)]
                    for gq in range(3):
                        nk = 8 if gq < 2 else 6
                        wt, rw = load_w(wb_dn[l], half * 512, 512, nk, krow0=gq * 1024)
                        for s in range(4):
                            p, rp = pss[s]
                            for kc in range(nk):
                                fc = gq * 8 + kc
                                sc.op("pe", lambda e: e.matmul(p[:, :], lhsT=aT[:, fc, s * 128:(s + 1) * 128],
                                                               rhs=wt[:, kc, :], start=(fc == 0), stop=(fc == 21)),
                                      [rw] + allres, [rp])
                    for s in range(4):
                        p, rp = pss[s]
                        sc.op("dve", lambda e: e.tensor_tensor(out=xt[:, s, half * 512:(half + 1) * 512],
                                                               in0=xt[:, s, half * 512:(half + 1) * 512], in1=p[:, :],
                                                               op=ALU.add), [rp, r_xt], [r_xt])
                sc.dma("sp", dst_d[t0:t0 + 512, :].rearrange("(s p) d -> p s d", p=128), xt[:, :, :],
                       [r_xt], [], "xo")
            sc.wait_tok("sp", (sc.dsem["xo"][0], sc.dsem["xo"][1], None, "d_xo"))
        sc.wait_tok("sp", (sc.dsem["xo"][0], sc.dsem["xo"][1], None, "d_xo"))
        print("ops:", sc.cnt, "waits:", sc.nwait, "sbuf bytes/partition:", sbtot[0])
    return nc


def host_consts(inp):
    lay, NCONST = _const_layout()
    c = np.zeros((128, NCONST), np.float32)

    def put(name, arr):
        o, w = lay[name]
        c[:, o:o + w] = arr

    p = np.arange(128)
    put("ident", np.eye(128, dtype=np.float32))
    same = (p[:, None] // 64) == (p[None, :] // 64)
    U = (same & (p[:, None] <= p[None, :])).astype(np.float32)
    Rb = (same & ((p[:, None] % 64) <= 31)).astype(np.float32)
    SL = (same & (p[:, None] > p[None, :])).astype(np.float32)
    put("Ublk", U)
    put("M1", U - Rb)
    put("SL", SL)
    put("triu", ((p[:, None] % 64) <= np.arange(64)[None, :]).astype(np.float32))
    put("CB", np.where(p[None, :] <= p[:, None], 0.0, NEG).astype(np.float32))
    put("pow2", np.tile((0.5 ** np.arange(32, dtype=np.float64)).astype(np.float32)[None, :], (128, 1)))
    invf = 1.0 / (500000.0 ** (np.arange(0, 16, 2, dtype=np.float32) / 16.0))
    put("invf", np.tile(invf.astype(np.float32)[None, :], (128, 1)))
    put("memn", np.asarray(inp["mem_norm"]).reshape(8, 128).T)
    for l in range(2):
        put(f"nmix{l}", np.asarray(inp["norm_mix"][l]).reshape(8, 128).T)
        put(f"nffn{l}", np.asarray(inp["norm_ffn"][l]).reshape(8, 128).T)
        cw = np.asarray(inp["conv_w"][l])
        put(f"convw{l}", np.concatenate([cw[k].reshape(4, 128).T for k in range(3)], axis=1))
        put(f"hgn{l}", np.asarray(inp["hgrn_out_norm"][l]).reshape(128, 1))
        put(f"mqn{l}", np.asarray(inp["mem_q_norm"][l]).reshape(128, 1))
        put(f"mkn{l}", np.asarray(inp["mem_k_norm"][l]).reshape(128, 1))
        put(f"dqn{l}", np.tile(np.asarray(inp["dsa_q_norm"][l])[None, :], (128, 1)))
        put(f"dkn{l}", np.tile(np.asarray(inp["dsa_k_norm"][l])[None, :], (128, 1)))
    lbr = np.asarray(inp["hgrn_lower_bounds"], dtype=np.float32)
    cbig = np.concatenate([np.tile(np.eye(128, dtype=np.float32), (1, 4)), np.tile(lbr[0][None, :], (128, 1)),
                           np.tile(lbr[1][None, :], (128, 1))], axis=1).astype(np.float32)
    return c, cbig


def make_in_maps(inp, S, nb):
    consts, cbig = host_consts(inp)
    f = lambda a: np.ascontiguousarray(np.asarray(a, dtype=np.float32))
    shared = dict(
        consts=consts,
        cbig=cbig,
        w_in=f(inp["w_in"]),
        mem_w_kv=f(inp["mem_w_kv"]),
        w_lift=f(inp["w_lift"]).reshape(2, 2048, D),
        w_out=f(inp["w_out"]),
        ffn_w_up=f(inp["ffn_w_up"]),
        ffn_w_down=f(inp["ffn_w_down"]),
    )
    maps = []
    for b in range(nb):
        m = dict(shared)
        m["x"] = f(inp["x"][b, :S])
        m["mem"] = f(inp["mem"][b])
        m["pos"] = np.ascontiguousarray(np.asarray(inp["positions"][b, :S], dtype=np.int32).reshape(S // 128, 128).T)
        maps.append(m)
    return maps


def kernel(**inputs):
    S = inputs["x"].shape[1]
    B = inputs["x"].shape[0]
    nc = build(S)
    maps = make_in_maps(inputs, S, B)
    maps = maps + maps
    res = run_bass_kernel_spmd(nc, maps, core_ids=list(range(8)))
    out = np.stack([res.results[b]["y"] for b in range(B)], axis=0)
    return out.astype(np.float32)
```

```python
import numpy as np
from contextlib import ExitStack
import concourse.bass as bass
import concourse.mybir as mybir
from concourse.bass_utils import run_bass_kernel_spmd

F32 = mybir.dt.float32
BF16 = mybir.dt.bfloat16
I32 = mybir.dt.int32
ALU = mybir.AluOpType
AF = mybir.ActivationFunctionType
AX = mybir.AxisListType

D = 1024
IN_COLS = 9416
FFN = 2816
MEMT = 256
EPS = 1e-6
STAGE = 99
NOCAST = False
C_AX, C_AB, C_AC = 0, 512, 1024
C_DQ, C_DK, C_DV = 1536, 2048, 2112
C_IQ, C_IK, C_IW = 2176, 2688, 2752
C_GQ, C_GF, C_GI, C_GG = 2760, 3272, 3784, 4296
C_MQ = 4808
C_GATE = 5320


class Res:
    __slots__ = ("name", "w", "r", "excl")

    def __init__(self, name, excl=False):
        self.name = name
        self.w = None
        self.r = {}
        self.excl = excl


class Sched:
    def __init__(self, nc, stack):
        self.nc = nc
        self.eng = dict(pe=nc.tensor, act=nc.scalar, dve=nc.vector, pool=nc.gpsimd, sp=nc.sync)
        self.stack = stack
        self.prog = {e: stack.enter_context(nc.semaphore("prog_" + e)) for e in self.eng}
        self.cnt = {e: 0 for e in self.eng}
        self.waited = {e: {} for e in self.eng}
        self.dsem = {}
        self.nwait = 0

    def dma_sem(self, key):
        if key not in self.dsem:
            self.dsem[key] = [self.stack.enter_context(self.nc.semaphore("d_" + key)), 0]
        return key

    def _wait(self, e, tok):
        sem, val, src, key = tok
        if src == e and e == "pe":
            return
        if self.waited[e].get(key, 0) >= val:
            return
        self.eng[e].wait_ge(sem, val)
        self.waited[e][key] = val
        self.nwait += 1

    def _deps(self, e, reads, writes):
        for r in reads:
            if r.w is not None:
                self._wait(e, r.w)
            if r.excl:
                for tok in r.r.values():
                    if tok[2] != e:
                        self._wait(e, tok)
        for w in writes:
            if w.w is not None:
                self._wait(e, w.w)
            for tok in w.r.values():
                self._wait(e, tok)

    def _mark(self, tok, reads, writes):
        key = tok[3]
        for r in reads:
            r.r[key] = tok
        for w in writes:
            w.w = tok
            w.r = {}

    def op(self, e, fn, reads=(), writes=()):
        self._deps(e, reads, writes)
        ins = fn(self.eng[e])
        self.cnt[e] += 1
        ins.then_inc(self.prog[e], 1)
        tok = (self.prog[e], self.cnt[e], e, "prog_" + e)
        self._mark(tok, reads, writes)
        return tok

    def dma(self, q, out, in_, reads, writes, key, **kw):
        self.dma_sem(key)
        self._deps(q, reads, writes)
        ent = self.dsem[key]
        ent[1] += 16
        self.eng[q].dma_start(out=out, in_=in_, **kw).then_inc(ent[0], 16)
        tok = (ent[0], ent[1], None, "d_" + key)
        self._mark(tok, reads, writes)
        return tok

    def wait_tok(self, e, tok):
        self._wait(e, tok)


BRANCHES = "ABCM"
NEG = -1.0e9
MBNEG = -30000.0
NBIS = 16
DVE_FRAC = 1.0
PIPELINE = True


def _const_layout():
    lay = {}
    off = 0

    def add(name, w):
        nonlocal off
        lay[name] = (off, w)
        off += w

    add("ident", 128)
    add("Ublk", 128)
    add("M1", 128)
    add("SL", 128)
    add("triu", 64)
    add("CB", 128)
    add("pow2", 32)
    add("invf", 8)
    add("memn", 8)
    for l in range(2):
        add(f"nmix{l}", 8)
        add(f"nffn{l}", 8)
        add(f"convw{l}", 12)
        add(f"hgn{l}", 1)
        add(f"mqn{l}", 1)
        add(f"mkn{l}", 1)
        add(f"dqn{l}", 64)
        add(f"dkn{l}", 64)
    return lay, off


def build(S, n_layers=2, dbg=False):
    nc = bass.Bass("TRN2", target_bir_lowering=False)
    NT = S // 512
    NQ = S // 128
    lay, NCONST = _const_layout()
    BR = BRANCHES

    def din(name, shape, dt=F32):
        return nc.dram_tensor(name, list(shape), dt, kind="ExternalInput").ap()

    x_d = din("x", [S, D])
    mem_d = din("mem", [MEMT, D])
    pos_d = din("pos", [128, NQ], I32)
    consts_d = din("consts", [128, NCONST])
    cbig_d = din("cbig", [128, 1536])
    w_in_d = din("w_in", [2, D, IN_COLS])
    w_kv_d = din("mem_w_kv", [2, D, D])
    w_lift_d = din("w_lift", [2, 4 * 512, D])
    w_out_d = din("w_out", [2, D, D])
    w_up_d = din("ffn_w_up", [2, D, 2 * FFN])
    w_dn_d = din("ffn_w_down", [2, FFN, D])
    y_d = nc.dram_tensor("y", [S, D], F32, kind="ExternalOutput").ap()

    def dscr(name, shape, dt):
        return nc.dram_tensor(name, list(shape), dt, kind="Internal").ap()

    wb_in = dscr("wb_in", [2, D, IN_COLS], BF16)
    wb_kv = dscr("wb_kv", [2, D, D], BF16)
    wb_lift = dscr("wb_lift", [2, 2048, D], BF16)
    wb_out = dscr("wb_out", [2, D, D], BF16)
    wb_up = dscr("wb_up", [2, D, 2 * FFN], BF16)
    wb_dn = dscr("wb_dn", [2, FFN, D], BF16)
    x1_d = dscr("x1", [S, D], F32)

    stack = ExitStack()
    with stack:
        sc = Sched(nc, stack)

        sbtot = [0]

        def sb(name, shape, dt=F32):
            n_ = 1
            for d_ in shape[1:]:
                n_ *= d_
            sbtot[0] += n_ * (2 if dt == BF16 else 4)
            return nc.alloc_sbuf_tensor("sb_" + name, list(shape), dt)

        def T(name, shape, dt=F32):
            t_ = sb(name, shape, dt)
            return t_[tuple(slice(None) for _ in shape)], Res(name)

        consts, r_consts = T("consts", [128, NCONST])
        sc.dma("sp", consts[:, :], consts_d[:, :], [], [r_consts], "const")

        def cc(name, j=0, w=None):
            o, ww = lay[name]
            if w is None:
                w = ww - j
            return consts[:, o + j:o + j + w]

        ident_bf, r_identbf = T("ident_bf", [128, 128], BF16)
        sc.op("dve", lambda e: e.tensor_copy(out=ident_bf[:, :], in_=cc("ident")), [r_consts], [r_identbf])
        ones_bf, r_ones = T("ones_bf", [128, 128], BF16)
        sc.op("pool", lambda e: e.memset(ones_bf[:, :], 1.0), [], [r_ones])
        E4_bf, r_E4 = T("E4_bf", [128, 512], BF16)
        epsc, r_epsc = T("epsc", [128, 1])
        sc.op("pool", lambda e: e.memset(epsc[:, :], EPS), [], [r_epsc])
        negpi, r_negpi = T("negpi", [128, 1])
        sc.op("pool", lambda e: e.memset(negpi[:, :], -float(np.pi)), [], [r_negpi])

        def rsqrt(out, r_out, in_, r_in, scale):
            np_ = out.shape[0]
            sc.op("act", lambda e: e.activation(out=out, in_=in_, func=AF.Sqrt, bias=epsc[0:np_, :],
                                                scale=scale), [r_in, r_epsc], [r_out])
            sc.op("dve", lambda e: e.reciprocal(out=out, in_=out), [r_out], [r_out])

        CW = 2048
        cast_toks = []
        with nc.sbuf_tensor("stg_all", [128, 4, CW], F32) as stg_all, \
                nc.sbuf_tensor("stgb_all", [128, 4, CW], BF16) as stgb_all:
            r_stg = [Res(f"stg{i}") for i in range(4)]
            r_stgb = [Res(f"stgb{i}") for i in range(4)]
            ci = [0]
            cast_eng = ["dve", "act", "dve", "act"]

            def cast_matrix(src, dst, rows, cols):
                for r0 in range(0, rows, 128):
                    for c0 in range(0, cols, CW):
                        w = min(CW, cols - c0)
                        k = ci[0] % 4
                        sc.dma("sp", stg_all[:, k, 0:w], src[r0:r0 + 128, c0:c0 + w], [], [r_stg[k]], f"stg{k}")
                        e = cast_eng[k]
                        if e == "act":
                            sc.op("act", lambda en: en.copy(out=stgb_all[:, k, 0:w], in_=stg_all[:, k, 0:w]),
                                  [r_stg[k]], [r_stgb[k]])
                        else:
                            sc.op(e, lambda en: en.tensor_copy(out=stgb_all[:, k, 0:w], in_=stg_all[:, k, 0:w]),
                                  [r_stg[k]], [r_stgb[k]])
                        t = sc.dma("pool", dst[r0:r0 + 128, c0:c0 + w], stgb_all[:, k, 0:w], [r_stgb[k]], [],
                                   f"stgb{k}")
                        cast_toks.append(t)
                        ci[0] += 1

            for l in range(0 if NOCAST else n_layers):
                cast_matrix(w_in_d[l], wb_in[l], D, IN_COLS)
                cast_matrix(w_kv_d[l], wb_kv[l], D, D)
                cast_matrix(w_lift_d[l], wb_lift[l], 2048, D)
                cast_matrix(w_out_d[l], wb_out[l], D, D)
                cast_matrix(w_up_d[l], wb_up[l], D, 2 * FFN)
                cast_matrix(w_dn_d[l], wb_dn[l], FFN, D)
            last = {}
            for t in cast_toks:
                last[t[3]] = t
            for t in last.values():
                for e in ("sp", "act", "dve", "pool", "pe"):
                    sc.wait_tok(e, t)

        NPS = 8
        ps = [nc.alloc_psum_tensor(f"ps{i}", [128, 512], F32) for i in range(NPS)]
        r_ps = [Res(f"ps{i}", excl=True) for i in range(NPS)]
        psi = [0]
        pti = [0]

        pinned = set()

        def next_ps(pin=False):
            while True:
                i = psi[0] % NPS
                psi[0] += 1
                if i not in pinned:
                    break
            if pin:
                pinned.add(i)
            return ps[i], r_ps[i]

        def unpin(p):
            pinned.discard(ps.index(p))

        def next_pst():
            p_, rp_ = next_ps()
            return p_[:, :].bitcast(BF16)[:, 0:512], rp_

        NW = 2
        wslot = [sb(f"wslot{i}", [128, 8, 512], BF16) for i in range(NW)]
        r_wslot = [Res(f"wslot{i}") for i in range(NW)]
        wi = [0]

        def load_w(src, c0, ncols, nk, krow0=0):
            i = wi[0] % NW
            wi[0] += 1
            v = src[krow0:krow0 + nk * 128, c0:c0 + ncols].rearrange("(kc p) c -> p kc c", p=128)
            sc.dma("sp", wslot[i][:, 0:nk, 0:ncols], v, [], [r_wslot[i]], f"wslot{i}")
            return wslot[i], r_wslot[i]

        xt, r_xt = T("xt", [128, 4, D])
        hT, r_hT = T("hT", [128, 8, 512], BF16)
        ssq, r_ssq = T("ssq", [128, 4])
        rstd, r_rstd = T("rstd", [128, 4])
        aT, r_aT = T("aT", [128, 24, 512], BF16)
        yA, yB, yC, yM = aT[:, 0:4, :], aT[:, 4:8, :], aT[:, 8:12, :], aT[:, 12:16, :]
        r_yA, r_yB, r_yC, r_yM = Res("yA"), Res("yB"), Res("yC"), Res("yM")
        mergedT = aT[:, 16:24, :]
        r_merged = Res("merged")
        xn = aT[:, 16:24, :].rearrange("p (s a) c -> p s (a c)", a=2)
        r_xn = r_merged
        uhalo, r_uhalo = T("uhalo", [128, 4, 2])
        G = [T(f"g{i}", [128, 514]) for i in range(6)]

        def g(i):
            return G[i][0][:, 0:512], G[i][1]

        memT = aT[:, 0:8, 0:256]
        r_memT = Res("memT")
        for i_, c0_ in ((0, 0), (1, 512), (2, 1024)):
            sc.dma("sp", G[i_][0][:, 0:512], cbig_d[:, c0_:c0_ + 512], [], [G[i_][1]], f"cbig{i_}")
        sc.op("dve", lambda e: e.tensor_copy(out=E4_bf[:, :], in_=G[0][0][:, 0:512]), [G[0][1]], [r_E4])

        def rmsnorm_to_F(src, r_src, nsub, gname, dst, r_dst):
            junk, r_junk = g(3)
            for s in range(nsub):
                for hf in range(2):
                    sc.op("act", lambda e: e.activation(out=junk, in_=src[:, s, hf * 512:(hf + 1) * 512],
                                                        func=AF.Square, accum_out=ssq2[:, 2 * s + hf:2 * s + hf + 1]),
                          [r_src], [r_junk, r_ssq2])
            sc.op("dve", lambda e: e.tensor_reduce(out=ssq[:, 0:nsub],
                                                   in_=ssq2[:, 0:2 * nsub].rearrange("p (s two) -> p s two", two=2),
                                                   axis=AX.X, op=ALU.add), [r_ssq2], [r_ssq])
            rsqrt(rstd[:, 0:nsub], r_rstd, ssq[:, 0:nsub], r_ssq, 1.0 / D)
            for s in range(nsub):
                if s % 2 == 0:
                    sc.op("dve", lambda e: e.tensor_scalar(out=xn[:, s, :], in0=src[:, s, :],
                                                           scalar1=rstd[:, s:s + 1], scalar2=None, op0=ALU.mult),
                          [r_src, r_rstd], [r_xn])
                else:
                    sc.op("act", lambda e: e.activation(out=xn[:, s, :], in_=src[:, s, :], func=AF.Copy,
                                                        scale=rstd[:, s:s + 1]), [r_src, r_rstd], [r_xn])
            for kc in range(8):
                pt, rpt = next_pst()
                for s in range(nsub):
                    sc.op("pe", lambda e: e.transpose(out=pt[:, s * 128:(s + 1) * 128],
                                                      in_=xn[:, s, kc * 128:(kc + 1) * 128], identity=ident_bf[:, :]),
                          [r_xn, r_identbf], [rpt])
                sc.op("act", lambda e: e.activation(out=dst[:, kc, 0:nsub * 128], in_=pt[:, 0:nsub * 128],
                                                    func=AF.Copy, scale=cc(gname, kc, 1)),
                      [rpt, r_consts], r_dst if isinstance(r_dst, list) else [r_dst])

        ssq2, r_ssq2 = T("ssq2", [128, 8])

        def proj_F(wsrc, c0, n128, consume, rhs, r_rhs, nk=8, krow0=0, ntok=512):
            for g0 in range(0, n128, 4):
                gn = min(4, n128 - g0)
                wt, rw = load_w(wsrc, c0 + g0 * 128, gn * 128, nk, krow0)
                for j in range(gn):
                    p, rp = next_ps()
                    for kc in range(nk):
                        sc.op("pe", lambda e: e.matmul(p[:, 0:ntok], lhsT=wt[:, kc, j * 128:(j + 1) * 128],
                                                       rhs=rhs[:, kc, 0:ntok], start=(kc == 0), stop=(kc == nk - 1)),
                              [rw, r_rhs], [rp])
                    consume(g0 + j, p, rp)

        def proj_T(wsrc, c0, ncols, consume, subs=(0, 1, 2, 3)):
            wt, rw = load_w(wsrc, c0, ncols, 8)
            for s in subs:
                p, rp = next_ps()
                for kc in range(8):
                    sc.op("pe", lambda e: e.matmul(p[:, 0:ncols], lhsT=hT[:, kc, s * 128:(s + 1) * 128],
                                                   rhs=wt[:, kc, 0:ncols], start=(kc == 0), stop=(kc == 7)),
                          [rw, r_hT], [rp])
                consume(s, p, rp)

        ident_f = cc("ident")
        if "B" in BR:
            posi, r_posi = T("posi", [128, NQ], I32)
            sc.dma("sp", posi[:, :], pos_d[:, :], [], [r_posi], "posi")
            posf, r_posf = T("posf", [128, NQ])
            sc.op("dve", lambda e: e.tensor_copy(out=posf[:, :], in_=posi[:, :]), [r_posi], [r_posf])
            cosT, r_cos = T("cosT", [128, NQ, 8])
            sinT, r_sin = T("sinT", [128, NQ, 8])
            ang, r_ang = G[3][0][:, 0:NQ * 8].rearrange("p (n j) -> p n j", j=8), G[3][1]
            sc.op("dve", lambda e: e.tensor_tensor(out=ang[:, :, :],
                                                   in0=posf[:, :].unsqueeze(2).to_broadcast([128, NQ, 8]),
                                                   in1=cc("invf").unsqueeze(1).to_broadcast([128, NQ, 8]),
                                                   op=ALU.mult), [r_posf, r_consts], [r_ang])
            TWO_PI = float(2 * np.pi)
            MAGIC = 12582912.0
            nrd, r_nrd = G[4][0][:, 0:NQ * 8].rearrange("p (n j) -> p n j", j=8), G[4][1]
            for (dst, rdst, shift) in ((sinT, r_sin, 0.0), (cosT, r_cos, 0.25)):
                sc.op("dve", lambda e: e.tensor_scalar(out=dst[:, :, :], in0=ang[:, :, :], scalar1=1.0 / TWO_PI,
                                                       scalar2=shift, op0=ALU.mult, op1=ALU.add), [r_ang], [rdst])
                sc.op("dve", lambda e: e.tensor_scalar(out=nrd[:, :, :], in0=dst[:, :, :], scalar1=MAGIC,
                                                       scalar2=None, op0=ALU.add), [rdst], [r_nrd])
                sc.op("dve", lambda e: e.tensor_scalar(out=nrd[:, :, :], in0=nrd[:, :, :], scalar1=MAGIC,
                                                       scalar2=None, op0=ALU.subtract), [r_nrd], [r_nrd])
                sc.op("dve", lambda e: e.tensor_tensor(out=dst[:, :, :], in0=dst[:, :, :], in1=nrd[:, :, :],
                                                       op=ALU.subtract), [rdst, r_nrd], [rdst])
                sc.op("dve", lambda e: e.tensor_scalar(out=dst[:, :, :], in0=dst[:, :, :], scalar1=-0.49999,
                                                       scalar2=0.49999, op0=ALU.max, op1=ALU.min), [rdst], [rdst])
                sc.op("act", lambda e: e.activation(out=dst[:, :, :], in_=dst[:, :, :], func=AF.Sin,
                                                    scale=TWO_PI), [rdst], [rdst])
            kikT, r_kik = T("kikT", [128, S], BF16)
            r_kikc = [Res(f"kik{c}") for c in range(NQ)]
            Vall, r_V = T("Vall", [128, NQ, 65], BF16)
            r_Vc = [Res(f"V{c}") for c in range(NQ)]
            sc.op("pool", lambda e: e.memset(Vall[:, :, 64:65], 1.0), [], [r_V])
            score, r_score = T("score", [128, S])
            qiq = [T(f"qiq{i}", [128, 1, 8, 128], BF16) for i in range(3)]
            iqTb = [q_[0] for q_ in qiq]
            r_iqTb = [Res(f"iqTb{i}") for i in range(3)]
            wq, r_wq = T("wq", [128, 4, 8])
            wabs, r_wabs = T("wabs", [128, 4, 8])
            wsgn, r_wsgn = T("wsgn", [128, 4, 8])
            rbb = [T(f"rbb{i}", [128, 512], BF16) for i in range(3)]
            Dm = [T(f"Dm{i}", [128, 8, 128], BF16) for i in range(2)]
            stg_iq, r_stgiq = T("stg_iq", [128, 8, 128])
            sc.op("pool", lambda e: e.memset(stg_iq[:, :, :], 0.0), [], [r_stgiq])
            PTb = [T(f"PT{i}", [128, 512], BF16) for i in range(3)]
            rt = [T(f"rt{i}", [128, 8, 8]) for i in range(4)]
            thr, r_thr = T("thr", [128, 1])
            thr0, r_thr0 = T("thr0", [128, 1])
            sc.op("pool", lambda e: e.memset(thr0[:, :], -1.0e8), [], [r_thr0])

        if "C" in BR:
            lb1, r_lb1 = T("lb1", [128, 512])
            sc.op("dve", lambda e: e.tensor_tensor(out=lb1[:, :], in0=G[2][0][:, 0:512], in1=G[1][0][:, 0:512],
                                                   op=ALU.subtract), [G[1][1], G[2][1]], [r_lb1])
            sc.op("act", lambda e: e.activation(out=lb1[:, :], in_=lb1[:, :], func=AF.Sigmoid), [r_lb1], [r_lb1])
            Sst, r_S = T("Sst", [128, 4, 128])
            sgg, r_sgg = aT[:, 16:20, :], r_merged
            ex, r_ex = T("ex", [128, 512])
            prod, r_prod = T("prod", [128, 512])
            Am, r_Am = T("Am", [128, 4, 64])
            vT_, r_vT = T("vTl", [128, 512])
            qTl, r_qTl = T("qTl", [128, 512])
            kTl, r_kTl = T("kTl", [128, 512])
            lfT, r_lfT = T("lfT", [128, 512])
            dl, r_dl = T("dl", [128, 4, 2])
            OSQ = None
            QD = [g(1 + h) for h in range(4)]


        if "B" in BR:
            RB = [g(3), g(4)]
            cjunk, r_cjunk = T("cjunk", [128, 1024], BF16)
            if DVE_FRAC < 1.0:
                ajunk, r_ajunk = T("ajunk", [128, 1024], BF16)
                asum, r_asum = T("asum", [128, 4])
            mid, r_mid = T("mid", [128, 1])
            MBf, r_MB = T("MBf", [128, S], BF16)
            qTb = [q_[0] for q_ in qiq]
            r_qTb = [q_[1] for q_ in qiq]
            ob, r_ob = g(2)
            stg_k4, r_stgk4 = T("stg_k4", [128, 4, 128])
            sm, r_sm = T("sm", [128, 8])
            wk, r_wk = T("wk", [128, 32])
            sA, r_sA = T("sA", [128, 16])
            sB, r_sB = T("sB", [128, 16])
            sC, r_sC = T("sC", [128, 4])
            rec8, r_rec8 = T("rec8", [128, 8])

            def rope(v3, nh, tq, r_v):
                cb = cosT[:, tq, :].unsqueeze(1).to_broadcast([128, nh, 8])
                sb_ = sinT[:, tq, :].unsqueeze(1).to_broadcast([128, nh, 8])
                x1 = v3[:, :, 0:8]
                x2 = v3[:, :, 8:16]
                tt = [rt[i][0][:, 0:nh, :] for i in range(4)]
                rr = [rt[i][1] for i in range(4)]
                sc.op("pool", lambda e: e.tensor_tensor(out=tt[0], in0=x1, in1=cb, op=ALU.mult), [r_v, r_cos], [rr[0]])
                sc.op("pool", lambda e: e.tensor_tensor(out=tt[1], in0=x2, in1=sb_, op=ALU.mult), [r_v, r_sin], [rr[1]])
                sc.op("pool", lambda e: e.tensor_tensor(out=tt[2], in0=x1, in1=sb_, op=ALU.mult), [r_v, r_sin], [rr[2]])
                sc.op("pool", lambda e: e.tensor_tensor(out=tt[3], in0=x2, in1=cb, op=ALU.mult), [r_v, r_cos], [rr[3]])
                sc.op("pool", lambda e: e.tensor_tensor(out=x1, in0=tt[0], in1=tt[1], op=ALU.subtract),
                      [rr[0], rr[1]], [r_v])
                sc.op("pool", lambda e: e.tensor_tensor(out=x2, in0=tt[2], in1=tt[3], op=ALU.add),
                      [rr[2], rr[3]], [r_v])

            def dsa_tile(l, t):
                def cons_q(s, p, rp):
                    tq = 4 * t + s
                    sq, r_sq = g(0)
                    sc.op("act", lambda e: e.activation(out=sq, in_=p[:, :], func=AF.Square), [rp], [r_sq])
                    sc.op("dve", lambda e: e.tensor_reduce(out=sA[:, 0:8], in_=sq.rearrange("p (h d) -> p h d", h=8),
                                                           axis=AX.X, op=ALU.add), [r_sq], [r_sA])
                    rsqrt(sA[:, 8:16], r_sB, sA[:, 0:8], r_sA, 1.0 / 64)
                    qn, r_qn = g(1)
                    qn3 = qn.rearrange("p (h d) -> p h d", h=8)
                    sc.op("dve", lambda e: e.tensor_tensor(out=qn3, in0=p[:, :].rearrange("p (h d) -> p h d", h=8),
                                                           in1=sA[:, 8:16].unsqueeze(2).to_broadcast([128, 8, 64]),
                                                           op=ALU.mult), [rp, r_sB], [r_qn])
                    sc.op("pool", lambda e: e.tensor_tensor(out=qn3, in0=qn3,
                                                            in1=cc(f"dqn{l}").unsqueeze(1).to_broadcast([128, 8, 64]),
                                                            op=ALU.mult), [r_qn, r_consts], [r_qn])
                    rope(qn3, 8, tq, r_qn)

                def q_b(s):
                    qn, r_qn = g(1)
                    for half in range(2):
                        pt, rpt = next_ps()
                        for hh in range(4):
                            h = half * 4 + hh
                            sc.op("pe", lambda e: e.transpose(out=pt[0:64, hh * 128:(hh + 1) * 128],
                                                              in_=qn[:, h * 64:(h + 1) * 64], identity=ident_f),
                                  [r_qn, r_consts], [rpt])
                        sc.op("act", lambda e: e.copy(out=qTb[s % 3][0:64, 0, half * 4:(half + 1) * 4, :],
                                                      in_=pt[0:64, :].rearrange("p (h t) -> p h t", h=4)),
                              [rpt], [r_qTb[s % 3]])

                def cons_iq(s, p, rp):
                    tq = 4 * t + s
                    sc.op("act", lambda e: e.copy(out=stg_iq[:, :, 64:128],
                                                  in_=p[:, :].rearrange("p (h d) -> p h d", h=8)), [rp], [r_stgiq])
                    sc.op("pool", lambda e: e.tensor_tensor(
                        out=stg_iq[:, :, 64:128], in0=stg_iq[:, :, 64:128],
                        in1=wabs[:, s, :].unsqueeze(2).to_broadcast([128, 8, 64]), op=ALU.mult),
                        [r_stgiq, r_wabs], [r_stgiq])
                    rope(stg_iq[:, :, 64:128], 8, tq, r_stgiq)

                def iq_b(s):
                    for half in range(2):
                        pt, rpt = next_ps()
                        for hh in range(4):
                            h = half * 4 + hh
                            sc.op("pe", lambda e: e.transpose(out=pt[:, hh * 128:(hh + 1) * 128],
                                                              in_=stg_iq[:, h, :], identity=ident_f),
                                  [r_stgiq, r_consts], [rpt])
                        sc.op("act", lambda e: e.copy(out=iqTb[s % 3][64:128, 0, half * 4:(half + 1) * 4, :],
                                                      in_=pt[64:128, :].rearrange("p (h t) -> p h t", h=4)),
                              [rpt], [r_iqTb[s % 3]])

                def gen_proj(s, phase):
                    if phase == 0:
                        proj_T(wb_in[l], C_DQ, 512, cons_q, subs=(s,))
                        proj_T(wb_in[l], C_IQ, 512, cons_iq, subs=(s,))
                        return
                    q_b(s)
                    iq_b(s)
                    dm_, r_dm_ = Dm[s % 2]
                    for h in range(8):
                        sc.op("pool", lambda e: e.tensor_scalar(out=dm_[:, h, :], in0=ident_bf[:, :],
                                                                scalar1=wsgn[:, s, h:h + 1], scalar2=None,
                                                                op0=ALU.mult), [r_identbf, r_wsgn], [r_dm_])
                    return
                    yield

                def cons_kv(s, p, rp):
                    tq = 4 * t + s
                    jk, r_jk = g(2)
                    sc.op("act", lambda e: e.activation(out=jk[:, 0:64], in_=p[:, 0:64], func=AF.Square,
                                                        accum_out=sC[:, 0:1]), [rp], [r_jk, r_sC])
                    rsqrt(sC[:, 1:2], r_sC, sC[:, 0:1], r_sC, 1.0 / 64)
                    sc.op("dve", lambda e: e.tensor_scalar(out=stg_k4[:, s, 0:64], in0=p[:, 0:64], scalar1=sC[:, 1:2],
                                                           scalar2=None, op0=ALU.mult), [rp, r_sC], [r_stgk4])
                    sc.op("pool", lambda e: e.tensor_tensor(out=stg_k4[:, s, 0:64], in0=stg_k4[:, s, 0:64],
                                                            in1=cc(f"dkn{l}"), op=ALU.mult),
                          [r_stgk4, r_consts], [r_stgk4])
                    rope(stg_k4[:, s:s + 1, 0:64], 1, tq, r_stgk4)
                    sc.op("act", lambda e: e.copy(out=Vall[:, tq, 0:64], in_=p[:, 64:128]), [rp], [r_Vc[tq], r_V])

                def cons_ik(s, p, rp):
                    tq = 4 * t + s
                    sc.op("act", lambda e: e.copy(out=stg_k4[:, s, 64:128], in_=p[:, 0:64]), [rp], [r_stgk4])
                    rope(stg_k4[:, s:s + 1, 64:128], 1, tq, r_stgk4)
                    sc.op("dve", lambda e: e.tensor_scalar(out=wq[:, s, :], in0=p[:, 64:72],
                                                           scalar1=float(8 ** -0.5 * 64 ** -0.5), scalar2=None,
                                                           op0=ALU.mult), [rp], [r_wq])
                    sc.op("act", lambda e: e.activation(out=wsgn[:, s, :], in_=wq[:, s, :], func=AF.Sign),
                          [r_wq], [r_wsgn])
                    sc.op("dve", lambda e: e.tensor_tensor(out=wabs[:, s, :], in0=wq[:, s, :], in1=wsgn[:, s, :],
                                                           op=ALU.mult), [r_wq, r_wsgn], [r_wabs])
                    pt, rpt = next_ps()
                    sc.op("pe", lambda e: e.transpose(out=pt[:, 0:128], in_=stg_k4[:, s, :], identity=ident_f),
                          [r_stgk4, r_consts], [rpt])
                    sc.op("act", lambda e: e.copy(out=kikT[:, tq * 128:(tq + 1) * 128], in_=pt[:, 0:128]),
                          [rpt], [r_kikc[tq]])

                proj_T(wb_in[l], C_DK, 128, cons_kv)
                proj_T(wb_in[l], C_IK, 72, cons_ik)

                def gen_idx(s, phase):
                    j = 4 * t + s
                    L = 128 * (j + 1)
                    iqT_, r_iqT_ = iqTb[s % 3], r_iqTb[s % 3]
                    if phase == 0:
                        dm_, r_dm_ = Dm[s % 2]
                        for c in range(t + 1):
                            W = 512 if c < t else (s + 1) * 128
                            kres = [r_kikc[4 * c + i] for i in range(W // 128)]
                            psc, rpsc = next_ps(pin=True)

                            def emit_x(h):
                                px, rpx = next_ps()
                                sc.op("pe", lambda e: e.matmul(px[:, 0:W], lhsT=iqT_[64:128, 0, h, :],
                                                               rhs=kikT[64:128, c * 512:c * 512 + W], start=True,
                                                               stop=True), [r_iqT_] + kres, [rpx])
                                rb, r_rb = rbb[h % 3]
                                sc.op("act", lambda e: e.activation(out=rb[:, 0:W], in_=px[:, 0:W], func=AF.Relu),
                                      [rpx], [r_rb])

                            def emit_acc(h):
                                rb, r_rb = rbb[h % 3]
                                sc.op("pe", lambda e: e.matmul(psc[:, 0:W], lhsT=dm_[:, h, :], rhs=rb[:, 0:W],
                                                               start=(h == 0), stop=(h == 7)), [r_dm_, r_rb], [rpsc])

                            emit_x(0)
                            emit_x(1)
                            for h in range(8):
                                if h + 2 < 8:
                                    emit_x(h + 2)
                                emit_acc(h)
                            sc.op("dve", lambda e: e.tensor_copy(out=score[:, c * 512:c * 512 + W], in_=psc[:, 0:W]),
                                  [rpsc], [r_score])
                            unpin(psc)
                            yield
                        sc.op("dve", lambda e: e.tensor_tensor(out=score[:, j * 128:(j + 1) * 128],
                                                               in0=score[:, j * 128:(j + 1) * 128], in1=cc("CB"),
                                                               op=ALU.add), [r_score, r_consts], [r_score])
                        return
                    if j >= 2 and phase == 1:
                        sc.op("dve", lambda e: e.tensor_reduce(out=sm[:, 0:1], in_=score[:, 0:256], axis=AX.X,
                                                               op=ALU.min), [r_score], [r_sm])
                        sc.op("dve", lambda e: e.tensor_reduce(out=sm[:, 1:2], in_=score[:, 0:L], axis=AX.X,
                                                               op=ALU.max), [r_score], [r_sm])
                        sc.op("dve", lambda e: e.tensor_scalar(out=sm[:, 2:3], in0=sm[:, 1:2], scalar1=sm[:, 0:1],
                                                               scalar2=0.5, op0=ALU.subtract, op1=ALU.mult),
                              [r_sm], [r_sm])
                        sc.op("dve", lambda e: e.tensor_scalar(out=wk[:, :], in0=cc("pow2"), scalar1=sm[:, 2:3],
                                                               scalar2=None, op0=ALU.mult), [r_sm, r_consts], [r_wk])
                        sc.op("dve", lambda e: e.tensor_tensor(out=mid[:, :], in0=sm[:, 0:1], in1=sm[:, 2:3],
                                                               op=ALU.add), [r_sm], [r_mid])
                        Ld = L if DVE_FRAC >= 1.0 else min(L, max(256, int(round(L * DVE_FRAC / 256.0)) * 256))
                        nact = L - Ld
                        for k in range(NBIS):
                            if nact > 0:
                                for ci_, c0_ in enumerate(range(Ld, L, 1024)):
                                    w_ = min(1024, L - c0_)
                                    sc.op("act", lambda e: e.activation(
                                        out=ajunk[:, 0:w_], in_=score[:, c0_:c0_ + w_], func=AF.Sign,
                                        bias=mid[:, 0:1], scale=-1.0, accum_out=asum[:, ci_:ci_ + 1]),
                                        [r_score, r_mid], [r_ajunk, r_asum])
                                nac = ci_ + 1
                            ccol = 4
                            for ci_, c0_ in enumerate(range(0, Ld, 1024)):
                                w_ = min(1024, Ld - c0_)
                                prev = ccol
                                ccol = 6 + (ci_ % 2)
                                sc.op("dve", lambda e: e.tensor_scalar(
                                    out=cjunk[:, 0:w_], in0=score[:, c0_:c0_ + w_], scalar1=mid[:, 0:1],
                                    scalar2=(None if ci_ == 0 else sm[:, prev:prev + 1]), op0=ALU.is_ge,
                                    op1=ALU.add, accum_out=sm[:, ccol:ccol + 1]),
                                    [r_score, r_mid, r_sm], [r_cjunk, r_sm])
                            if nact > 0:
                                for a_ in range(nac):
                                    sc.op("dve", lambda e: e.scalar_tensor_tensor(
                                        out=sm[:, ccol:ccol + 1], in0=asum[:, a_:a_ + 1], scalar=-0.5,
                                        in1=sm[:, ccol:ccol + 1], op0=ALU.mult, op1=ALU.add),
                                        [r_asum, r_sm], [r_sm])
                            sc.op("dve", lambda e: e.tensor_scalar(out=sm[:, 5:6], in0=sm[:, ccol:ccol + 1],
                                                                   scalar1=256.0 - nact / 2.0,
                                                                   scalar2=0.5, op0=ALU.is_ge, op1=ALU.subtract),
                                  [r_sm], [r_sm])
                            sc.op("dve", lambda e: e.scalar_tensor_tensor(out=mid[:, :], in0=sm[:, 5:6],
                                                                          scalar=wk[:, k:k + 1], in1=mid[:, :],
                                                                          op0=ALU.mult, op1=ALU.add),
                                  [r_sm, r_wk, r_mid], [r_mid])
                            yield
                        sc.op("dve", lambda e: e.tensor_tensor(out=thr[:, :], in0=mid[:, :],
                                                               in1=wk[:, NBIS:NBIS + 1], op=ALU.subtract),
                              [r_mid, r_wk], [r_thr])
                        return
                    if phase == 1:
                        return
                    th, r_th = (thr, r_thr) if j >= 2 else (thr0, r_thr0)
                    for c0_ in range(0, L, 2048):
                        w_ = min(2048, L - c0_)
                        sc.op("dve", lambda e: e.tensor_scalar(out=MBf[:, c0_:c0_ + w_], in0=score[:, c0_:c0_ + w_],
                                                               scalar1=th[:, 0:1], scalar2=MBNEG, op0=ALU.is_lt,
                                                               op1=ALU.mult), [r_score, r_th], [r_MB])
                        yield

                def gen_att(s):
                    j = 4 * t + s
                    qb = qTb[s % 3]
                    r_qb = r_qTb[s % 3]
                    pacc = [next_ps(pin=True), next_ps(pin=True)]
                    units = [(kc, gi) for kc in range(j + 1) for gi in range(2)]
                    LA = 2

                    def emit_logits(i):
                        kc, gi = units[i]
                        pl, rpl = next_ps()
                        sc.op("pe", lambda e: e.matmul(pl[:, :], lhsT=kikT[0:64, kc * 128:(kc + 1) * 128],
                                                       rhs=qb[0:64, 0, 4 * gi:4 * gi + 4, :], start=True,
                                                       stop=False), [r_kikc[kc], r_qb], [rpl])
                        sc.op("pe", lambda e: e.matmul(pl[:, :], lhsT=MBf[:, kc * 128:(kc + 1) * 128],
                                                       rhs=E4_bf[:, :], start=False, stop=True),
                              [r_MB, r_E4], [rpl])
                        PT, r_PT = PTb[i % 3]
                        sc.op("act", lambda e: e.activation(out=PT, in_=pl[:, :], func=AF.Exp, scale=0.125),
                              [rpl], [r_PT])

                    def emit_pv(i):
                        kc, gi = units[i]
                        PT, r_PT = PTb[i % 3]
                        pa_, rpa_ = pacc[gi]
                        for hh in range(4):
                            sc.op("pe", lambda e: e.matmul(pa_[:, hh * 65:(hh + 1) * 65],
                                                           lhsT=PT[:, hh * 128:(hh + 1) * 128],
                                                           rhs=Vall[:, kc, :], start=(kc == 0 and hh == 0),
                                                           stop=(kc == j), skip_group_check=True),
                                  [r_PT, r_Vc[kc], r_V], [rpa_])

                    for i in range(min(LA, len(units))):
                        emit_logits(i)
                    for i in range(len(units)):
                        if i + LA < len(units):
                            emit_logits(i + LA)
                        emit_pv(i)
                    yield "TAIL"
                    for gi in range(2):
                        pa_, rpa_ = pacc[gi]
                        a3 = pa_[:, 0:260].rearrange("p (h d) -> p h d", h=4)
                        sc.op("dve", lambda e: e.reciprocal(out=rec8[:, 4 * gi:4 * gi + 4], in_=a3[:, :, 64]),
                              [rpa_], [r_rec8])
                        sc.op("dve", lambda e: e.tensor_tensor(
                            out=ob[:, gi * 256:(gi + 1) * 256].rearrange("p (h d) -> p h d", h=4), in0=a3[:, :, 0:64],
                            in1=rec8[:, 4 * gi:4 * gi + 4].unsqueeze(2).to_broadcast([128, 4, 64]), op=ALU.mult),
                            [rpa_, r_rec8], [r_ob])
                        unpin(pa_)
                    pt, rpt = next_ps()
                    for k4 in range(4):
                        sc.op("pe", lambda e: e.transpose(out=pt[:, k4 * 128:(k4 + 1) * 128],
                                                          in_=ob[:, k4 * 128:(k4 + 1) * 128], identity=ident_f),
                              [r_ob, r_consts], [rpt])
                    sc.op("act", lambda e: e.copy(out=yB[:, :, s * 128:(s + 1) * 128],
                                                  in_=pt[:, :].rearrange("p (k t) -> p k t", k=4)), [rpt], [r_yB])
                    yield

                def drain(gen):
                    for _ in gen:
                        pass

                def interleave_n(items):
                    prog = [0] * len(items)
                    done = [False] * len(items)
                    while not all(done):
                        best = None
                        for i_, (g_, ex_) in enumerate(items):
                            if done[i_]:
                                continue
                            r_ = prog[i_] / float(ex_)
                            if best is None or r_ < best[0]:
                                best = (r_, i_)
                        i_ = best[1]
                        try:
                            next(items[i_][0])
                            prog[i_] += 1
                        except StopIteration:
                            done[i_] = True

                def run_main(gen):
                    for v_ in gen:
                        if v_ == "TAIL":
                            break
                    return gen

                drain(gen_proj(0, 0))
                drain(gen_proj(0, 1))
                pend = None
                for s in range(4):
                    drain(gen_idx(s, 0))
                    if pend is not None:
                        drain(pend)
                        pend = None
                    if s < 3:
                        drain(gen_proj(s + 1, 0))
                    drain(gen_idx(s, 1))
                    if s > 0:
                        pend = run_main(gen_att(s - 1))
                    drain(gen_idx(s, 2))
                    if s < 3:
                        drain(gen_proj(s + 1, 1))
                if pend is not None:
                    drain(pend)
                drain(gen_att(3))
        for l in range(n_layers):
            src_d = x_d if l == 0 else x1_d
            dst_d = y_d if l == n_layers - 1 else x1_d

            if "M" in BR:
                sc.dma("sp", xt[:, 0:2, :], mem_d[:, :].rearrange("(s p) d -> p s d", p=128), [], [r_xt], "xt")
                rmsnorm_to_F(xt, r_xt, 2, "memn", memT, [r_memT, r_yA, r_yB])
                if l == 0:
                    mkT, r_mkT = T("mkT", [128, 4, 256], BF16)
                    mv, r_mv = T("mv", [128, 2, 512], BF16)
                    gkq, r_gkq = T("gkq", [128, 1])
                sc.op("dve", lambda e: e.tensor_tensor(out=gkq[:, :], in0=cc(f"mqn{l}"), in1=cc(f"mkn{l}"),
                                                       op=ALU.mult), [r_consts], [r_gkq])
                sc.op("dve", lambda e: e.tensor_scalar(out=gkq[:, :], in0=gkq[:, :], scalar1=128.0 ** -0.5,
                                                       scalar2=None, op0=ALU.mult), [r_gkq], [r_gkq])

                def cons_mk(j, p, rp):
                    sq, r_sq = g(0)
                    sc.op("act", lambda e: e.activation(out=sq[:, 0:256], in_=p[:, 0:256], func=AF.Square),
                          [rp], [r_sq])
                    sqb, r_sqb = MQB
                    sc.op("pool", lambda e: e.tensor_copy(out=sqb[:, 0:256], in_=sq[:, 0:256]), [r_sq], [r_sqb])
                    p2, rp2 = next_ps()
                    sc.op("pe", lambda e: e.matmul(p2[:, 0:256], lhsT=ones_bf[:, :], rhs=sqb[:, 0:256], start=True,
                                                   stop=True), [r_ones, r_sqb], [rp2])
                    rs, r_rs = g(1)
                    rsqrt(rs[:, 0:256], r_rs, p2[:, 0:256], rp2, 1.0 / 128)
                    sc.op("dve", lambda e: e.tensor_tensor(out=rs[:, 0:256], in0=rs[:, 0:256], in1=p[:, 0:256],
                                                           op=ALU.mult), [r_rs, rp], [r_rs])
                    sc.op("dve", lambda e: e.tensor_scalar(out=mkT[:, j, :], in0=rs[:, 0:256], scalar1=gkq[:, 0:1],
                                                           scalar2=None, op0=ALU.mult), [r_rs, r_gkq], [r_mkT])

                if l == 0:
                    MQB = PTb[2] if "B" in BR else T("mqb", [128, 512], BF16)
                proj_F(wb_kv[l], 0, 4, cons_mk, memT, r_memT, ntok=256)
                wt, rw = load_w(wb_kv[l], 512, 512, 8)
                for mc in range(2):
                    p, rp = next_ps()
                    for kc in range(8):
                        sc.op("pe", lambda e: e.matmul(p[:, :], lhsT=memT[:, kc, mc * 128:(mc + 1) * 128],
                                                       rhs=wt[:, kc, :], start=(kc == 0), stop=(kc == 7)),
                              [rw, r_memT], [rp])
                    sc.op("act", lambda e: e.copy(out=mv[:, mc, :], in_=p[:, :]), [rp], [r_mv])
                for r_ in (r_yA, r_yB):
                    r_.r.update(r_memT.r)

            if "C" in BR:
                sc.op("pool", lambda e: e.memset(Sst[:, :, :], 0.0), [], [r_S])
                if l == 0:
                    OSQ_ = rbb[0] if "B" in BR else T("osq", [128, 512], BF16)

            for t in range(NT):
                t0 = t * 512
                sc.dma("sp", xt[:, :, :], src_d[t0:t0 + 512, :].rearrange("(s p) d -> p s d", p=128),
                       [], [r_xt], "xt")
                rmsnorm_to_F(xt, r_xt, 4, f"nmix{l}", hT, r_hT)
                branches = []

                if "A" in BR:
                    U = [g(i) for i in range(4)]
                    UT = [G[i][0] for i in range(4)]
                    for c in range(4):
                        if t == 0:
                            sc.op("pool", lambda e: e.memset(UT[c][:, 0:2], 0.0), [], [G[c][1]])
                        else:
                            sc.op("pool", lambda e: e.tensor_copy(out=UT[c][:, 0:2], in_=uhalo[:, c, :]),
                                  [r_uhalo], [G[c][1]])

                    def cons_ax(j, p, rp):
                        sc.op("act", lambda e: e.copy(out=UT[j][:, 2:514], in_=p[:, :]), [rp], [G[j][1]])

                    proj_F(wb_in[l], C_AX, 4, cons_ax, hT, r_hT)

                    def cons_ac(j, p, rp):
                        sc.op("dve", lambda e: e.tensor_tensor(out=UT[j][:, 2:514], in0=UT[j][:, 2:514], in1=p[:, :],
                                                               op=ALU.mult), [rp, G[j][1]], [G[j][1]])
                        sc.op("pool", lambda e: e.tensor_copy(out=uhalo[:, j, :], in_=UT[j][:, 512:514]),
                              [G[j][1]], [r_uhalo])

                    proj_F(wb_in[l], C_AC, 4, cons_ac, hT, r_hT)

                    def cons_ab(j, p, rp):
                        cw = lay[f"convw{l}"][0]
                        ctmp, r_ctmp = g(4 + (j % 2))
                        sc.op("pool", lambda e: e.tensor_scalar(out=ctmp, in0=UT[j][:, 0:512],
                                                                scalar1=consts[:, cw + j:cw + j + 1], scalar2=None,
                                                                op0=ALU.mult), [G[j][1], r_consts], [r_ctmp])
                        sc.op("dve", lambda e: e.scalar_tensor_tensor(out=ctmp, in0=UT[j][:, 1:513],
                                                                      scalar=consts[:, cw + 4 + j:cw + 5 + j],
                                                                      in1=ctmp, op0=ALU.mult, op1=ALU.add),
                              [G[j][1], r_consts, r_ctmp], [r_ctmp])
                        sc.op("dve", lambda e: e.scalar_tensor_tensor(out=ctmp, in0=UT[j][:, 2:514],
                                                                      scalar=consts[:, cw + 8 + j:cw + 9 + j],
                                                                      in1=ctmp, op0=ALU.mult, op1=ALU.add),
                              [G[j][1], r_consts, r_ctmp], [r_ctmp])
                        sc.op("dve", lambda e: e.tensor_tensor(out=yA[:, j, :], in0=ctmp, in1=p[:, :], op=ALU.mult),
                              [rp, r_ctmp], [r_yA])

                    proj_F(wb_in[l], C_AB, 4, cons_ab, hT, r_hT)
                    branches.append((0, yA, r_yA))

                if "M" in BR:
                    def cons_mq(j, p, rp):
                        sq, r_sq = MQB
                        sc.op("act", lambda e: e.activation(out=sq, in_=p[:, :], func=AF.Square), [rp], [r_sq])
                        p2, rp2 = next_ps()
                        sc.op("pe", lambda e: e.matmul(p2[:, :], lhsT=ones_bf[:, :], rhs=sq, start=True, stop=True),
                              [r_ones, r_sq], [rp2])
                        rs, r_rs = g(4)
                        rsqrt(rs, r_rs, p2[:, :], rp2, 1.0 / 128)
                        mqn, r_mqn = MQN
                        sc.op("dve", lambda e: e.tensor_tensor(out=mqn, in0=rs, in1=p[:, :], op=ALU.mult),
                              [r_rs, rp], [r_mqn])
                        pts = []
                        for mc in range(2):
                            pl, rpl = next_ps()
                            sc.op("pe", lambda e: e.matmul(pl[:, :], lhsT=mkT[:, j, mc * 128:(mc + 1) * 128], rhs=mqn,
                                                           start=True, stop=True), [r_mkT, r_mqn], [rpl])
                            PT, r_PT = MPT[mc]
                            sc.op("act", lambda e: e.activation(out=PT, in_=pl[:, :], func=AF.Exp), [rpl], [r_PT])
                            pts.append((PT, r_PT))
                        py, rpy = next_ps()
                        psm, rpsm = next_ps()
                        for mc in range(2):
                            PT, r_PT = pts[mc]
                            sc.op("pe", lambda e: e.matmul(py[:, :], lhsT=mv[:, mc, j * 128:(j + 1) * 128], rhs=PT,
                                                           start=(mc == 0), stop=(mc == 1)), [r_mv, r_PT], [rpy])
                        for mc in range(2):
                            PT, r_PT = pts[mc]
                            sc.op("pe", lambda e: e.matmul(psm[:, :], lhsT=ones_bf[:, :], rhs=PT,
                                                           start=(mc == 0), stop=(mc == 1)), [r_ones, r_PT], [rpsm])
                        rc, r_rc = g(5)
                        sc.op("dve", lambda e: e.reciprocal(out=rc, in_=psm[:, :]), [rpsm], [r_rc])
                        sc.op("dve", lambda e: e.tensor_tensor(out=yM[:, j, :], in0=rc, in1=py[:, :], op=ALU.mult),
                              [r_rc, rpy], [r_yM])

                    if l == 0 and t == 0:
                        if "B" in BR:
                            MQN = (cjunk[:, 0:512], r_cjunk)
                            MPT = [PTb[0], PTb[1]]
                        else:
                            MQN = T("mqn", [128, 512], BF16)
                            MPT = [T(f"mpt{i}", [128, 512], BF16) for i in range(2)]
                    proj_F(wb_in[l], C_MQ, 4, cons_mq, hT, r_hT)
                    branches.append((3, yM, r_yM))

                if "C" in BR:
                    def cons_gg(j, p, rp):
                        sc.op("act", lambda e: e.activation(out=sgg[:, j, :], in_=p[:, :], func=AF.Silu), [rp], [r_sgg])

                    proj_F(wb_in[l], C_GG, 4, cons_gg, hT, r_hT)
                    for s in range(4):
                        tk = slice(s * 128, (s + 1) * 128)

                        def tproj(c0):
                            wt, rw = load_w(wb_in[l], c0, 512, 8)
                            p, rp = next_ps()
                            for kc in range(8):
                                sc.op("pe", lambda e: e.matmul(p[:, :], lhsT=hT[:, kc, tk], rhs=wt[:, kc, :],
                                                               start=(kc == 0), stop=(kc == 7)), [rw, r_hT], [rp])
                            return p, rp

                        p, rp = tproj(C_GQ)
                        sc.op("act", lambda e: e.activation(out=qTl[:, :], in_=p[:, :], func=AF.Silu), [rp], [r_qTl])
                        p, rp = tproj(C_GI)
                        sc.op("act", lambda e: e.copy(out=vT_[:, :], in_=p[:, :]), [rp], [r_vT])
                        p, rp = tproj(C_GF)
                        sc.op("act", lambda e: e.activation(out=lfT[:, :], in_=p[:, :], func=AF.Sigmoid),
                              [rp], [r_lfT])
                        if l == 1:
                            sc.op("dve", lambda e: e.tensor_tensor(out=kTl[:, :], in0=lfT[:, :], in1=lb1[:, :],
                                                                   op=ALU.mult), [r_lfT, r_lb1], [r_kTl])
                            sc.op("dve", lambda e: e.tensor_tensor(out=lfT[:, :], in0=lfT[:, :], in1=kTl[:, :],
                                                                   op=ALU.subtract), [r_lfT, r_kTl], [r_lfT])
                            sc.op("dve", lambda e: e.tensor_tensor(out=lfT[:, :], in0=lfT[:, :], in1=lb1[:, :],
                                                                   op=ALU.add), [r_lfT, r_lb1], [r_lfT])
                        sc.op("dve", lambda e: e.tensor_scalar(out=kTl[:, :], in0=lfT[:, :], scalar1=-1.0, scalar2=1.0,
                                                               op0=ALU.mult, op1=ALU.add), [r_lfT], [r_kTl])
                        sc.op("act", lambda e: e.activation(out=lfT[:, :], in_=lfT[:, :], func=AF.Ln), [r_lfT], [r_lfT])
                        for h in range(4):
                            hc = slice(h * 128, (h + 1) * 128)
                            pc, rpc = next_ps()
                            sc.op("pe", lambda e: e.matmul(pc[:, 0:128], lhsT=lfT[:, hc], rhs=cc("M1"), start=True,
                                                           stop=True), [r_lfT, r_consts], [rpc])
                            sc.op("pe", lambda e: e.matmul(pc[:, 128:256], lhsT=lfT[:, hc], rhs=cc("Ublk"), start=True,
                                                           stop=True), [r_lfT, r_consts], [rpc])
                            sc.op("pe", lambda e: e.matmul(pc[:, 256:384], lhsT=cc("SL"), rhs=lfT[:, hc], start=True,
                                                           stop=True), [r_lfT, r_consts], [rpc])
                            sc.op("act", lambda e: e.activation(out=ex[:, 0:128], in_=pc[:, 0:128], func=AF.Exp),
                                  [rpc], [r_ex])
                            sc.op("act", lambda e: e.activation(out=ex[:, 128:256], in_=pc[:, 0:128], func=AF.Exp,
                                                                scale=-1.0), [rpc], [r_ex])
                            sc.op("act", lambda e: e.activation(out=ex[:, 256:512], in_=pc[:, 128:384], func=AF.Exp),
                                  [rpc], [r_ex])
                            ptq, rptq = next_ps()
                            sc.op("pe", lambda e: e.transpose(out=ptq[:, 0:128], in_=qTl[:, hc], identity=ident_f),
                                  [r_qTl, r_consts], [rptq])
                            sc.op("pe", lambda e: e.transpose(out=ptq[:, 128:256], in_=kTl[:, hc], identity=ident_f),
                                  [r_kTl, r_consts], [rptq])
                            qd, r_qd = QD[h]
                            sc.op("dve", lambda e: e.tensor_tensor(out=qd[:, 0:128], in0=ptq[:, 0:128],
                                                                   in1=ex[:, 0:128], op=ALU.mult),
                                  [rptq, r_ex], [r_qd])
                            sc.op("dve", lambda e: e.tensor_tensor(out=qd[:, 128:256], in0=ptq[:, 128:256],
                                                                   in1=ex[:, 128:256], op=ALU.mult),
                                  [rptq, r_ex], [r_qd])
                            sc.op("dve", lambda e: e.tensor_tensor(out=qd[:, 256:384], in0=ptq[:, 0:128],
                                                                   in1=ex[:, 256:384], op=ALU.mult),
                                  [rptq, r_ex], [r_qd])
                            sc.op("pool", lambda e: e.tensor_tensor(out=qd[:, 384:512], in0=kTl[:, hc],
                                                                    in1=ex[:, 384:512], op=ALU.mult),
                                  [r_kTl, r_ex], [r_qd])
                            sc.op("pool", lambda e: e.tensor_copy(out=dl[:, h, 0:1], in_=ex[:, 256 + 63:256 + 64]),
                                  [r_ex], [r_dl])
                            sc.op("pool", lambda e: e.tensor_copy(out=dl[:, h, 1:2], in_=ex[:, 256 + 127:256 + 128]),
                                  [r_ex], [r_dl])
                        po, rpo = next_ps(pin=True)
                        for c in range(2):
                            rows = slice(64 * c, 64 * c + 64)
                            cs = slice(64 * c, 64 * c + 64)
                            pa, rpa = next_ps()
                            pd, rpd = next_ps()
                            for h in range(4):
                                qd, r_qd = QD[h]
                                hc = slice(h * 128, (h + 1) * 128)
                                sc.op("pe", lambda e: e.matmul(pa[rows, h * 64:(h + 1) * 64],
                                                               lhsT=qd[:, 128 + 64 * c:128 + 64 * c + 64],
                                                               rhs=qd[:, 64 * c:64 * c + 64], start=True, stop=True),
                                      [r_qd], [rpa])
                                sc.op("pe", lambda e: e.matmul(po[:, h * 128 + 64 * c:h * 128 + 64 * c + 64],
                                                               lhsT=Sst[:, h, :],
                                                               rhs=qd[:, 256 + 64 * c:256 + 64 * c + 64],
                                                               start=(c == 0 and h == 0), stop=False,
                                                               skip_group_check=True), [r_S, r_qd], [rpo])
                            sc.op("dve", lambda e: e.tensor_tensor(
                                out=Am[rows, :, :], in0=pa[rows, 0:256].rearrange("p (h t) -> p h t", h=4),
                                in1=cc("triu")[rows, :].unsqueeze(1).to_broadcast([64, 4, 64]), op=ALU.mult),
                                [rpa, r_consts], [r_Am])
                            for h in range(4):
                                qd, r_qd = QD[h]
                                hc = slice(h * 128, (h + 1) * 128)
                                sc.op("pe", lambda e: e.matmul(po[:, h * 128 + 64 * c:h * 128 + 64 * c + 64],
                                                               lhsT=vT_[rows, hc], rhs=Am[rows, h, :],
                                                               start=False, stop=True, skip_group_check=True),
                                      [r_vT, r_Am], [rpo])
                                sc.op("pe", lambda e: e.matmul(pd[:, hc], lhsT=qd[rows, 384:512], rhs=vT_[rows, hc],
                                                               start=True, stop=True), [r_qd, r_vT], [rpd])
                            for h in range(4):
                                hc = slice(h * 128, (h + 1) * 128)
                                sc.op("dve", lambda e: e.scalar_tensor_tensor(out=Sst[:, h, :], in0=Sst[:, h, :],
                                                                              scalar=dl[:, h, c:c + 1], in1=pd[:, hc],
                                                                              op0=ALU.mult, op1=ALU.add),
                                      [r_S, r_dl, rpd], [r_S])
                        sqb, r_sqb = OSQ_
                        sc.op("act", lambda e: e.activation(out=sqb, in_=po[:, :], func=AF.Square), [rpo], [r_sqb])
                        p2, rp2 = next_ps()
                        sc.op("pe", lambda e: e.matmul(p2[:, :], lhsT=ones_bf[:, :], rhs=sqb, start=True, stop=True),
                              [r_ones, r_sqb], [rp2])
                        rs, r_rs = g(0)
                        rsqrt(rs, r_rs, p2[:, :], rp2, 1.0 / 128)
                        sc.op("dve", lambda e: e.tensor_tensor(out=rs, in0=rs, in1=po[:, :], op=ALU.mult),
                              [r_rs, rpo], [r_rs])
                        sc.op("dve", lambda e: e.scalar_tensor_tensor(
                            out=yC[:, :, tk], in0=rs.rearrange("p (h t) -> p h t", h=4), scalar=cc(f"hgn{l}"),
                            in1=sgg[:, :, tk], op0=ALU.mult, op1=ALU.mult), [r_rs, r_consts, r_sgg], [r_yC])
                        unpin(po)
                    branches.append((2, yC, r_yC))

                if "B" in BR:
                    dsa_tile(l, t)
                    branches.append((1, yB, r_yB))

                branches.sort(key=lambda b: b[0])
                gsig, r_gsig = g(4)
                gl, r_gl = g(5)
                for mg in range(2):
                    for bi, (n, yb, ryb) in enumerate(branches):
                        wt, rw = load_w(wb_in[l], C_GATE + n * D + mg * 512, 512, 8)
                        wl, rwl = load_w(wb_lift[l], mg * 512, 512, 4, krow0=n * 512)
                        last_n = (bi == len(branches) - 1)
                        for mm in range(4):
                            m = mg * 4 + mm
                            macc, r_macc = g(mm)
                            pg, rpg = next_ps()
                            for kc in range(8):
                                sc.op("pe", lambda e: e.matmul(pg[:, :], lhsT=wt[:, kc, mm * 128:(mm + 1) * 128],
                                                               rhs=hT[:, kc, :], start=(kc == 0), stop=(kc == 7)),
                                      [rw, r_hT], [rpg])
                            pl, rpl = next_ps()
                            for kc in range(4):
                                sc.op("pe", lambda e: e.matmul(pl[:, :], lhsT=wl[:, kc, mm * 128:(mm + 1) * 128],
                                                               rhs=yb[:, kc, :], start=(kc == 0), stop=(kc == 3)),
                                      [rwl, ryb], [rpl])
                            sc.op("act", lambda e: e.activation(out=gsig, in_=pg[:, :], func=AF.Sigmoid),
                                  [rpg], [r_gsig])
                            dst, rdst = (mergedT[:, m, :], r_merged) if last_n else (macc, r_macc)
                            if bi == 0:
                                sc.op("dve", lambda e: e.tensor_tensor(out=dst, in0=gsig, in1=pl[:, :], op=ALU.mult),
                                      [r_gsig, rpl], [rdst])
                            else:
                                sc.op("dve", lambda e: e.tensor_tensor(out=gl, in0=gsig, in1=pl[:, :], op=ALU.mult),
                                      [r_gsig, rpl], [r_gl])
                                sc.op("pool", lambda e: e.tensor_tensor(out=dst, in0=gl, in1=macc, op=ALU.add),
                                      [r_gl, r_macc], [rdst])

                for half in range(2):
                    wt, rw = load_w(wb_out[l], half * 512, 512, 8)
                    for s in range(4):
                        p, rp = next_ps()
                        for kc in range(8):
                            sc.op("pe", lambda e: e.matmul(p[:, :], lhsT=mergedT[:, kc, s * 128:(s + 1) * 128],
                                                           rhs=wt[:, kc, :], start=(kc == 0), stop=(kc == 7)),
                                  [rw, r_merged], [rp])
                        sc.op("dve", lambda e: e.tensor_tensor(out=xt[:, s, half * 512:(half + 1) * 512],
                                                               in0=xt[:, s, half * 512:(half + 1) * 512], in1=p[:, :],
                                                               op=ALU.add), [rp, r_xt], [r_xt])

                rmsnorm_to_F(xt, r_xt, 4, f"nffn{l}", hT, r_hT)
                allres = [r_aT, r_yA, r_yB, r_yC, r_yM, r_merged]

                def cons_gate(j, p, rp):
                    sc.op("act", lambda e: e.activation(out=aT[:, j, :], in_=p[:, :], func=AF.Silu), [rp], allres)

                proj_F(wb_up[l], 0, 22, cons_gate, hT, r_hT)

                def cons_up(j, p, rp):
                    sc.op("dve", lambda e: e.tensor_tensor(out=aT[:, j, :], in0=aT[:, j, :], in1=p[:, :], op=ALU.mult),
                          [rp, r_aT], allres)

                proj_F(wb_up[l], FFN, 22, cons_up, hT, r_hT)

                for half in range(2):
                    pss = [next_ps() for _ in range(4)]
                    for gq in range(3):
                        nk = 8 if gq < 2 else 6
                        wt, rw = load_w(wb_dn[l], half * 512, 512, nk, krow0=gq * 1024)
                        for s in range(4):
                            p, rp = pss[s]
                            for kc in range(nk):
                                fc = gq * 8 + kc
                                sc.op("pe", lambda e: e.matmul(p[:, :], lhsT=aT[:, fc, s * 128:(s + 1) * 128],
                                                               rhs=wt[:, kc, :], start=(fc == 0), stop=(fc == 21)),
                                      [rw] + allres, [rp])
                    for s in range(4):
                        p, rp = pss[s]
                        sc.op("dve", lambda e: e.tensor_tensor(out=xt[:, s, half * 512:(half + 1) * 512],
                                                               in0=xt[:, s, half * 512:(half + 1) * 512], in1=p[:, :],
                                                               op=ALU.add), [rp, r_xt], [r_xt])
                sc.dma("sp", dst_d[t0:t0 + 512, :].rearrange("(s p) d -> p s d", p=128), xt[:, :, :],
                       [r_xt], [], "xo")
            sc.wait_tok("sp", (sc.dsem["xo"][0], sc.dsem["xo"][1], None, "d_xo"))
        sc.wait_tok("sp", (sc.dsem["xo"][0], sc.dsem["xo"][1], None, "d_xo"))
        print("ops:", sc.cnt, "waits:", sc.nwait, "sbuf bytes/partition:", sbtot[0])
    return nc


def host_consts(inp):
    lay, NCONST = _const_layout()
    c = np.zeros((128, NCONST), np.float32)

    def put(name, arr):
        o, w = lay[name]
        c[:, o:o + w] = arr

    p = np.arange(128)
    put("ident", np.eye(128, dtype=np.float32))
    same = (p[:, None] // 64) == (p[None, :] // 64)
    U = (same & (p[:, None] <= p[None, :])).astype(np.float32)
    Rb = (same & ((p[:, None] % 64) <= 31)).astype(np.float32)
    SL = (same & (p[:, None] > p[None, :])).astype(np.float32)
    put("Ublk", U)
    put("M1", U - Rb)
    put("SL", SL)
    put("triu", ((p[:, None] % 64) <= np.arange(64)[None, :]).astype(np.float32))
    put("CB", np.where(p[None, :] <= p[:, None], 0.0, NEG).astype(np.float32))
    put("pow2", np.tile((0.5 ** np.arange(32, dtype=np.float64)).astype(np.float32)[None, :], (128, 1)))
    invf = 1.0 / (500000.0 ** (np.arange(0, 16, 2, dtype=np.float32) / 16.0))
    put("invf", np.tile(invf.astype(np.float32)[None, :], (128, 1)))
    put("memn", np.asarray(inp["mem_norm"]).reshape(8, 128).T)
    for l in range(2):
        put(f"nmix{l}", np.asarray(inp["norm_mix"][l]).reshape(8, 128).T)
        put(f"nffn{l}", np.asarray(inp["norm_ffn"][l]).reshape(8, 128).T)
        cw = np.asarray(inp["conv_w"][l])
        put(f"convw{l}", np.concatenate([cw[k].reshape(4, 128).T for k in range(3)], axis=1))
        put(f"hgn{l}", np.asarray(inp["hgrn_out_norm"][l]).reshape(128, 1))
        put(f"mqn{l}", np.asarray(inp["mem_q_norm"][l]).reshape(128, 1))
        put(f"mkn{l}", np.asarray(inp["mem_k_norm"][l]).reshape(128, 1))
        put(f"dqn{l}", np.tile(np.asarray(inp["dsa_q_norm"][l])[None, :], (128, 1)))
        put(f"dkn{l}", np.tile(np.asarray(inp["dsa_k_norm"][l])[None, :], (128, 1)))
    lbr = np.asarray(inp["hgrn_lower_bounds"], dtype=np.float32)
    cbig = np.concatenate([np.tile(np.eye(128, dtype=np.float32), (1, 4)), np.tile(lbr[0][None, :], (128, 1)),
                           np.tile(lbr[1][None, :], (128, 1))], axis=1).astype(np.float32)
    return c, cbig


def make_in_maps(inp, S, nb):
    consts, cbig = host_consts(inp)
    f = lambda a: np.ascontiguousarray(np.asarray(a, dtype=np.float32))
    shared = dict(
        consts=consts,
        cbig=cbig,
        w_in=f(inp["w_in"]),
        mem_w_kv=f(inp["mem_w_kv"]),
        w_lift=f(inp["w_lift"]).reshape(2, 2048, D),
        w_out=f(inp["w_out"]),
        ffn_w_up=f(inp["ffn_w_up"]),
        ffn_w_down=f(inp["ffn_w_down"]),
    )
    maps = []
    for b in range(nb):
        m = dict(shared)
        m["x"] = f(inp["x"][b, :S])
        m["mem"] = f(inp["mem"][b])
        m["pos"] = np.ascontiguousarray(np.asarray(inp["positions"][b, :S], dtype=np.int32).reshape(S // 128, 128).T)
        maps.append(m)
    return maps


def kernel(**inputs):
    S = inputs["x"].shape[1]
    B = inputs["x"].shape[0]
    nc = build(S)
    maps = make_in_maps(inputs, S, B)
    maps = maps + maps
    res = run_bass_kernel_spmd(nc, maps, core_ids=list(range(8)))
    out = np.stack([res.results[b]["y"] for b in range(B)], axis=0)
    return out.astype(np.float32)
```

```python
import numpy as np
from contextlib import ExitStack
import concourse.bass as bass
import concourse.mybir as mybir
from concourse.bass_utils import run_bass_kernel_spmd

F32 = mybir.dt.float32
BF16 = mybir.dt.bfloat16
I32 = mybir.dt.int32
ALU = mybir.AluOpType
AF = mybir.ActivationFunctionType
AX = mybir.AxisListType

D = 1024
IN_COLS = 9416
FFN = 2816
MEMT = 256
EPS = 1e-6
STAGE = 99
NOCAST = False
C_AX, C_AB, C_AC = 0, 512, 1024
C_DQ, C_DK, C_DV = 1536, 2048, 2112
C_IQ, C_IK, C_IW = 2176, 2688, 2752
C_GQ, C_GF, C_GI, C_GG = 2760, 3272, 3784, 4296
C_MQ = 4808
C_GATE = 5320


class Res:
    __slots__ = ("name", "w", "r", "excl")

    def __init__(self, name, excl=False):
        self.name = name
        self.w = None
        self.r = {}
        self.excl = excl


class Sched:
    def __init__(self, nc, stack):
        self.nc = nc
        self.eng = dict(pe=nc.tensor, act=nc.scalar, dve=nc.vector, pool=nc.gpsimd, sp=nc.sync)
        self.stack = stack
        self.prog = {e: stack.enter_context(nc.semaphore("prog_" + e)) for e in self.eng}
        self.cnt = {e: 0 for e in self.eng}
        self.waited = {e: {} for e in self.eng}
        self.dsem = {}
        self.nwait = 0

    def dma_sem(self, key):
        if key not in self.dsem:
            self.dsem[key] = [self.stack.enter_context(self.nc.semaphore("d_" + key)), 0]
        return key

    def _wait(self, e, tok):
        sem, val, src, key = tok
        if src == e and e == "pe":
            return
        if self.waited[e].get(key, 0) >= val:
            return
        self.eng[e].wait_ge(sem, val)
        self.waited[e][key] = val
        self.nwait += 1

    def _deps(self, e, reads, writes):
        for r in reads:
            if r.w is not None:
                self._wait(e, r.w)
            if r.excl:
                for tok in r.r.values():
                    if tok[2] != e:
                        self._wait(e, tok)
        for w in writes:
            if w.w is not None:
                self._wait(e, w.w)
            for tok in w.r.values():
                self._wait(e, tok)

    def _mark(self, tok, reads, writes):
        key = tok[3]
        for r in reads:
            r.r[key] = tok
        for w in writes:
            w.w = tok
            w.r = {}

    def op(self, e, fn, reads=(), writes=()):
        self._deps(e, reads, writes)
        ins = fn(self.eng[e])
        self.cnt[e] += 1
        ins.then_inc(self.prog[e], 1)
        tok = (self.prog[e], self.cnt[e], e, "prog_" + e)
        self._mark(tok, reads, writes)
        return tok

    def dma(self, q, out, in_, reads, writes, key, **kw):
        self.dma_sem(key)
        self._deps(q, reads, writes)
        ent = self.dsem[key]
        ent[1] += 16
        self.eng[q].dma_start(out=out, in_=in_, **kw).then_inc(ent[0], 16)
        tok = (ent[0], ent[1], None, "d_" + key)
        self._mark(tok, reads, writes)
        return tok

    def wait_tok(self, e, tok):
        self._wait(e, tok)


BRANCHES = "ABCM"
NEG = -1.0e9
MBNEG = -30000.0
NBIS = 16
DVE_FRAC = 1.0
PIPELINE = True


def _const_layout():
    lay = {}
    off = 0

    def add(name, w):
        nonlocal off
        lay[name] = (off, w)
        off += w

    add("ident", 128)
    add("Ublk", 128)
    add("M1", 128)
    add("SL", 128)
    add("triu", 64)
    add("CB", 128)
    add("pow2", 32)
    add("invf", 8)
    add("memn", 8)
    for l in range(2):
        add(f"nmix{l}", 8)
        add(f"nffn{l}", 8)
        add(f"convw{l}", 12)
        add(f"hgn{l}", 1)
        add(f"mqn{l}", 1)
        add(f"mkn{l}", 1)
        add(f"dqn{l}", 64)
        add(f"dkn{l}", 64)
    return lay, off


def build(S, n_layers=2, dbg=False):
    nc = bass.Bass("TRN2", target_bir_lowering=False)
    NT = S // 512
    NQ = S // 128
    lay, NCONST = _const_layout()
    BR = BRANCHES

    def din(name, shape, dt=F32):
        return nc.dram_tensor(name, list(shape), dt, kind="ExternalInput").ap()

    x_d = din("x", [S, D])
    mem_d = din("mem", [MEMT, D])
    pos_d = din("pos", [128, NQ], I32)
    consts_d = din("consts", [128, NCONST])
    cbig_d = din("cbig", [128, 1536])
    w_in_d = din("w_in", [2, D, IN_COLS])
    w_kv_d = din("mem_w_kv", [2, D, D])
    w_lift_d = din("w_lift", [2, 4 * 512, D])
    w_out_d = din("w_out", [2, D, D])
    w_up_d = din("ffn_w_up", [2, D, 2 * FFN])
    w_dn_d = din("ffn_w_down", [2, FFN, D])
    y_d = nc.dram_tensor("y", [S, D], F32, kind="ExternalOutput").ap()

    def dscr(name, shape, dt):
        return nc.dram_tensor(name, list(shape), dt, kind="Internal").ap()

    wb_in = dscr("wb_in", [2, D, IN_COLS], BF16)
    wb_kv = dscr("wb_kv", [2, D, D], BF16)
    wb_lift = dscr("wb_lift", [2, 2048, D], BF16)
    wb_out = dscr("wb_out", [2, D, D], BF16)
    wb_up = dscr("wb_up", [2, D, 2 * FFN], BF16)
    wb_dn = dscr("wb_dn", [2, FFN, D], BF16)
    x1_d = dscr("x1", [S, D], F32)

    stack = ExitStack()
    with stack:
        sc = Sched(nc, stack)

        sbtot = [0]

        def sb(name, shape, dt=F32):
            n_ = 1
            for d_ in shape[1:]:
                n_ *= d_
            sbtot[0] += n_ * (2 if dt == BF16 else 4)
            return nc.alloc_sbuf_tensor("sb_" + name, list(shape), dt)

        def T(name, shape, dt=F32):
            t_ = sb(name, shape, dt)
            return t_[tuple(slice(None) for _ in shape)], Res(name)

        consts, r_consts = T("consts", [128, NCONST])
        sc.dma("sp", consts[:, :], consts_d[:, :], [], [r_consts], "const")

        def cc(name, j=0, w=None):
            o, ww = lay[name]
            if w is None:
                w = ww - j
            return consts[:, o + j:o + j + w]

        ident_bf, r_identbf = T("ident_bf", [128, 128], BF16)
        sc.op("dve", lambda e: e.tensor_copy(out=ident_bf[:, :], in_=cc("ident")), [r_consts], [r_identbf])
        ones_bf, r_ones = T("ones_bf", [128, 128], BF16)
        sc.op("pool", lambda e: e.memset(ones_bf[:, :], 1.0), [], [r_ones])
        E4_bf, r_E4 = T("E4_bf", [128, 512], BF16)
        epsc, r_epsc = T("epsc", [128, 1])
        sc.op("pool", lambda e: e.memset(epsc[:, :], EPS), [], [r_epsc])
        negpi, r_negpi = T("negpi", [128, 1])
        sc.op("pool", lambda e: e.memset(negpi[:, :], -float(np.pi)), [], [r_negpi])

        def rsqrt(out, r_out, in_, r_in, scale):
            np_ = out.shape[0]
            sc.op("act", lambda e: e.activation(out=out, in_=in_, func=AF.Sqrt, bias=epsc[0:np_, :],
                                                scale=scale), [r_in, r_epsc], [r_out])
            sc.op("dve", lambda e: e.reciprocal(out=out, in_=out), [r_out], [r_out])

        CW = 2048
        cast_toks = []
        with nc.sbuf_tensor("stg_all", [128, 4, CW], F32) as stg_all, \
                nc.sbuf_tensor("stgb_all", [128, 4, CW], BF16) as stgb_all:
            r_stg = [Res(f"stg{i}") for i in range(4)]
            r_stgb = [Res(f"stgb{i}") for i in range(4)]
            ci = [0]
            cast_eng = ["dve", "act", "dve", "act"]

            def cast_matrix(src, dst, rows, cols):
                for r0 in range(0, rows, 128):
                    for c0 in range(0, cols, CW):
                        w = min(CW, cols - c0)
                        k = ci[0] % 4
                        sc.dma("sp", stg_all[:, k, 0:w], src[r0:r0 + 128, c0:c0 + w], [], [r_stg[k]], f"stg{k}")
                        e = cast_eng[k]
                        if e == "act":
                            sc.op("act", lambda en: en.copy(out=stgb_all[:, k, 0:w], in_=stg_all[:, k, 0:w]),
                                  [r_stg[k]], [r_stgb[k]])
                        else:
                            sc.op(e, lambda en: en.tensor_copy(out=stgb_all[:, k, 0:w], in_=stg_all[:, k, 0:w]),
                                  [r_stg[k]], [r_stgb[k]])
                        t = sc.dma("pool", dst[r0:r0 + 128, c0:c0 + w], stgb_all[:, k, 0:w], [r_stgb[k]], [],
                                   f"stgb{k}")
                        cast_toks.append(t)
                        ci[0] += 1

            for l in range(0 if NOCAST else n_layers):
                cast_matrix(w_in_d[l], wb_in[l], D, IN_COLS)
                cast_matrix(w_kv_d[l], wb_kv[l], D, D)
                cast_matrix(w_lift_d[l], wb_lift[l], 2048, D)
                cast_matrix(w_out_d[l], wb_out[l], D, D)
                cast_matrix(w_up_d[l], wb_up[l], D, 2 * FFN)
                cast_matrix(w_dn_d[l], wb_dn[l], FFN, D)
            last = {}
            for t in cast_toks:
                last[t[3]] = t
            for t in last.values():
                for e in ("sp", "act", "dve", "pool", "pe"):
                    sc.wait_tok(e, t)

        NPS = 8
        ps = [nc.alloc_psum_tensor(f"ps{i}", [128, 512], F32) for i in range(NPS)]
        r_ps = [Res(f"ps{i}", excl=True) for i in range(NPS)]
        psi = [0]
        pti = [0]

        pinned = set()

        def next_ps(pin=False):
            while True:
                i = psi[0] % NPS
                psi[0] += 1
                if i not in pinned:
                    break
            if pin:
                pinned.add(i)
            return ps[i], r_ps[i]

        def unpin(p):
            pinned.discard(ps.index(p))

        def next_pst():
            p_, rp_ = next_ps()
            return p_[:, :].bitcast(BF16)[:, 0:512], rp_

        NW = 2
        wslot = [sb(f"wslot{i}", [128, 8, 512], BF16) for i in range(NW)]
        r_wslot = [Res(f"wslot{i}") for i in range(NW)]
        wi = [0]

        def load_w(src, c0, ncols, nk, krow0=0):
            i = wi[0] % NW
            wi[0] += 1
            v = src[krow0:krow0 + nk * 128, c0:c0 + ncols].rearrange("(kc p) c -> p kc c", p=128)
            sc.dma("sp", wslot[i][:, 0:nk, 0:ncols], v, [], [r_wslot[i]], f"wslot{i}")
            return wslot[i], r_wslot[i]

        xt, r_xt = T("xt", [128, 4, D])
        hT, r_hT = T("hT", [128, 8, 512], BF16)
        ssq, r_ssq = T("ssq", [128, 4])
        rstd, r_rstd = T("rstd", [128, 4])
        aT, r_aT = T("aT", [128, 24, 512], BF16)
        yA, yB, yC, yM = aT[:, 0:4, :], aT[:, 4:8, :], aT[:, 8:12, :], aT[:, 12:16, :]
        r_yA, r_yB, r_yC, r_yM = Res("yA"), Res("yB"), Res("yC"), Res("yM")
        mergedT = aT[:, 16:24, :]
        r_merged = Res("merged")
        xn = aT[:, 16:24, :].rearrange("p (s a) c -> p s (a c)", a=2)
        r_xn = r_merged
        uhalo, r_uhalo = T("uhalo", [128, 4, 2])
        G = [T(f"g{i}", [128, 514]) for i in range(6)]

        def g(i):
            return G[i][0][:, 0:512], G[i][1]

        memT = aT[:, 0:8, 0:256]
        r_memT = Res("memT")
        for i_, c0_ in ((0, 0), (1, 512), (2, 1024)):
            sc.dma("sp", G[i_][0][:, 0:512], cbig_d[:, c0_:c0_ + 512], [], [G[i_][1]], f"cbig{i_}")
        sc.op("dve", lambda e: e.tensor_copy(out=E4_bf[:, :], in_=G[0][0][:, 0:512]), [G[0][1]], [r_E4])

        def rmsnorm_to_F(src, r_src, nsub, gname, dst, r_dst):
            junk, r_junk = g(3)
            for s in range(nsub):
                for hf in range(2):
                    sc.op("act", lambda e: e.activation(out=junk, in_=src[:, s, hf * 512:(hf + 1) * 512],
                                                        func=AF.Square, accum_out=ssq2[:, 2 * s + hf:2 * s + hf + 1]),
                          [r_src], [r_junk, r_ssq2])
            sc.op("dve", lambda e: e.tensor_reduce(out=ssq[:, 0:nsub],
                                                   in_=ssq2[:, 0:2 * nsub].rearrange("p (s two) -> p s two", two=2),
                                                   axis=AX.X, op=ALU.add), [r_ssq2], [r_ssq])
            rsqrt(rstd[:, 0:nsub], r_rstd, ssq[:, 0:nsub], r_ssq, 1.0 / D)
            for s in range(nsub):
                if s % 2 == 0:
                    sc.op("dve", lambda e: e.tensor_scalar(out=xn[:, s, :], in0=src[:, s, :],
                                                           scalar1=rstd[:, s:s + 1], scalar2=None, op0=ALU.mult),
                          [r_src, r_rstd], [r_xn])
                else:
                    sc.op("act", lambda e: e.activation(out=xn[:, s, :], in_=src[:, s, :], func=AF.Copy,
                                                        scale=rstd[:, s:s + 1]), [r_src, r_rstd], [r_xn])
            for kc in range(8):
                pt, rpt = next_pst()
                for s in range(nsub):
                    sc.op("pe", lambda e: e.transpose(out=pt[:, s * 128:(s + 1) * 128],
                                                      in_=xn[:, s, kc * 128:(kc + 1) * 128], identity=ident_bf[:, :]),
                          [r_xn, r_identbf], [rpt])
                sc.op("act", lambda e: e.activation(out=dst[:, kc, 0:nsub * 128], in_=pt[:, 0:nsub * 128],
                                                    func=AF.Copy, scale=cc(gname, kc, 1)),
                      [rpt, r_consts], r_dst if isinstance(r_dst, list) else [r_dst])

        ssq2, r_ssq2 = T("ssq2", [128, 8])

        def proj_F(wsrc, c0, n128, consume, rhs, r_rhs, nk=8, krow0=0, ntok=512):
            for g0 in range(0, n128, 4):
                gn = min(4, n128 - g0)
                wt, rw = load_w(wsrc, c0 + g0 * 128, gn * 128, nk, krow0)
                for j in range(gn):
                    p, rp = next_ps()
                    for kc in range(nk):
                        sc.op("pe", lambda e: e.matmul(p[:, 0:ntok], lhsT=wt[:, kc, j * 128:(j + 1) * 128],
                                                       rhs=rhs[:, kc, 0:ntok], start=(kc == 0), stop=(kc == nk - 1)),
                              [rw, r_rhs], [rp])
                    consume(g0 + j, p, rp)

        def proj_T(wsrc, c0, ncols, consume, subs=(0, 1, 2, 3)):
            wt, rw = load_w(wsrc, c0, ncols, 8)
            for s in subs:
                p, rp = next_ps()
                for kc in range(8):
                    sc.op("pe", lambda e: e.matmul(p[:, 0:ncols], lhsT=hT[:, kc, s * 128:(s + 1) * 128],
                                                   rhs=wt[:, kc, 0:ncols], start=(kc == 0), stop=(kc == 7)),
                          [rw, r_hT], [rp])
                consume(s, p, rp)

        ident_f = cc("ident")
        if "B" in BR:
            posi, r_posi = T("posi", [128, NQ], I32)
            sc.dma("sp", posi[:, :], pos_d[:, :], [], [r_posi], "posi")
            posf, r_posf = T("posf", [128, NQ])
            sc.op("dve", lambda e: e.tensor_copy(out=posf[:, :], in_=posi[:, :]), [r_posi], [r_posf])
            cosT, r_cos = T("cosT", [128, NQ, 8])
            sinT, r_sin = T("sinT", [128, NQ, 8])
            ang, r_ang = G[3][0][:, 0:NQ * 8].rearrange("p (n j) -> p n j", j=8), G[3][1]
            sc.op("dve", lambda e: e.tensor_tensor(out=ang[:, :, :],
                                                   in0=posf[:, :].unsqueeze(2).to_broadcast([128, NQ, 8]),
                                                   in1=cc("invf").unsqueeze(1).to_broadcast([128, NQ, 8]),
                                                   op=ALU.mult), [r_posf, r_consts], [r_ang])
            TWO_PI = float(2 * np.pi)
            MAGIC = 12582912.0
            nrd, r_nrd = G[4][0][:, 0:NQ * 8].rearrange("p (n j) -> p n j", j=8), G[4][1]
            for (dst, rdst, shift) in ((sinT, r_sin, 0.0), (cosT, r_cos, 0.25)):
                sc.op("dve", lambda e: e.tensor_scalar(out=dst[:, :, :], in0=ang[:, :, :], scalar1=1.0 / TWO_PI,
                                                       scalar2=shift, op0=ALU.mult, op1=ALU.add), [r_ang], [rdst])
                sc.op("dve", lambda e: e.tensor_scalar(out=nrd[:, :, :], in0=dst[:, :, :], scalar1=MAGIC,
                                                       scalar2=None, op0=ALU.add), [rdst], [r_nrd])
                sc.op("dve", lambda e: e.tensor_scalar(out=nrd[:, :, :], in0=nrd[:, :, :], scalar1=MAGIC,
                                                       scalar2=None, op0=ALU.subtract), [r_nrd], [r_nrd])
                sc.op("dve", lambda e: e.tensor_tensor(out=dst[:, :, :], in0=dst[:, :, :], in1=nrd[:, :, :],
                                                       op=ALU.subtract), [rdst, r_nrd], [rdst])
                sc.op("dve", lambda e: e.tensor_scalar(out=dst[:, :, :], in0=dst[:, :, :], scalar1=-0.49999,
                                                       scalar2=0.49999, op0=ALU.max, op1=ALU.min), [rdst], [rdst])
                sc.op("act", lambda e: e.activation(out=dst[:, :, :], in_=dst[:, :, :], func=AF.Sin,
                                                    scale=TWO_PI), [rdst], [rdst])
            kikT, r_kik = T("kikT", [128, S], BF16)
            r_kikc = [Res(f"kik{c}") for c in range(NQ)]
            Vall, r_V = T("Vall", [128, NQ, 65], BF16)
            r_Vc = [Res(f"V{c}") for c in range(NQ)]
            sc.op("pool", lambda e: e.memset(Vall[:, :, 64:65], 1.0), [], [r_V])
            score, r_score = T("score", [128, S])
            qiq = [T(f"qiq{i}", [128, 1, 8, 128], BF16) for i in range(3)]
            iqTb = [q_[0] for q_ in qiq]
            r_iqTb = [Res(f"iqTb{i}") for i in range(3)]
            wq, r_wq = T("wq", [128, 4, 8])
            wabs, r_wabs = T("wabs", [128, 4, 8])
            wsgn, r_wsgn = T("wsgn", [128, 4, 8])
            rbb = [T(f"rbb{i}", [128, 512], BF16) for i in range(3)]
            Dm = [T(f"Dm{i}", [128, 8, 128], BF16) for i in range(2)]
            stg_iq, r_stgiq = T("stg_iq", [128, 8, 128])
            sc.op("pool", lambda e: e.memset(stg_iq[:, :, :], 0.0), [], [r_stgiq])
            PTb = [T(f"PT{i}", [128, 512], BF16) for i in range(3)]
            rt = [T(f"rt{i}", [128, 8, 8]) for i in range(4)]
            thr, r_thr = T("thr", [128, 1])
            thr0, r_thr0 = T("thr0", [128, 1])
            sc.op("pool", lambda e: e.memset(thr0[:, :], -1.0e8), [], [r_thr0])

        if "C" in BR:
            lb1, r_lb1 = T("lb1", [128, 512])
            sc.op("dve", lambda e: e.tensor_tensor(out=lb1[:, :], in0=G[2][0][:, 0:512], in1=G[1][0][:, 0:512],
                                                   op=ALU.subtract), [G[1][1], G[2][1]], [r_lb1])
            sc.op("act", lambda e: e.activation(out=lb1[:, :], in_=lb1[:, :], func=AF.Sigmoid), [r_lb1], [r_lb1])
            Sst, r_S = T("Sst", [128, 4, 128])
            sgg, r_sgg = aT[:, 16:20, :], r_merged
            ex, r_ex = T("ex", [128, 512])
            prod, r_prod = T("prod", [128, 512])
            Am, r_Am = T("Am", [128, 4, 64])
            vT_, r_vT = T("vTl", [128, 512])
            qTl, r_qTl = T("qTl", [128, 512])
            kTl, r_kTl = T("kTl", [128, 512])
            lfT, r_lfT = T("lfT", [128, 512])
            dl, r_dl = T("dl", [128, 4, 2])
            OSQ = None
            QD = [g(1 + h) for h in range(4)]


        if "B" in BR:
            RB = [g(3), g(4)]
            cjunk, r_cjunk = T("cjunk", [128, 1024], BF16)
            if DVE_FRAC < 1.0:
                ajunk, r_ajunk = T("ajunk", [128, 1024], BF16)
                asum, r_asum = T("asum", [128, 4])
            mid, r_mid = T("mid", [128, 1])
            cntp, _ = T("cntp", [128, 16])
            r_cntp = [Res(f"cntp{i}") for i in range(16)]
            r_cjh = [Res("cjh0"), Res("cjh1")]
            MBf, r_MB = T("MBf", [128, S], BF16)
            qTb = [q_[0] for q_ in qiq]
            r_qTb = [q_[1] for q_ in qiq]
            ob, r_ob = g(2)
            stg_k4, r_stgk4 = T("stg_k4", [128, 4, 128])
            sm, r_sm = T("sm", [128, 8])
            wk, r_wk = T("wk", [128, 32])
            sA, r_sA = T("sA", [128, 16])
            sB, r_sB = T("sB", [128, 16])
            sC, r_sC = T("sC", [128, 4])
            rec8, r_rec8 = T("rec8", [128, 8])

            def rope(v3, nh, tq, r_v):
                cb = cosT[:, tq, :].unsqueeze(1).to_broadcast([128, nh, 8])
                sb_ = sinT[:, tq, :].unsqueeze(1).to_broadcast([128, nh, 8])
                x1 = v3[:, :, 0:8]
                x2 = v3[:, :, 8:16]
                tt = [rt[i][0][:, 0:nh, :] for i in range(4)]
                rr = [rt[i][1] for i in range(4)]
                sc.op("pool", lambda e: e.tensor_tensor(out=tt[0], in0=x1, in1=cb, op=ALU.mult), [r_v, r_cos], [rr[0]])
                sc.op("pool", lambda e: e.tensor_tensor(out=tt[1], in0=x2, in1=sb_, op=ALU.mult), [r_v, r_sin], [rr[1]])
                sc.op("pool", lambda e: e.tensor_tensor(out=tt[2], in0=x1, in1=sb_, op=ALU.mult), [r_v, r_sin], [rr[2]])
                sc.op("pool", lambda e: e.tensor_tensor(out=tt[3], in0=x2, in1=cb, op=ALU.mult), [r_v, r_cos], [rr[3]])
                sc.op("pool", lambda e: e.tensor_tensor(out=x1, in0=tt[0], in1=tt[1], op=ALU.subtract),
                      [rr[0], rr[1]], [r_v])
                sc.op("pool", lambda e: e.tensor_tensor(out=x2, in0=tt[2], in1=tt[3], op=ALU.add),
                      [rr[2], rr[3]], [r_v])

            def dsa_tile(l, t):
                def cons_q(s, p, rp):
                    tq = 4 * t + s
                    sq, r_sq = g(0)
                    sc.op("act", lambda e: e.activation(out=sq, in_=p[:, :], func=AF.Square), [rp], [r_sq])
                    sc.op("dve", lambda e: e.tensor_reduce(out=sA[:, 0:8], in_=sq.rearrange("p (h d) -> p h d", h=8),
                                                           axis=AX.X, op=ALU.add), [r_sq], [r_sA])
                    rsqrt(sA[:, 8:16], r_sB, sA[:, 0:8], r_sA, 1.0 / 64)
                    qn, r_qn = g(1)
                    qn3 = qn.rearrange("p (h d) -> p h d", h=8)
                    sc.op("dve", lambda e: e.tensor_tensor(out=qn3, in0=p[:, :].rearrange("p (h d) -> p h d", h=8),
                                                           in1=sA[:, 8:16].unsqueeze(2).to_broadcast([128, 8, 64]),
                                                           op=ALU.mult), [rp, r_sB], [r_qn])
                    sc.op("pool", lambda e: e.tensor_tensor(out=qn3, in0=qn3,
                                                            in1=cc(f"dqn{l}").unsqueeze(1).to_broadcast([128, 8, 64]),
                                                            op=ALU.mult), [r_qn, r_consts], [r_qn])
                    rope(qn3, 8, tq, r_qn)

                def q_b(s):
                    qn, r_qn = g(1)
                    for half in range(2):
                        pt, rpt = next_ps()
                        for hh in range(4):
                            h = half * 4 + hh
                            sc.op("pe", lambda e: e.transpose(out=pt[0:64, hh * 128:(hh + 1) * 128],
                                                              in_=qn[:, h * 64:(h + 1) * 64], identity=ident_f),
                                  [r_qn, r_consts], [rpt])
                        sc.op("act", lambda e: e.copy(out=qTb[s % 3][0:64, 0, half * 4:(half + 1) * 4, :],
                                                      in_=pt[0:64, :].rearrange("p (h t) -> p h t", h=4)),
                              [rpt], [r_qTb[s % 3]])

                def cons_iq(s, p, rp):
                    tq = 4 * t + s
                    sc.op("act", lambda e: e.copy(out=stg_iq[:, :, 64:128],
                                                  in_=p[:, :].rearrange("p (h d) -> p h d", h=8)), [rp], [r_stgiq])
                    sc.op("pool", lambda e: e.tensor_tensor(
                        out=stg_iq[:, :, 64:128], in0=stg_iq[:, :, 64:128],
                        in1=wabs[:, s, :].unsqueeze(2).to_broadcast([128, 8, 64]), op=ALU.mult),
                        [r_stgiq, r_wabs], [r_stgiq])
                    rope(stg_iq[:, :, 64:128], 8, tq, r_stgiq)

                def iq_b(s):
                    for half in range(2):
                        pt, rpt = next_ps()
                        for hh in range(4):
                            h = half * 4 + hh
                            sc.op("pe", lambda e: e.transpose(out=pt[:, hh * 128:(hh + 1) * 128],
                                                              in_=stg_iq[:, h, :], identity=ident_f),
                                  [r_stgiq, r_consts], [rpt])
                        sc.op("act", lambda e: e.copy(out=iqTb[s % 3][64:128, 0, half * 4:(half + 1) * 4, :],
                                                      in_=pt[64:128, :].rearrange("p (h t) -> p h t", h=4)),
                              [rpt], [r_iqTb[s % 3]])

                def gen_proj(s, phase):
                    if phase == 0:
                        proj_T(wb_in[l], C_DQ, 512, cons_q, subs=(s,))
                        proj_T(wb_in[l], C_IQ, 512, cons_iq, subs=(s,))
                        return
                    q_b(s)
                    iq_b(s)
                    dm_, r_dm_ = Dm[s % 2]
                    for h in range(8):
                        sc.op("pool", lambda e: e.tensor_scalar(out=dm_[:, h, :], in0=ident_bf[:, :],
                                                                scalar1=wsgn[:, s, h:h + 1], scalar2=None,
                                                                op0=ALU.mult), [r_identbf, r_wsgn], [r_dm_])
                    return
                    yield

                def cons_kv(s, p, rp):
                    tq = 4 * t + s
                    jk, r_jk = g(2)
                    sc.op("act", lambda e: e.activation(out=jk[:, 0:64], in_=p[:, 0:64], func=AF.Square,
                                                        accum_out=sC[:, 0:1]), [rp], [r_jk, r_sC])
                    rsqrt(sC[:, 1:2], r_sC, sC[:, 0:1], r_sC, 1.0 / 64)
                    sc.op("dve", lambda e: e.tensor_scalar(out=stg_k4[:, s, 0:64], in0=p[:, 0:64], scalar1=sC[:, 1:2],
                                                           scalar2=None, op0=ALU.mult), [rp, r_sC], [r_stgk4])
                    sc.op("pool", lambda e: e.tensor_tensor(out=stg_k4[:, s, 0:64], in0=stg_k4[:, s, 0:64],
                                                            in1=cc(f"dkn{l}"), op=ALU.mult),
                          [r_stgk4, r_consts], [r_stgk4])
                    rope(stg_k4[:, s:s + 1, 0:64], 1, tq, r_stgk4)
                    sc.op("act", lambda e: e.copy(out=Vall[:, tq, 0:64], in_=p[:, 64:128]), [rp], [r_Vc[tq], r_V])

                def cons_ik(s, p, rp):
                    tq = 4 * t + s
                    sc.op("act", lambda e: e.copy(out=stg_k4[:, s, 64:128], in_=p[:, 0:64]), [rp], [r_stgk4])
                    rope(stg_k4[:, s:s + 1, 64:128], 1, tq, r_stgk4)
                    sc.op("dve", lambda e: e.tensor_scalar(out=wq[:, s, :], in0=p[:, 64:72],
                                                           scalar1=float(8 ** -0.5 * 64 ** -0.5), scalar2=None,
                                                           op0=ALU.mult), [rp], [r_wq])
                    sc.op("act", lambda e: e.activation(out=wsgn[:, s, :], in_=wq[:, s, :], func=AF.Sign),
                          [r_wq], [r_wsgn])
                    sc.op("dve", lambda e: e.tensor_tensor(out=wabs[:, s, :], in0=wq[:, s, :], in1=wsgn[:, s, :],
                                                           op=ALU.mult), [r_wq, r_wsgn], [r_wabs])
                    pt, rpt = next_ps()
                    sc.op("pe", lambda e: e.transpose(out=pt[:, 0:128], in_=stg_k4[:, s, :], identity=ident_f),
                          [r_stgk4, r_consts], [rpt])
                    sc.op("act", lambda e: e.copy(out=kikT[:, tq * 128:(tq + 1) * 128], in_=pt[:, 0:128]),
                          [rpt], [r_kikc[tq]])

                proj_T(wb_in[l], C_DK, 128, cons_kv)
                proj_T(wb_in[l], C_IK, 72, cons_ik)

                def gen_idx(s, phase):
                    j = 4 * t + s
                    L = 128 * (j + 1)
                    iqT_, r_iqT_ = iqTb[s % 3], r_iqTb[s % 3]
                    if phase == 0:
                        dm_, r_dm_ = Dm[s % 2]
                        for c in range(t + 1):
                            W = 512 if c < t else (s + 1) * 128
                            kres = [r_kikc[4 * c + i] for i in range(W // 128)]
                            psc, rpsc = next_ps(pin=True)

                            def emit_x(h):
                                px, rpx = next_ps()
                                sc.op("pe", lambda e: e.matmul(px[:, 0:W], lhsT=iqT_[64:128, 0, h, :],
                                                               rhs=kikT[64:128, c * 512:c * 512 + W], start=True,
                                                               stop=True), [r_iqT_] + kres, [rpx])
                                rb, r_rb = rbb[h % 3]
                                sc.op("act", lambda e: e.activation(out=rb[:, 0:W], in_=px[:, 0:W], func=AF.Relu),
                                      [rpx], [r_rb])

                            def emit_acc(h):
                                rb, r_rb = rbb[h % 3]
                                sc.op("pe", lambda e: e.matmul(psc[:, 0:W], lhsT=dm_[:, h, :], rhs=rb[:, 0:W],
                                                               start=(h == 0), stop=(h == 7)), [r_dm_, r_rb], [rpsc])

                            emit_x(0)
                            emit_x(1)
                            for h in range(8):
                                if h + 2 < 8:
                                    emit_x(h + 2)
                                emit_acc(h)
                            sc.op("dve", lambda e: e.tensor_copy(out=score[:, c * 512:c * 512 + W], in_=psc[:, 0:W]),
                                  [rpsc], [r_score])
                            unpin(psc)
                            yield
                        sc.op("dve", lambda e: e.tensor_tensor(out=score[:, j * 128:(j + 1) * 128],
                                                               in0=score[:, j * 128:(j + 1) * 128], in1=cc("CB"),
                                                               op=ALU.add), [r_score, r_consts], [r_score])
                        return
                    if j >= 2 and phase == 1:
                        sc.op("dve", lambda e: e.tensor_reduce(out=sm[:, 0:1], in_=score[:, 0:256], axis=AX.X,
                                                               op=ALU.min), [r_score], [r_sm])
                        sc.op("dve", lambda e: e.tensor_reduce(out=sm[:, 1:2], in_=score[:, 0:L], axis=AX.X,
                                                               op=ALU.max), [r_score], [r_sm])
                        sc.op("dve", lambda e: e.tensor_scalar(out=sm[:, 2:3], in0=sm[:, 1:2], scalar1=sm[:, 0:1],
                                                               scalar2=0.5, op0=ALU.subtract, op1=ALU.mult),
                              [r_sm], [r_sm])
                        sc.op("dve", lambda e: e.tensor_scalar(out=wk[:, :], in0=cc("pow2"), scalar1=sm[:, 2:3],
                                                               scalar2=None, op0=ALU.mult), [r_sm, r_consts], [r_wk])
                        sc.op("dve", lambda e: e.tensor_tensor(out=mid[:, :], in0=sm[:, 0:1], in1=sm[:, 2:3],
                                                               op=ALU.add), [r_sm], [r_mid])
                        Ld = L if DVE_FRAC >= 1.0 else min(L, max(256, int(round(L * DVE_FRAC / 256.0)) * 256))
                        nact = L - Ld
                        for k in range(NBIS):
                            if nact > 0:
                                for ci_, c0_ in enumerate(range(Ld, L, 1024)):
                                    w_ = min(1024, L - c0_)
                                    sc.op("act", lambda e: e.activation(
                                        out=ajunk[:, 0:w_], in_=score[:, c0_:c0_ + w_], func=AF.Sign,
                                        bias=mid[:, 0:1], scale=-1.0, accum_out=asum[:, ci_:ci_ + 1]),
                                        [r_score, r_mid], [r_ajunk, r_asum])
                                nac = ci_ + 1
                            ccol = 4
                            for ci_, c0_ in enumerate(range(0, Ld, 512)):
                                w_ = min(512, Ld - c0_)
                                hj_ = ci_ % 2
                                sc.op("dve", lambda e: e.tensor_scalar(
                                    out=cjunk[:, hj_ * 512:hj_ * 512 + w_], in0=score[:, c0_:c0_ + w_],
                                    scalar1=mid[:, 0:1], scalar2=None, op0=ALU.is_ge,
                                    op1=ALU.add, accum_out=cntp[:, ci_:ci_ + 1]),
                                    [r_score, r_mid], [r_cntp[ci_], r_cjh[hj_]] + ([r_cjunk] if hj_ == 0 else []))
                            ncnt_ = ci_ + 1
                            sc.op("dve", lambda e: e.tensor_reduce(out=sm[:, 4:5], in_=cntp[:, 0:ncnt_], axis=AX.X,
                                                                   op=ALU.add), r_cntp[0:ncnt_], [r_sm])
                            if nact > 0:
                                for a_ in range(nac):
                                    sc.op("dve", lambda e: e.scalar_tensor_tensor(
                                        out=sm[:, ccol:ccol + 1], in0=asum[:, a_:a_ + 1], scalar=-0.5,
                                        in1=sm[:, ccol:ccol + 1], op0=ALU.mult, op1=ALU.add),
                                        [r_asum, r_sm], [r_sm])
                            sc.op("dve", lambda e: e.tensor_scalar(out=sm[:, 5:6], in0=sm[:, ccol:ccol + 1],
                                                                   scalar1=256.0 - nact / 2.0,
                                                                   scalar2=0.5, op0=ALU.is_ge, op1=ALU.subtract),
                                  [r_sm], [r_sm])
                            sc.op("dve", lambda e: e.scalar_tensor_tensor(out=mid[:, :], in0=sm[:, 5:6],
                                                                          scalar=wk[:, k:k + 1], in1=mid[:, :],
                                                                          op0=ALU.mult, op1=ALU.add),
                                  [r_sm, r_wk, r_mid], [r_mid])
                            yield
                        sc.op("dve", lambda e: e.tensor_tensor(out=thr[:, :], in0=mid[:, :],
                                                               in1=wk[:, NBIS:NBIS + 1], op=ALU.subtract),
                              [r_mid, r_wk], [r_thr])
                        return
                    if phase == 1:
                        return
                    th, r_th = (thr, r_thr) if j >= 2 else (thr0, r_thr0)
                    for c0_ in range(0, L, 2048):
                        w_ = min(2048, L - c0_)
                        sc.op("dve", lambda e: e.tensor_scalar(out=MBf[:, c0_:c0_ + w_], in0=score[:, c0_:c0_ + w_],
                                                               scalar1=th[:, 0:1], scalar2=MBNEG, op0=ALU.is_lt,
                                                               op1=ALU.mult), [r_score, r_th], [r_MB])
                        yield

                def gen_att(s):
                    j = 4 * t + s
                    qb = qTb[s % 3]
                    r_qb = r_qTb[s % 3]
                    pacc = [next_ps(pin=True), next_ps(pin=True)]
                    units = [(kc, gi) for kc in range(j + 1) for gi in range(2)]
                    LA = 2

                    def emit_logits(i):
                        kc, gi = units[i]
                        pl, rpl = next_ps()
                        sc.op("pe", lambda e: e.matmul(pl[:, :], lhsT=kikT[0:64, kc * 128:(kc + 1) * 128],
                                                       rhs=qb[0:64, 0, 4 * gi:4 * gi + 4, :], start=True,
                                                       stop=False), [r_kikc[kc], r_qb], [rpl])
                        sc.op("pe", lambda e: e.matmul(pl[:, :], lhsT=MBf[:, kc * 128:(kc + 1) * 128],
                                                       rhs=E4_bf[:, :], start=False, stop=True),
                              [r_MB, r_E4], [rpl])
                        PT, r_PT = PTb[i % 3]
                        sc.op("act", lambda e: e.activation(out=PT, in_=pl[:, :], func=AF.Exp, scale=0.125),
                              [rpl], [r_PT])

                    def emit_pv(i):
                        kc, gi = units[i]
                        PT, r_PT = PTb[i % 3]
                        pa_, rpa_ = pacc[gi]
                        for hh in range(4):
                            sc.op("pe", lambda e: e.matmul(pa_[:, hh * 65:(hh + 1) * 65],
                                                           lhsT=PT[:, hh * 128:(hh + 1) * 128],
                                                           rhs=Vall[:, kc, :], start=(kc == 0 and hh == 0),
                                                           stop=(kc == j), skip_group_check=True),
                                  [r_PT, r_Vc[kc], r_V], [rpa_])

                    for i in range(min(LA, len(units))):
                        emit_logits(i)
                    for i in range(len(units)):
                        if i + LA < len(units):
                            emit_logits(i + LA)
                        emit_pv(i)
                    yield "TAIL"
                    for gi in range(2):
                        pa_, rpa_ = pacc[gi]
                        a3 = pa_[:, 0:260].rearrange("p (h d) -> p h d", h=4)
                        sc.op("dve", lambda e: e.reciprocal(out=rec8[:, 4 * gi:4 * gi + 4], in_=a3[:, :, 64]),
                              [rpa_], [r_rec8])
                        sc.op("dve", lambda e: e.tensor_tensor(
                            out=ob[:, gi * 256:(gi + 1) * 256].rearrange("p (h d) -> p h d", h=4), in0=a3[:, :, 0:64],
                            in1=rec8[:, 4 * gi:4 * gi + 4].unsqueeze(2).to_broadcast([128, 4, 64]), op=ALU.mult),
                            [rpa_, r_rec8], [r_ob])
                        unpin(pa_)
                    pt, rpt = next_ps()
                    for k4 in range(4):
                        sc.op("pe", lambda e: e.transpose(out=pt[:, k4 * 128:(k4 + 1) * 128],
                                                          in_=ob[:, k4 * 128:(k4 + 1) * 128], identity=ident_f),
                              [r_ob, r_consts], [rpt])
                    sc.op("act", lambda e: e.copy(out=yB[:, :, s * 128:(s + 1) * 128],
                                                  in_=pt[:, :].rearrange("p (k t) -> p k t", k=4)), [rpt], [r_yB])
                    yield

                def drain(gen):
                    for _ in gen:
                        pass

                def interleave_n(items):
                    prog = [0] * len(items)
                    done = [False] * len(items)
                    while not all(done):
                        best = None
                        for i_, (g_, ex_) in enumerate(items):
                            if done[i_]:
                                continue
                            r_ = prog[i_] / float(ex_)
                            if best is None or r_ < best[0]:
                                best = (r_, i_)
                        i_ = best[1]
                        try:
                            next(items[i_][0])
                            prog[i_] += 1
                        except StopIteration:
                            done[i_] = True

                def run_main(gen):
                    for v_ in gen:
                        if v_ == "TAIL":
                            break
                    return gen

                drain(gen_proj(0, 0))
                drain(gen_proj(0, 1))
                pend = None
                for s in range(4):
                    drain(gen_idx(s, 0))
                    if pend is not None:
                        drain(pend)
                        pend = None
                    if s < 3:
                        drain(gen_proj(s + 1, 0))
                    drain(gen_idx(s, 1))
                    if s > 0:
                        pend = run_main(gen_att(s - 1))
                    drain(gen_idx(s, 2))
                    if s < 3:
                        drain(gen_proj(s + 1, 1))
                if pend is not None:
                    drain(pend)
                drain(gen_att(3))
        for l in range(n_layers):
            src_d = x_d if l == 0 else x1_d
            dst_d = y_d if l == n_layers - 1 else x1_d

            if "M" in BR:
                sc.dma("sp", xt[:, 0:2, :], mem_d[:, :].rearrange("(s p) d -> p s d", p=128), [], [r_xt], "xt")
                rmsnorm_to_F(xt, r_xt, 2, "memn", memT, [r_memT, r_yA, r_yB])
                if l == 0:
                    mkT, r_mkT = T("mkT", [128, 4, 256], BF16)
                    mv, r_mv = T("mv", [128, 2, 512], BF16)
                    gkq, r_gkq = T("gkq", [128, 1])
                sc.op("dve", lambda e: e.tensor_tensor(out=gkq[:, :], in0=cc(f"mqn{l}"), in1=cc(f"mkn{l}"),
                                                       op=ALU.mult), [r_consts], [r_gkq])
                sc.op("dve", lambda e: e.tensor_scalar(out=gkq[:, :], in0=gkq[:, :], scalar1=128.0 ** -0.5,
                                                       scalar2=None, op0=ALU.mult), [r_gkq], [r_gkq])

                def cons_mk(j, p, rp):
                    sq, r_sq = g(0)
                    sc.op("act", lambda e: e.activation(out=sq[:, 0:256], in_=p[:, 0:256], func=AF.Square),
                          [rp], [r_sq])
                    sqb, r_sqb = MQB
                    sc.op("pool", lambda e: e.tensor_copy(out=sqb[:, 0:256], in_=sq[:, 0:256]), [r_sq], [r_sqb])
                    p2, rp2 = next_ps()
                    sc.op("pe", lambda e: e.matmul(p2[:, 0:256], lhsT=ones_bf[:, :], rhs=sqb[:, 0:256], start=True,
                                                   stop=True), [r_ones, r_sqb], [rp2])
                    rs, r_rs = g(1)
                    rsqrt(rs[:, 0:256], r_rs, p2[:, 0:256], rp2, 1.0 / 128)
                    sc.op("dve", lambda e: e.tensor_tensor(out=rs[:, 0:256], in0=rs[:, 0:256], in1=p[:, 0:256],
                                                           op=ALU.mult), [r_rs, rp], [r_rs])
                    sc.op("dve", lambda e: e.tensor_scalar(out=mkT[:, j, :], in0=rs[:, 0:256], scalar1=gkq[:, 0:1],
                                                           scalar2=None, op0=ALU.mult), [r_rs, r_gkq], [r_mkT])

                if l == 0:
                    MQB = PTb[2] if "B" in BR else T("mqb", [128, 512], BF16)
                proj_F(wb_kv[l], 0, 4, cons_mk, memT, r_memT, ntok=256)
                wt, rw = load_w(wb_kv[l], 512, 512, 8)
                for mc in range(2):
                    p, rp = next_ps()
                    for kc in range(8):
                        sc.op("pe", lambda e: e.matmul(p[:, :], lhsT=memT[:, kc, mc * 128:(mc + 1) * 128],
                                                       rhs=wt[:, kc, :], start=(kc == 0), stop=(kc == 7)),
                              [rw, r_memT], [rp])
                    sc.op("act", lambda e: e.copy(out=mv[:, mc, :], in_=p[:, :]), [rp], [r_mv])
                for r_ in (r_yA, r_yB):
                    r_.r.update(r_memT.r)

            if "C" in BR:
                sc.op("pool", lambda e: e.memset(Sst[:, :, :], 0.0), [], [r_S])
                if l == 0:
                    OSQ_ = rbb[0] if "B" in BR else T("osq", [128, 512], BF16)

            for t in range(NT):
                t0 = t * 512
                sc.dma("sp", xt[:, :, :], src_d[t0:t0 + 512, :].rearrange("(s p) d -> p s d", p=128),
                       [], [r_xt], "xt")
                rmsnorm_to_F(xt, r_xt, 4, f"nmix{l}", hT, r_hT)
                branches = []

                if "A" in BR:
                    U = [g(i) for i in range(4)]
                    UT = [G[i][0] for i in range(4)]
                    for c in range(4):
                        if t == 0:
                            sc.op("pool", lambda e: e.memset(UT[c][:, 0:2], 0.0), [], [G[c][1]])
                        else:
                            sc.op("pool", lambda e: e.tensor_copy(out=UT[c][:, 0:2], in_=uhalo[:, c, :]),
                                  [r_uhalo], [G[c][1]])

                    def cons_ax(j, p, rp):
                        sc.op("act", lambda e: e.copy(out=UT[j][:, 2:514], in_=p[:, :]), [rp], [G[j][1]])

                    proj_F(wb_in[l], C_AX, 4, cons_ax, hT, r_hT)

                    def cons_ac(j, p, rp):
                        sc.op("dve", lambda e: e.tensor_tensor(out=UT[j][:, 2:514], in0=UT[j][:, 2:514], in1=p[:, :],
                                                               op=ALU.mult), [rp, G[j][1]], [G[j][1]])
                        sc.op("pool", lambda e: e.tensor_copy(out=uhalo[:, j, :], in_=UT[j][:, 512:514]),
                              [G[j][1]], [r_uhalo])

                    proj_F(wb_in[l], C_AC, 4, cons_ac, hT, r_hT)

                    def cons_ab(j, p, rp):
                        cw = lay[f"convw{l}"][0]
                        ctmp, r_ctmp = g(4 + (j % 2))
                        sc.op("pool", lambda e: e.tensor_scalar(out=ctmp, in0=UT[j][:, 0:512],
                                                                scalar1=consts[:, cw + j:cw + j + 1], scalar2=None,
                                                                op0=ALU.mult), [G[j][1], r_consts], [r_ctmp])
                        sc.op("dve", lambda e: e.scalar_tensor_tensor(out=ctmp, in0=UT[j][:, 1:513],
                                                                      scalar=consts[:, cw + 4 + j:cw + 5 + j],
                                                                      in1=ctmp, op0=ALU.mult, op1=ALU.add),
                              [G[j][1], r_consts, r_ctmp], [r_ctmp])
                        sc.op("dve", lambda e: e.scalar_tensor_tensor(out=ctmp, in0=UT[j][:, 2:514],
                                                                      scalar=consts[:, cw + 8 + j:cw + 9 + j],
                                                                      in1=ctmp, op0=ALU.mult, op1=ALU.add),
                              [G[j][1], r_consts, r_ctmp], [r_ctmp])
                        sc.op("dve", lambda e: e.tensor_tensor(out=yA[:, j, :], in0=ctmp, in1=p[:, :], op=ALU.mult),
                              [rp, r_ctmp], [r_yA])

                    proj_F(wb_in[l], C_AB, 4, cons_ab, hT, r_hT)
                    branches.append((0, yA, r_yA))

                if "M" in BR:
                    def cons_mq(j, p, rp):
                        sq, r_sq = MQB
                        sc.op("act", lambda e: e.activation(out=sq, in_=p[:, :], func=AF.Square), [rp], [r_sq])
                        p2, rp2 = next_ps()
                        sc.op("pe", lambda e: e.matmul(p2[:, :], lhsT=ones_bf[:, :], rhs=sq, start=True, stop=True),
                              [r_ones, r_sq], [rp2])
                        rs, r_rs = g(4)
                        rsqrt(rs, r_rs, p2[:, :], rp2, 1.0 / 128)
                        mqn, r_mqn = MQN
                        sc.op("dve", lambda e: e.tensor_tensor(out=mqn, in0=rs, in1=p[:, :], op=ALU.mult),
                              [r_rs, rp], [r_mqn])
                        pts = []
                        for mc in range(2):
                            pl, rpl = next_ps()
                            sc.op("pe", lambda e: e.matmul(pl[:, :], lhsT=mkT[:, j, mc * 128:(mc + 1) * 128], rhs=mqn,
                                                           start=True, stop=True), [r_mkT, r_mqn], [rpl])
                            PT, r_PT = MPT[mc]
                            sc.op("act", lambda e: e.activation(out=PT, in_=pl[:, :], func=AF.Exp), [rpl], [r_PT])
                            pts.append((PT, r_PT))
                        py, rpy = next_ps()
                        psm, rpsm = next_ps()
                        for mc in range(2):
                            PT, r_PT = pts[mc]
                            sc.op("pe", lambda e: e.matmul(py[:, :], lhsT=mv[:, mc, j * 128:(j + 1) * 128], rhs=PT,
                                                           start=(mc == 0), stop=(mc == 1)), [r_mv, r_PT], [rpy])
                        for mc in range(2):
                            PT, r_PT = pts[mc]
                            sc.op("pe", lambda e: e.matmul(psm[:, :], lhsT=ones_bf[:, :], rhs=PT,
                                                           start=(mc == 0), stop=(mc == 1)), [r_ones, r_PT], [rpsm])
                        rc, r_rc = g(5)
                        sc.op("dve", lambda e: e.reciprocal(out=rc, in_=psm[:, :]), [rpsm], [r_rc])
                        sc.op("dve", lambda e: e.tensor_tensor(out=yM[:, j, :], in0=rc, in1=py[:, :], op=ALU.mult),
                              [r_rc, rpy], [r_yM])

                    if l == 0 and t == 0:
                        if "B" in BR:
                            MQN = (cjunk[:, 0:512], r_cjunk)
                            MPT = [PTb[0], PTb[1]]
                        else:
                            MQN = T("mqn", [128, 512], BF16)
                            MPT = [T(f"mpt{i}", [128, 512], BF16) for i in range(2)]
                    proj_F(wb_in[l], C_MQ, 4, cons_mq, hT, r_hT)
                    branches.append((3, yM, r_yM))

                if "C" in BR:
                    def cons_gg(j, p, rp):
                        sc.op("act", lambda e: e.activation(out=sgg[:, j, :], in_=p[:, :], func=AF.Silu), [rp], [r_sgg])

                    proj_F(wb_in[l], C_GG, 4, cons_gg, hT, r_hT)
                    for s in range(4):
                        tk = slice(s * 128, (s + 1) * 128)

                        def tproj(c0):
                            wt, rw = load_w(wb_in[l], c0, 512, 8)
                            p, rp = next_ps()
                            for kc in range(8):
                                sc.op("pe", lambda e: e.matmul(p[:, :], lhsT=hT[:, kc, tk], rhs=wt[:, kc, :],
                                                               start=(kc == 0), stop=(kc == 7)), [rw, r_hT], [rp])
                            return p, rp

                        p, rp = tproj(C_GQ)
                        sc.op("act", lambda e: e.activation(out=qTl[:, :], in_=p[:, :], func=AF.Silu), [rp], [r_qTl])
                        p, rp = tproj(C_GI)
                        sc.op("act", lambda e: e.copy(out=vT_[:, :], in_=p[:, :]), [rp], [r_vT])
                        p, rp = tproj(C_GF)
                        sc.op("act", lambda e: e.activation(out=lfT[:, :], in_=p[:, :], func=AF.Sigmoid),
                              [rp], [r_lfT])
                        if l == 1:
                            sc.op("dve", lambda e: e.tensor_tensor(out=kTl[:, :], in0=lfT[:, :], in1=lb1[:, :],
                                                                   op=ALU.mult), [r_lfT, r_lb1], [r_kTl])
                            sc.op("dve", lambda e: e.tensor_tensor(out=lfT[:, :], in0=lfT[:, :], in1=kTl[:, :],
                                                                   op=ALU.subtract), [r_lfT, r_kTl], [r_lfT])
                            sc.op("dve", lambda e: e.tensor_tensor(out=lfT[:, :], in0=lfT[:, :], in1=lb1[:, :],
                                                                   op=ALU.add), [r_lfT, r_lb1], [r_lfT])
                        sc.op("dve", lambda e: e.tensor_scalar(out=kTl[:, :], in0=lfT[:, :], scalar1=-1.0, scalar2=1.0,
                                                               op0=ALU.mult, op1=ALU.add), [r_lfT], [r_kTl])
                        sc.op("act", lambda e: e.activation(out=lfT[:, :], in_=lfT[:, :], func=AF.Ln), [r_lfT], [r_lfT])
                        for h in range(4):
                            hc = slice(h * 128, (h + 1) * 128)
                            pc, rpc = next_ps()
                            sc.op("pe", lambda e: e.matmul(pc[:, 0:128], lhsT=lfT[:, hc], rhs=cc("M1"), start=True,
                                                           stop=True), [r_lfT, r_consts], [rpc])
                            sc.op("pe", lambda e: e.matmul(pc[:, 128:256], lhsT=lfT[:, hc], rhs=cc("Ublk"), start=True,
                                                           stop=True), [r_lfT, r_consts], [rpc])
                            sc.op("pe", lambda e: e.matmul(pc[:, 256:384], lhsT=cc("SL"), rhs=lfT[:, hc], start=True,
                                                           stop=True), [r_lfT, r_consts], [rpc])
                            sc.op("act", lambda e: e.activation(out=ex[:, 0:128], in_=pc[:, 0:128], func=AF.Exp),
                                  [rpc], [r_ex])
                            sc.op("act", lambda e: e.activation(out=ex[:, 128:256], in_=pc[:, 0:128], func=AF.Exp,
                                                                scale=-1.0), [rpc], [r_ex])
                            sc.op("act", lambda e: e.activation(out=ex[:, 256:512], in_=pc[:, 128:384], func=AF.Exp),
                                  [rpc], [r_ex])
                            ptq, rptq = next_ps()
                            sc.op("pe", lambda e: e.transpose(out=ptq[:, 0:128], in_=qTl[:, hc], identity=ident_f),
                                  [r_qTl, r_consts], [rptq])
                            sc.op("pe", lambda e: e.transpose(out=ptq[:, 128:256], in_=kTl[:, hc], identity=ident_f),
                                  [r_kTl, r_consts], [rptq])
                            qd, r_qd = QD[h]
                            sc.op("dve", lambda e: e.tensor_tensor(out=qd[:, 0:128], in0=ptq[:, 0:128],
                                                                   in1=ex[:, 0:128], op=ALU.mult),
                                  [rptq, r_ex], [r_qd])
                            sc.op("dve", lambda e: e.tensor_tensor(out=qd[:, 128:256], in0=ptq[:, 128:256],
                                                                   in1=ex[:, 128:256], op=ALU.mult),
                                  [rptq, r_ex], [r_qd])
                            sc.op("dve", lambda e: e.tensor_tensor(out=qd[:, 256:384], in0=ptq[:, 0:128],
                                                                   in1=ex[:, 256:384], op=ALU.mult),
                                  [rptq, r_ex], [r_qd])
                            sc.op("pool", lambda e: e.tensor_tensor(out=qd[:, 384:512], in0=kTl[:, hc],
                                                                    in1=ex[:, 384:512], op=ALU.mult),
                                  [r_kTl, r_ex], [r_qd])
                            sc.op("pool", lambda e: e.tensor_copy(out=dl[:, h, 0:1], in_=ex[:, 256 + 63:256 + 64]),
                                  [r_ex], [r_dl])
                            sc.op("pool", lambda e: e.tensor_copy(out=dl[:, h, 1:2], in_=ex[:, 256 + 127:256 + 128]),
                                  [r_ex], [r_dl])
                        po, rpo = next_ps(pin=True)
                        for c in range(2):
                            rows = slice(64 * c, 64 * c + 64)
                            cs = slice(64 * c, 64 * c + 64)
                            pa, rpa = next_ps()
                            pd, rpd = next_ps()
                            for h in range(4):
                                qd, r_qd = QD[h]
                                hc = slice(h * 128, (h + 1) * 128)
                                sc.op("pe", lambda e: e.matmul(pa[rows, h * 64:(h + 1) * 64],
                                                               lhsT=qd[:, 128 + 64 * c:128 + 64 * c + 64],
                                                               rhs=qd[:, 64 * c:64 * c + 64], start=True, stop=True),
                                      [r_qd], [rpa])
                                sc.op("pe", lambda e: e.matmul(po[:, h * 128 + 64 * c:h * 128 + 64 * c + 64],
                                                               lhsT=Sst[:, h, :],
                                                               rhs=qd[:, 256 + 64 * c:256 + 64 * c + 64],
                                                               start=(c == 0 and h == 0), stop=False,
                                                               skip_group_check=True), [r_S, r_qd], [rpo])
                            sc.op("dve", lambda e: e.tensor_tensor(
                                out=Am[rows, :, :], in0=pa[rows, 0:256].rearrange("p (h t) -> p h t", h=4),
                                in1=cc("triu")[rows, :].unsqueeze(1).to_broadcast([64, 4, 64]), op=ALU.mult),
                                [rpa, r_consts], [r_Am])
                            for h in range(4):
                                qd, r_qd = QD[h]
                                hc = slice(h * 128, (h + 1) * 128)
                                sc.op("pe", lambda e: e.matmul(po[:, h * 128 + 64 * c:h * 128 + 64 * c + 64],
                                                               lhsT=vT_[rows, hc], rhs=Am[rows, h, :],
                                                               start=False, stop=True, skip_group_check=True),
                                      [r_vT, r_Am], [rpo])
                                sc.op("pe", lambda e: e.matmul(pd[:, hc], lhsT=qd[rows, 384:512], rhs=vT_[rows, hc],
                                                               start=True, stop=True), [r_qd, r_vT], [rpd])
                            for h in range(4):
                                hc = slice(h * 128, (h + 1) * 128)
                                sc.op("dve", lambda e: e.scalar_tensor_tensor(out=Sst[:, h, :], in0=Sst[:, h, :],
                                                                              scalar=dl[:, h, c:c + 1], in1=pd[:, hc],
                                                                              op0=ALU.mult, op1=ALU.add),
                                      [r_S, r_dl, rpd], [r_S])
                        sqb, r_sqb = OSQ_
                        sc.op("act", lambda e: e.activation(out=sqb, in_=po[:, :], func=AF.Square), [rpo], [r_sqb])
                        p2, rp2 = next_ps()
                        sc.op("pe", lambda e: e.matmul(p2[:, :], lhsT=ones_bf[:, :], rhs=sqb, start=True, stop=True),
                              [r_ones, r_sqb], [rp2])
                        rs, r_rs = g(0)
                        rsqrt(rs, r_rs, p2[:, :], rp2, 1.0 / 128)
                        sc.op("dve", lambda e: e.tensor_tensor(out=rs, in0=rs, in1=po[:, :], op=ALU.mult),
                              [r_rs, rpo], [r_rs])
                        sc.op("dve", lambda e: e.scalar_tensor_tensor(
                            out=yC[:, :, tk], in0=rs.rearrange("p (h t) -> p h t", h=4), scalar=cc(f"hgn{l}"),
                            in1=sgg[:, :, tk], op0=ALU.mult, op1=ALU.mult), [r_rs, r_consts, r_sgg], [r_yC])
                        unpin(po)
                    branches.append((2, yC, r_yC))

                if "B" in BR:
                    dsa_tile(l, t)
                    branches.append((1, yB, r_yB))

                branches.sort(key=lambda b: b[0])
                gsig, r_gsig = g(4)
                gl, r_gl = g(5)
                for mg in range(2):
                    for bi, (n, yb, ryb) in enumerate(branches):
                        wt, rw = load_w(wb_in[l], C_GATE + n * D + mg * 512, 512, 8)
                        wl, rwl = load_w(wb_lift[l], mg * 512, 512, 4, krow0=n * 512)
                        last_n = (bi == len(branches) - 1)
                        for mm in range(4):
                            m = mg * 4 + mm
                            macc, r_macc = g(mm)
                            pg, rpg = next_ps()
                            for kc in range(8):
                                sc.op("pe", lambda e: e.matmul(pg[:, :], lhsT=wt[:, kc, mm * 128:(mm + 1) * 128],
                                                               rhs=hT[:, kc, :], start=(kc == 0), stop=(kc == 7)),
                                      [rw, r_hT], [rpg])
                            pl, rpl = next_ps()
                            for kc in range(4):
                                sc.op("pe", lambda e: e.matmul(pl[:, :], lhsT=wl[:, kc, mm * 128:(mm + 1) * 128],
                                                               rhs=yb[:, kc, :], start=(kc == 0), stop=(kc == 3)),
                                      [rwl, ryb], [rpl])
                            sc.op("act", lambda e: e.activation(out=gsig, in_=pg[:, :], func=AF.Sigmoid),
                                  [rpg], [r_gsig])
                            dst, rdst = (mergedT[:, m, :], r_merged) if last_n else (macc, r_macc)
                            if bi == 0:
                                sc.op("dve", lambda e: e.tensor_tensor(out=dst, in0=gsig, in1=pl[:, :], op=ALU.mult),
                                      [r_gsig, rpl], [rdst])
                            else:
                                sc.op("dve", lambda e: e.tensor_tensor(out=gl, in0=gsig, in1=pl[:, :], op=ALU.mult),
                                      [r_gsig, rpl], [r_gl])
                                sc.op("pool", lambda e: e.tensor_tensor(out=dst, in0=gl, in1=macc, op=ALU.add),
                                      [r_gl, r_macc], [rdst])

                for half in range(2):
                    wt, rw = load_w(wb_out[l], half * 512, 512, 8)
                    for s in range(4):
                        p, rp = next_ps()
                        for kc in range(8):
                            sc.op("pe", lambda e: e.matmul(p[:, :], lhsT=mergedT[:, kc, s * 128:(s + 1) * 128],
                                                           rhs=wt[:, kc, :], start=(kc == 0), stop=(kc == 7)),
                                  [rw, r_merged], [rp])
                        sc.op("dve", lambda e: e.tensor_tensor(out=xt[:, s, half * 512:(half + 1) * 512],
                                                               in0=xt[:, s, half * 512:(half + 1) * 512], in1=p[:, :],
                                                               op=ALU.add), [rp, r_xt], [r_xt])

                rmsnorm_to_F(xt, r_xt, 4, f"nffn{l}", hT, r_hT)
                allres = [r_aT, r_yA, r_yB, r_yC, r_yM, r_merged]

                def cons_gate(j, p, rp):
                    sc.op("act", lambda e: e.activation(out=aT[:, j, :], in_=p[:, :], func=AF.Silu), [rp], allres)

                proj_F(wb_up[l], 0, 22, cons_gate, hT, r_hT)

                def cons_up(j, p, rp):
                    sc.op("dve", lambda e: e.tensor_tensor(out=aT[:, j, :], in0=aT[:, j, :], in1=p[:, :], op=ALU.mult),
                          [rp, r_aT], allres)

                proj_F(wb_up[l], FFN, 22, cons_up, hT, r_hT)

                for half in range(2):
                    pss = [next_ps() for _ in range(4)]
                    for gq in range(3):
                        nk = 8 if gq < 2 else 6
                        wt, rw = load_w(wb_dn[l], half * 512, 512, nk, krow0=gq * 1024)
                        for s in range(4):
                            p, rp = pss[s]
                            for kc in range(nk):
                                fc = gq * 8 + kc
                                sc.op("pe", lambda e: e.matmul(p[:, :], lhsT=aT[:, fc, s * 128:(s + 1) * 128],
                                                               rhs=wt[:, kc, :], start=(fc == 0), stop=(fc == 21)),
                                      [rw] + allres, [rp])
                    for s in range(4):
                        p, rp = pss[s]
                        sc.op("dve", lambda e: e.tensor_tensor(out=xt[:, s, half * 512:(half + 1) * 512],
                                                               in0=xt[:, s, half * 512:(half + 1) * 512], in1=p[:, :],
                                                               op=ALU.add), [rp, r_xt], [r_xt])
                sc.dma("sp", dst_d[t0:t0 + 512, :].rearrange("(s p) d -> p s d", p=128), xt[:, :, :],
                       [r_xt], [], "xo")
            sc.wait_tok("sp", (sc.dsem["xo"][0], sc.dsem["xo"][1], None, "d_xo"))
        sc.wait_tok("sp", (sc.dsem["xo"][0], sc.dsem["xo"][1], None, "d_xo"))
        print("ops:", sc.cnt, "waits:", sc.nwait, "sbuf bytes/partition:", sbtot[0])
    return nc


def host_consts(inp):
    lay, NCONST = _const_layout()
    c = np.zeros((128, NCONST), np.float32)

    def put(name, arr):
        o, w = lay[name]
        c[:, o:o + w] = arr

    p = np.arange(128)
    put("ident", np.eye(128, dtype=np.float32))
    same = (p[:, None] // 64) == (p[None, :] // 64)
    U = (same & (p[:, None] <= p[None, :])).astype(np.float32)
    Rb = (same & ((p[:, None] % 64) <= 31)).astype(np.float32)
    SL = (same & (p[:, None] > p[None, :])).astype(np.float32)
    put("Ublk", U)
    put("M1", U - Rb)
    put("SL", SL)
    put("triu", ((p[:, None] % 64) <= np.arange(64)[None, :]).astype(np.float32))
    put("CB", np.where(p[None, :] <= p[:, None], 0.0, NEG).astype(np.float32))
    put("pow2", np.tile((0.5 ** np.arange(32, dtype=np.float64)).astype(np.float32)[None, :], (128, 1)))
    invf = 1.0 / (500000.0 ** (np.arange(0, 16, 2, dtype=np.float32) / 16.0))
    put("invf", np.tile(invf.astype(np.float32)[None, :], (128, 1)))
    put("memn", np.asarray(inp["mem_norm"]).reshape(8, 128).T)
    for l in range(2):
        put(f"nmix{l}", np.asarray(inp["norm_mix"][l]).reshape(8, 128).T)
        put(f"nffn{l}", np.asarray(inp["norm_ffn"][l]).reshape(8, 128).T)
        cw = np.asarray(inp["conv_w"][l])
        put(f"convw{l}", np.concatenate([cw[k].reshape(4, 128).T for k in range(3)], axis=1))
        put(f"hgn{l}", np.asarray(inp["hgrn_out_norm"][l]).reshape(128, 1))
        put(f"mqn{l}", np.asarray(inp["mem_q_norm"][l]).reshape(128, 1))
        put(f"mkn{l}", np.asarray(inp["mem_k_norm"][l]).reshape(128, 1))
        put(f"dqn{l}", np.tile(np.asarray(inp["dsa_q_norm"][l])[None, :], (128, 1)))
        put(f"dkn{l}", np.tile(np.asarray(inp["dsa_k_norm"][l])[None, :], (128, 1)))
    lbr = np.asarray(inp["hgrn_lower_bounds"], dtype=np.float32)
    cbig = np.concatenate([np.tile(np.eye(128, dtype=np.float32), (1, 4)), np.tile(lbr[0][None, :], (128, 1)),
                           np.tile(lbr[1][None, :], (128, 1))], axis=1).astype(np.float32)
    return c, cbig


def make_in_maps(inp, S, nb):
    consts, cbig = host_consts(inp)
    f = lambda a: np.ascontiguousarray(np.asarray(a, dtype=np.float32))
    shared = dict(
        consts=consts,
        cbig=cbig,
        w_in=f(inp["w_in"]),
        mem_w_kv=f(inp["mem_w_kv"]),
        w_lift=f(inp["w_lift"]).reshape(2, 2048, D),
        w_out=f(inp["w_out"]),
        ffn_w_up=f(inp["ffn_w_up"]),
        ffn_w_down=f(inp["ffn_w_down"]),
    )
    maps = []
    for b in range(nb):
        m = dict(shared)
        m["x"] = f(inp["x"][b, :S])
        m["mem"] = f(inp["mem"][b])
        m["pos"] = np.ascontiguousarray(np.asarray(inp["positions"][b, :S], dtype=np.int32).reshape(S // 128, 128).T)
        maps.append(m)
    return maps


def kernel(**inputs):
    S = inputs["x"].shape[1]
    B = inputs["x"].shape[0]
    nc = build(S)
    maps = make_in_maps(inputs, S, B)
    maps = maps + maps
    res = run_bass_kernel_spmd(nc, maps, core_ids=list(range(8)))
    out = np.stack([res.results[b]["y"] for b in range(B)], axis=0)
    return out.astype(np.float32)
```

```python
import numpy as np
from contextlib import ExitStack
import concourse.bass as bass
import concourse.mybir as mybir
from concourse.bass_utils import run_bass_kernel_spmd

F32 = mybir.dt.float32
BF16 = mybir.dt.bfloat16
I32 = mybir.dt.int32
ALU = mybir.AluOpType
AF = mybir.ActivationFunctionType
AX = mybir.AxisListType

D = 1024
IN_COLS = 9416
FFN = 2816
MEMT = 256
EPS = 1e-6
STAGE = 99
NOCAST = False
C_AX, C_AB, C_AC = 0, 512, 1024
C_DQ, C_DK, C_DV = 1536, 2048, 2112
C_IQ, C_IK, C_IW = 2176, 2688, 2752
C_GQ, C_GF, C_GI, C_GG = 2760, 3272, 3784, 4296
C_MQ = 4808
C_GATE = 5320


class Res:
    __slots__ = ("name", "w", "r", "excl")

    def __init__(self, name, excl=False):
        self.name = name
        self.w = None
        self.r = {}
        self.excl = excl


class Sched:
    def __init__(self, nc, stack):
        self.nc = nc
        self.eng = dict(pe=nc.tensor, act=nc.scalar, dve=nc.vector, pool=nc.gpsimd, sp=nc.sync)
        self.stack = stack
        self.prog = {e: stack.enter_context(nc.semaphore("prog_" + e)) for e in self.eng}
        self.cnt = {e: 0 for e in self.eng}
        self.waited = {e: {} for e in self.eng}
        self.dsem = {}
        self.nwait = 0

    def dma_sem(self, key):
        if key not in self.dsem:
            self.dsem[key] = [self.stack.enter_context(self.nc.semaphore("d_" + key)), 0]
        return key

    def _wait(self, e, tok):
        sem, val, src, key = tok
        if src == e and e == "pe":
            return
        if self.waited[e].get(key, 0) >= val:
            return
        self.eng[e].wait_ge(sem, val)
        self.waited[e][key] = val
        self.nwait += 1

    def _deps(self, e, reads, writes):
        for r in reads:
            if r.w is not None:
                self._wait(e, r.w)
            if r.excl:
                for tok in r.r.values():
                    if tok[2] != e:
                        self._wait(e, tok)
        for w in writes:
            if w.w is not None:
                self._wait(e, w.w)
            for tok in w.r.values():
                self._wait(e, tok)

    def _mark(self, tok, reads, writes):
        key = tok[3]
        for r in reads:
            r.r[key] = tok
        for w in writes:
            w.w = tok
            w.r = {}

    def op(self, e, fn, reads=(), writes=()):
        self._deps(e, reads, writes)
        ins = fn(self.eng[e])
        self.cnt[e] += 1
        ins.then_inc(self.prog[e], 1)
        tok = (self.prog[e], self.cnt[e], e, "prog_" + e)
        self._mark(tok, reads, writes)
        return tok

    def dma(self, q, out, in_, reads, writes, key, **kw):
        self.dma_sem(key)
        self._deps(q, reads, writes)
        ent = self.dsem[key]
        ent[1] += 16
        self.eng[q].dma_start(out=out, in_=in_, **kw).then_inc(ent[0], 16)
        tok = (ent[0], ent[1], None, "d_" + key)
        self._mark(tok, reads, writes)
        return tok

    def wait_tok(self, e, tok):
        self._wait(e, tok)


BRANCHES = "ABCM"
NEG = -1.0e9
MBNEG = -30000.0
NBIS = 16
DVE_FRAC = 1.0
PIPELINE = True


def _const_layout():
    lay = {}
    off = 0

    def add(name, w):
        nonlocal off
        lay[name] = (off, w)
        off += w

    add("ident", 128)
    add("Ublk", 128)
    add("M1", 128)
    add("SL", 128)
    add("triu", 64)
    add("CB", 128)
    add("pow2", 32)
    add("invf", 8)
    add("memn", 8)
    for l in range(2):
        add(f"nmix{l}", 8)
        add(f"nffn{l}", 8)
        add(f"convw{l}", 12)
        add(f"hgn{l}", 1)
        add(f"mqn{l}", 1)
        add(f"mkn{l}", 1)
        add(f"dqn{l}", 64)
        add(f"dkn{l}", 64)
    return lay, off


def build(S, n_layers=2, dbg=False):
    nc = bass.Bass("TRN2", target_bir_lowering=False)
    NT = S // 512
    NQ = S // 128
    lay, NCONST = _const_layout()
    BR = BRANCHES

    def din(name, shape, dt=F32):
        return nc.dram_tensor(name, list(shape), dt, kind="ExternalInput").ap()

    x_d = din("x", [S, D])
    mem_d = din("mem", [MEMT, D])
    pos_d = din("pos", [128, NQ], I32)
    consts_d = din("consts", [128, NCONST])
    cbig_d = din("cbig", [128, 1536])
    w_in_d = din("w_in", [2, D, IN_COLS])
    w_kv_d = din("mem_w_kv", [2, D, D])
    w_lift_d = din("w_lift", [2, 4 * 512, D])
    w_out_d = din("w_out", [2, D, D])
    w_up_d = din("ffn_w_up", [2, D, 2 * FFN])
    w_dn_d = din("ffn_w_down", [2, FFN, D])
    y_d = nc.dram_tensor("y", [S, D], F32, kind="ExternalOutput").ap()

    def dscr(name, shape, dt):
        return nc.dram_tensor(name, list(shape), dt, kind="Internal").ap()

    wb_in = dscr("wb_in", [2, D, IN_COLS], BF16)
    wb_kv = dscr("wb_kv", [2, D, D], BF16)
    wb_lift = dscr("wb_lift", [2, 2048, D], BF16)
    wb_out = dscr("wb_out", [2, D, D], BF16)
    wb_up = dscr("wb_up", [2, D, 2 * FFN], BF16)
    wb_dn = dscr("wb_dn", [2, FFN, D], BF16)
    x1_d = dscr("x1", [S, D], F32)

    stack = ExitStack()
    with stack:
        sc = Sched(nc, stack)

        sbtot = [0]

        def sb(name, shape, dt=F32):
            n_ = 1
            for d_ in shape[1:]:
                n_ *= d_
            sbtot[0] += n_ * (2 if dt == BF16 else 4)
            return nc.alloc_sbuf_tensor("sb_" + name, list(shape), dt)

        def T(name, shape, dt=F32):
            t_ = sb(name, shape, dt)
            return t_[tuple(slice(None) for _ in shape)], Res(name)

        consts, r_consts = T("consts", [128, NCONST])
        sc.dma("sp", consts[:, :], consts_d[:, :], [], [r_consts], "const")

        def cc(name, j=0, w=None):
            o, ww = lay[name]
            if w is None:
                w = ww - j
            return consts[:, o + j:o + j + w]

        ident_bf, r_identbf = T("ident_bf", [128, 128], BF16)
        sc.op("dve", lambda e: e.tensor_copy(out=ident_bf[:, :], in_=cc("ident")), [r_consts], [r_identbf])
        ones_bf, r_ones = T("ones_bf", [128, 128], BF16)
        sc.op("pool", lambda e: e.memset(ones_bf[:, :], 1.0), [], [r_ones])
        E4_bf, r_E4 = T("E4_bf", [128, 512], BF16)
        epsc, r_epsc = T("epsc", [128, 1])
        sc.op("pool", lambda e: e.memset(epsc[:, :], EPS), [], [r_epsc])
        negpi, r_negpi = T("negpi", [128, 1])
        sc.op("pool", lambda e: e.memset(negpi[:, :], -float(np.pi)), [], [r_negpi])

        def rsqrt(out, r_out, in_, r_in, scale):
            np_ = out.shape[0]
            sc.op("act", lambda e: e.activation(out=out, in_=in_, func=AF.Sqrt, bias=epsc[0:np_, :],
                                                scale=scale), [r_in, r_epsc], [r_out])
            sc.op("dve", lambda e: e.reciprocal(out=out, in_=out), [r_out], [r_out])

        CW = 2048
        cast_toks = []
        with nc.sbuf_tensor("stg_all", [128, 4, CW], F32) as stg_all, \
                nc.sbuf_tensor("stgb_all", [128, 4, CW], BF16) as stgb_all:
            r_stg = [Res(f"stg{i}") for i in range(4)]
            r_stgb = [Res(f"stgb{i}") for i in range(4)]
            ci = [0]
            cast_eng = ["dve", "act", "dve", "act"]

            def cast_matrix(src, dst, rows, cols):
                for r0 in range(0, rows, 128):
                    for c0 in range(0, cols, CW):
                        w = min(CW, cols - c0)
                        k = ci[0] % 4
                        sc.dma("sp", stg_all[:, k, 0:w], src[r0:r0 + 128, c0:c0 + w], [], [r_stg[k]], f"stg{k}")
                        e = cast_eng[k]
                        if e == "act":
                            sc.op("act", lambda en: en.copy(out=stgb_all[:, k, 0:w], in_=stg_all[:, k, 0:w]),
                                  [r_stg[k]], [r_stgb[k]])
                        else:
                            sc.op(e, lambda en: en.tensor_copy(out=stgb_all[:, k, 0:w], in_=stg_all[:, k, 0:w]),
                                  [r_stg[k]], [r_stgb[k]])
                        t = sc.dma("pool", dst[r0:r0 + 128, c0:c0 + w], stgb_all[:, k, 0:w], [r_stgb[k]], [],
                                   f"stgb{k}")
                        cast_toks.append(t)
                        ci[0] += 1

            for l in range(0 if NOCAST else n_layers):
                cast_matrix(w_in_d[l], wb_in[l], D, IN_COLS)
                cast_matrix(w_kv_d[l], wb_kv[l], D, D)
                cast_matrix(w_lift_d[l], wb_lift[l], 2048, D)
                cast_matrix(w_out_d[l], wb_out[l], D, D)
                cast_matrix(w_up_d[l], wb_up[l], D, 2 * FFN)
                cast_matrix(w_dn_d[l], wb_dn[l], FFN, D)
            last = {}
            for t in cast_toks:
                last[t[3]] = t
            for t in last.values():
                for e in ("sp", "act", "dve", "pool", "pe"):
                    sc.wait_tok(e, t)

        NPS = 8
        ps = [nc.alloc_psum_tensor(f"ps{i}", [128, 512], F32) for i in range(NPS)]
        r_ps = [Res(f"ps{i}", excl=True) for i in range(NPS)]
        psi = [0]
        pti = [0]

        pinned = set()

        def next_ps(pin=False):
            while True:
                i = psi[0] % NPS
                psi[0] += 1
                if i not in pinned:
                    break
            if pin:
                pinned.add(i)
            return ps[i], r_ps[i]

        def unpin(p):
            pinned.discard(ps.index(p))

        def next_pst():
            p_, rp_ = next_ps()
            return p_[:, :].bitcast(BF16)[:, 0:512], rp_

        NW = 2
        wslot = [sb(f"wslot{i}", [128, 8, 512], BF16) for i in range(NW)]
        r_wslot = [Res(f"wslot{i}") for i in range(NW)]
        wi = [0]

        def load_w(src, c0, ncols, nk, krow0=0):
            i = wi[0] % NW
            wi[0] += 1
            v = src[krow0:krow0 + nk * 128, c0:c0 + ncols].rearrange("(kc p) c -> p kc c", p=128)
            sc.dma("sp", wslot[i][:, 0:nk, 0:ncols], v, [], [r_wslot[i]], f"wslot{i}")
            return wslot[i], r_wslot[i]

        xt, r_xt = T("xt", [128, 4, D])
        hT, r_hT = T("hT", [128, 8, 512], BF16)
        ssq, r_ssq = T("ssq", [128, 4])
        rstd, r_rstd = T("rstd", [128, 4])
        aT, r_aT = T("aT", [128, 24, 512], BF16)
        yA, yB, yC, yM = aT[:, 0:4, :], aT[:, 4:8, :], aT[:, 8:12, :], aT[:, 12:16, :]
        r_yA, r_yB, r_yC, r_yM = Res("yA"), Res("yB"), Res("yC"), Res("yM")
        mergedT = aT[:, 16:24, :]
        r_merged = Res("merged")
        xn = aT[:, 16:24, :].rearrange("p (s a) c -> p s (a c)", a=2)
        r_xn = r_merged
        uhalo, r_uhalo = T("uhalo", [128, 4, 2])
        G = [T(f"g{i}", [128, 514]) for i in range(6)]

        def g(i):
            return G[i][0][:, 0:512], G[i][1]

        memT = aT[:, 0:8, 0:256]
        r_memT = Res("memT")
        for i_, c0_ in ((0, 0), (1, 512), (2, 1024)):
            sc.dma("sp", G[i_][0][:, 0:512], cbig_d[:, c0_:c0_ + 512], [], [G[i_][1]], f"cbig{i_}")
        sc.op("dve", lambda e: e.tensor_copy(out=E4_bf[:, :], in_=G[0][0][:, 0:512]), [G[0][1]], [r_E4])

        def rmsnorm_to_F(src, r_src, nsub, gname, dst, r_dst):
            junk, r_junk = g(3)
            for s in range(nsub):
                for hf in range(2):
                    sc.op("act", lambda e: e.activation(out=junk, in_=src[:, s, hf * 512:(hf + 1) * 512],
                                                        func=AF.Square, accum_out=ssq2[:, 2 * s + hf:2 * s + hf + 1]),
                          [r_src], [r_junk, r_ssq2])
            sc.op("dve", lambda e: e.tensor_reduce(out=ssq[:, 0:nsub],
                                                   in_=ssq2[:, 0:2 * nsub].rearrange("p (s two) -> p s two", two=2),
                                                   axis=AX.X, op=ALU.add), [r_ssq2], [r_ssq])
            rsqrt(rstd[:, 0:nsub], r_rstd, ssq[:, 0:nsub], r_ssq, 1.0 / D)
            for s in range(nsub):
                if s % 2 == 0:
                    sc.op("dve", lambda e: e.tensor_scalar(out=xn[:, s, :], in0=src[:, s, :],
                                                           scalar1=rstd[:, s:s + 1], scalar2=None, op0=ALU.mult),
                          [r_src, r_rstd], [r_xn])
                else:
                    sc.op("act", lambda e: e.activation(out=xn[:, s, :], in_=src[:, s, :], func=AF.Copy,
                                                        scale=rstd[:, s:s + 1]), [r_src, r_rstd], [r_xn])
            for kc in range(8):
                pt, rpt = next_pst()
                for s in range(nsub):
                    sc.op("pe", lambda e: e.transpose(out=pt[:, s * 128:(s + 1) * 128],
                                                      in_=xn[:, s, kc * 128:(kc + 1) * 128], identity=ident_bf[:, :]),
                          [r_xn, r_identbf], [rpt])
                sc.op("act", lambda e: e.activation(out=dst[:, kc, 0:nsub * 128], in_=pt[:, 0:nsub * 128],
                                                    func=AF.Copy, scale=cc(gname, kc, 1)),
                      [rpt, r_consts], r_dst if isinstance(r_dst, list) else [r_dst])

        ssq2, r_ssq2 = T("ssq2", [128, 8])

        def proj_F(wsrc, c0, n128, consume, rhs, r_rhs, nk=8, krow0=0, ntok=512):
            for g0 in range(0, n128, 4):
                gn = min(4, n128 - g0)
                wt, rw = load_w(wsrc, c0 + g0 * 128, gn * 128, nk, krow0)
                for j in range(gn):
                    p, rp = next_ps()
                    for kc in range(nk):
                        sc.op("pe", lambda e: e.matmul(p[:, 0:ntok], lhsT=wt[:, kc, j * 128:(j + 1) * 128],
                                                       rhs=rhs[:, kc, 0:ntok], start=(kc == 0), stop=(kc == nk - 1)),
                              [rw, r_rhs], [rp])
                    consume(g0 + j, p, rp)

        def proj_T(wsrc, c0, ncols, consume, subs=(0, 1, 2, 3)):
            wt, rw = load_w(wsrc, c0, ncols, 8)
            for s in subs:
                p, rp = next_ps()
                for kc in range(8):
                    sc.op("pe", lambda e: e.matmul(p[:, 0:ncols], lhsT=hT[:, kc, s * 128:(s + 1) * 128],
                                                   rhs=wt[:, kc, 0:ncols], start=(kc == 0), stop=(kc == 7)),
                          [rw, r_hT], [rp])
                consume(s, p, rp)

        ident_f = cc("ident")
        if "B" in BR:
            posi, r_posi = T("posi", [128, NQ], I32)
            sc.dma("sp", posi[:, :], pos_d[:, :], [], [r_posi], "posi")
            posf, r_posf = T("posf", [128, NQ])
            sc.op("dve", lambda e: e.tensor_copy(out=posf[:, :], in_=posi[:, :]), [r_posi], [r_posf])
            cosT, r_cos = T("cosT", [128, NQ, 8])
            sinT, r_sin = T("sinT", [128, NQ, 8])
            ang, r_ang = G[3][0][:, 0:NQ * 8].rearrange("p (n j) -> p n j", j=8), G[3][1]
            sc.op("dve", lambda e: e.tensor_tensor(out=ang[:, :, :],
                                                   in0=posf[:, :].unsqueeze(2).to_broadcast([128, NQ, 8]),
                                                   in1=cc("invf").unsqueeze(1).to_broadcast([128, NQ, 8]),
                                                   op=ALU.mult), [r_posf, r_consts], [r_ang])
            TWO_PI = float(2 * np.pi)
            MAGIC = 12582912.0
            nrd, r_nrd = G[4][0][:, 0:NQ * 8].rearrange("p (n j) -> p n j", j=8), G[4][1]
            for (dst, rdst, shift) in ((sinT, r_sin, 0.0), (cosT, r_cos, 0.25)):
                sc.op("dve", lambda e: e.tensor_scalar(out=dst[:, :, :], in0=ang[:, :, :], scalar1=1.0 / TWO_PI,
                                                       scalar2=shift, op0=ALU.mult, op1=ALU.add), [r_ang], [rdst])
                sc.op("dve", lambda e: e.tensor_scalar(out=nrd[:, :, :], in0=dst[:, :, :], scalar1=MAGIC,
                                                       scalar2=None, op0=ALU.add), [rdst], [r_nrd])
                sc.op("dve", lambda e: e.tensor_scalar(out=nrd[:, :, :], in0=nrd[:, :, :], scalar1=MAGIC,
                                                       scalar2=None, op0=ALU.subtract), [r_nrd], [r_nrd])
                sc.op("dve", lambda e: e.tensor_tensor(out=dst[:, :, :], in0=dst[:, :, :], in1=nrd[:, :, :],
                                                       op=ALU.subtract), [rdst, r_nrd], [rdst])
                sc.op("dve", lambda e: e.tensor_scalar(out=dst[:, :, :], in0=dst[:, :, :], scalar1=-0.49999,
                                                       scalar2=0.49999, op0=ALU.max, op1=ALU.min), [rdst], [rdst])
                sc.op("act", lambda e: e.activation(out=dst[:, :, :], in_=dst[:, :, :], func=AF.Sin,
                                                    scale=TWO_PI), [rdst], [rdst])
            kikT, r_kik = T("kikT", [128, S], BF16)
            r_kikc = [Res(f"kik{c}") for c in range(NQ)]
            Vall, r_V = T("Vall", [128, NQ, 65], BF16)
            r_Vc = [Res(f"V{c}") for c in range(NQ)]
            sc.op("pool", lambda e: e.memset(Vall[:, :, 64:65], 1.0), [], [r_V])
            score, r_score = T("score", [128, S])
            qiq = [T(f"qiq{i}", [128, 1, 8, 128], BF16) for i in range(3)]
            iqTb = [q_[0] for q_ in qiq]
            r_iqTb = [Res(f"iqTb{i}") for i in range(3)]
            wq, r_wq = T("wq", [128, 4, 8])
            wabs, r_wabs = T("wabs", [128, 4, 8])
            wsgn, r_wsgn = T("wsgn", [128, 4, 8])
            rbb = [T(f"rbb{i}", [128, 512], BF16) for i in range(3)]
            Dm = [T(f"Dm{i}", [128, 8, 128], BF16) for i in range(2)]
            stg_iq, r_stgiq = T("stg_iq", [128, 8, 128])
            sc.op("pool", lambda e: e.memset(stg_iq[:, :, :], 0.0), [], [r_stgiq])
            PTb = [T(f"PT{i}", [128, 512], BF16) for i in range(3)]
            rt = [T(f"rt{i}", [128, 8, 8]) for i in range(4)]
            thr, r_thr = T("thr", [128, 1])
            thr0, r_thr0 = T("thr0", [128, 1])
            sc.op("pool", lambda e: e.memset(thr0[:, :], -1.0e8), [], [r_thr0])

        if "C" in BR:
            lb1, r_lb1 = T("lb1", [128, 512])
            sc.op("dve", lambda e: e.tensor_tensor(out=lb1[:, :], in0=G[2][0][:, 0:512], in1=G[1][0][:, 0:512],
                                                   op=ALU.subtract), [G[1][1], G[2][1]], [r_lb1])
            sc.op("act", lambda e: e.activation(out=lb1[:, :], in_=lb1[:, :], func=AF.Sigmoid), [r_lb1], [r_lb1])
            Sst, r_S = T("Sst", [128, 4, 128])
            sgg, r_sgg = aT[:, 16:20, :], r_merged
            ex, r_ex = T("ex", [128, 512])
            prod, r_prod = T("prod", [128, 512])
            Am, r_Am = T("Am", [128, 4, 64])
            vT_, r_vT = T("vTl", [128, 512])
            qTl, r_qTl = T("qTl", [128, 512])
            kTl, r_kTl = T("kTl", [128, 512])
            lfT, r_lfT = T("lfT", [128, 512])
            dl, r_dl = T("dl", [128, 4, 2])
            OSQ = None
            QD = [g(1 + h) for h in range(4)]


        if "B" in BR:
            RB = [g(3), g(4)]
            cjunk, r_cjunk = T("cjunk", [128, 1024], BF16)
            if DVE_FRAC < 1.0:
                ajunk, r_ajunk = T("ajunk", [128, 1024], BF16)
                asum, r_asum = T("asum", [128, 4])
            mid, r_mid = T("mid", [128, 1])
            cntp, _ = T("cntp", [128, 16])
            r_cntp = [Res(f"cntp{i}") for i in range(16)]
            r_cjh = [Res("cjh0"), Res("cjh1")]
            MBf, r_MB = T("MBf", [128, S], BF16)
            qTb = [q_[0] for q_ in qiq]
            r_qTb = [q_[1] for q_ in qiq]
            ob, r_ob = g(2)
            stg_k4, r_stgk4 = T("stg_k4", [128, 4, 128])
            sm, r_sm = T("sm", [128, 8])
            wk, r_wk = T("wk", [128, 32])
            sA, r_sA = T("sA", [128, 16])
            sB, r_sB = T("sB", [128, 16])
            sC, r_sC = T("sC", [128, 4])
            rec8, r_rec8 = T("rec8", [128, 8])

            def rope(v3, nh, tq, r_v):
                cb = cosT[:, tq, :].unsqueeze(1).to_broadcast([128, nh, 8])
                sb_ = sinT[:, tq, :].unsqueeze(1).to_broadcast([128, nh, 8])
                x1 = v3[:, :, 0:8]
                x2 = v3[:, :, 8:16]
                tt = [rt[i][0][:, 0:nh, :] for i in range(4)]
                rr = [rt[i][1] for i in range(4)]
                sc.op("pool", lambda e: e.tensor_tensor(out=tt[0], in0=x1, in1=cb, op=ALU.mult), [r_v, r_cos], [rr[0]])
                sc.op("pool", lambda e: e.tensor_tensor(out=tt[1], in0=x2, in1=sb_, op=ALU.mult), [r_v, r_sin], [rr[1]])
                sc.op("pool", lambda e: e.tensor_tensor(out=tt[2], in0=x1, in1=sb_, op=ALU.mult), [r_v, r_sin], [rr[2]])
                sc.op("pool", lambda e: e.tensor_tensor(out=tt[3], in0=x2, in1=cb, op=ALU.mult), [r_v, r_cos], [rr[3]])
                sc.op("pool", lambda e: e.tensor_tensor(out=x1, in0=tt[0], in1=tt[1], op=ALU.subtract),
                      [rr[0], rr[1]], [r_v])
                sc.op("pool", lambda e: e.tensor_tensor(out=x2, in0=tt[2], in1=tt[3], op=ALU.add),
                      [rr[2], rr[3]], [r_v])

            def dsa_tile(l, t):
                def cons_q(s, p, rp):
                    tq = 4 * t + s
                    sq, r_sq = g(0)
                    sc.op("act", lambda e: e.activation(out=sq, in_=p[:, :], func=AF.Square), [rp], [r_sq])
                    sc.op("dve", lambda e: e.tensor_reduce(out=sA[:, 0:8], in_=sq.rearrange("p (h d) -> p h d", h=8),
                                                           axis=AX.X, op=ALU.add), [r_sq], [r_sA])
                    rsqrt(sA[:, 8:16], r_sB, sA[:, 0:8], r_sA, 1.0 / 64)
                    qn, r_qn = g(1)
                    qn3 = qn.rearrange("p (h d) -> p h d", h=8)
                    sc.op("dve", lambda e: e.tensor_tensor(out=qn3, in0=p[:, :].rearrange("p (h d) -> p h d", h=8),
                                                           in1=sA[:, 8:16].unsqueeze(2).to_broadcast([128, 8, 64]),
                                                           op=ALU.mult), [rp, r_sB], [r_qn])
                    sc.op("pool", lambda e: e.tensor_tensor(out=qn3, in0=qn3,
                                                            in1=cc(f"dqn{l}").unsqueeze(1).to_broadcast([128, 8, 64]),
                                                            op=ALU.mult), [r_qn, r_consts], [r_qn])
                    rope(qn3, 8, tq, r_qn)

                def q_b(s):
                    qn, r_qn = g(1)
                    for half in range(2):
                        pt, rpt = next_ps()
                        for hh in range(4):
                            h = half * 4 + hh
                            sc.op("pe", lambda e: e.transpose(out=pt[0:64, hh * 128:(hh + 1) * 128],
                                                              in_=qn[:, h * 64:(h + 1) * 64], identity=ident_f),
                                  [r_qn, r_consts], [rpt])
                        sc.op("act", lambda e: e.copy(out=qTb[s % 3][0:64, 0, half * 4:(half + 1) * 4, :],
                                                      in_=pt[0:64, :].rearrange("p (h t) -> p h t", h=4)),
                              [rpt], [r_qTb[s % 3]])

                def cons_iq(s, p, rp):
                    tq = 4 * t + s
                    sc.op("act", lambda e: e.copy(out=stg_iq[:, :, 64:128],
                                                  in_=p[:, :].rearrange("p (h d) -> p h d", h=8)), [rp], [r_stgiq])
                    sc.op("pool", lambda e: e.tensor_tensor(
                        out=stg_iq[:, :, 64:128], in0=stg_iq[:, :, 64:128],
                        in1=wabs[:, s, :].unsqueeze(2).to_broadcast([128, 8, 64]), op=ALU.mult),
                        [r_stgiq, r_wabs], [r_stgiq])
                    rope(stg_iq[:, :, 64:128], 8, tq, r_stgiq)

                def iq_b(s):
                    for half in range(2):
                        pt, rpt = next_ps()
                        for hh in range(4):
                            h = half * 4 + hh
                            sc.op("pe", lambda e: e.transpose(out=pt[:, hh * 128:(hh + 1) * 128],
                                                              in_=stg_iq[:, h, :], identity=ident_f),
                                  [r_stgiq, r_consts], [rpt])
                        sc.op("act", lambda e: e.copy(out=iqTb[s % 3][64:128, 0, half * 4:(half + 1) * 4, :],
                                                      in_=pt[64:128, :].rearrange("p (h t) -> p h t", h=4)),
                              [rpt], [r_iqTb[s % 3]])

                def gen_proj(s, phase):
                    if phase == 0:
                        proj_T(wb_in[l], C_DQ, 512, cons_q, subs=(s,))
                        proj_T(wb_in[l], C_IQ, 512, cons_iq, subs=(s,))
                        return
                    q_b(s)
                    iq_b(s)
                    dm_, r_dm_ = Dm[s % 2]
                    for h in range(8):
                        sc.op("pool", lambda e: e.tensor_scalar(out=dm_[:, h, :], in0=ident_bf[:, :],
                                                                scalar1=wsgn[:, s, h:h + 1], scalar2=None,
                                                                op0=ALU.mult), [r_identbf, r_wsgn], [r_dm_])
                    return
                    yield

                def cons_kv(s, p, rp):
                    tq = 4 * t + s
                    jk, r_jk = g(2)
                    sc.op("act", lambda e: e.activation(out=jk[:, 0:64], in_=p[:, 0:64], func=AF.Square,
                                                        accum_out=sC[:, 0:1]), [rp], [r_jk, r_sC])
                    rsqrt(sC[:, 1:2], r_sC, sC[:, 0:1], r_sC, 1.0 / 64)
                    sc.op("dve", lambda e: e.tensor_scalar(out=stg_k4[:, s, 0:64], in0=p[:, 0:64], scalar1=sC[:, 1:2],
                                                           scalar2=None, op0=ALU.mult), [rp, r_sC], [r_stgk4])
                    sc.op("pool", lambda e: e.tensor_tensor(out=stg_k4[:, s, 0:64], in0=stg_k4[:, s, 0:64],
                                                            in1=cc(f"dkn{l}"), op=ALU.mult),
                          [r_stgk4, r_consts], [r_stgk4])
                    rope(stg_k4[:, s:s + 1, 0:64], 1, tq, r_stgk4)
                    sc.op("act", lambda e: e.copy(out=Vall[:, tq, 0:64], in_=p[:, 64:128]), [rp], [r_Vc[tq], r_V])

                def cons_ik(s, p, rp):
                    tq = 4 * t + s
                    sc.op("act", lambda e: e.copy(out=stg_k4[:, s, 64:128], in_=p[:, 0:64]), [rp], [r_stgk4])
                    rope(stg_k4[:, s:s + 1, 64:128], 1, tq, r_stgk4)
                    sc.op("dve", lambda e: e.tensor_scalar(out=wq[:, s, :], in0=p[:, 64:72],
                                                           scalar1=float(8 ** -0.5 * 64 ** -0.5), scalar2=None,
                                                           op0=ALU.mult), [rp], [r_wq])
                    sc.op("act", lambda e: e.activation(out=wsgn[:, s, :], in_=wq[:, s, :], func=AF.Sign),
                          [r_wq], [r_wsgn])
                    sc.op("dve", lambda e: e.tensor_tensor(out=wabs[:, s, :], in0=wq[:, s, :], in1=wsgn[:, s, :],
                                                           op=ALU.mult), [r_wq, r_wsgn], [r_wabs])
                    pt, rpt = next_ps()
                    sc.op("pe", lambda e: e.transpose(out=pt[:, 0:128], in_=stg_k4[:, s, :], identity=ident_f),
                          [r_stgk4, r_consts], [rpt])
                    sc.op("act", lambda e: e.copy(out=kikT[:, tq * 128:(tq + 1) * 128], in_=pt[:, 0:128]),
                          [rpt], [r_kikc[tq]])

                proj_T(wb_in[l], C_DK, 128, cons_kv)
                proj_T(wb_in[l], C_IK, 72, cons_ik)

                def gen_idx(s, phase):
                    j = 4 * t + s
                    L = 128 * (j + 1)
                    iqT_, r_iqT_ = iqTb[s % 3], r_iqTb[s % 3]
                    if phase == 0:
                        dm_, r_dm_ = Dm[s % 2]
                        for c in range(t + 1):
                            W = 512 if c < t else (s + 1) * 128
                            kres = [r_kikc[4 * c + i] for i in range(W // 128)]
                            for h in range(8):
                                px, rpx = next_ps()
                                sc.op("pe", lambda e: e.matmul(px[:, 0:W], lhsT=iqT_[64:128, 0, h, :],
                                                               rhs=kikT[64:128, c * 512:c * 512 + W], start=True,
                                                               stop=True), [r_iqT_] + kres, [rpx])
                                rb, r_rb = RB[h % 2]
                                sc.op("act", lambda e: e.activation(out=rb[:, 0:W], in_=px[:, 0:W], func=AF.Relu),
                                      [rpx], [r_rb])
                                if h == 0:
                                    sc.op("dve", lambda e: e.tensor_scalar(out=score[:, c * 512:c * 512 + W],
                                                                           in0=rb[:, 0:W], scalar1=wsgn[:, s, 0:1],
                                                                           scalar2=None, op0=ALU.mult),
                                          [r_rb, r_wsgn], [r_score])
                                else:
                                    sc.op("dve", lambda e: e.scalar_tensor_tensor(
                                        out=score[:, c * 512:c * 512 + W], in0=rb[:, 0:W], scalar=wsgn[:, s, h:h + 1],
                                        in1=score[:, c * 512:c * 512 + W], op0=ALU.mult, op1=ALU.add),
                                        [r_rb, r_wsgn, r_score], [r_score])
                            yield
                        sc.op("dve", lambda e: e.tensor_tensor(out=score[:, j * 128:(j + 1) * 128],
                                                               in0=score[:, j * 128:(j + 1) * 128], in1=cc("CB"),
                                                               op=ALU.add), [r_score, r_consts], [r_score])
                        return
                    if j >= 2 and phase == 1:
                        sc.op("dve", lambda e: e.tensor_reduce(out=sm[:, 0:1], in_=score[:, 0:256], axis=AX.X,
                                                               op=ALU.min), [r_score], [r_sm])
                        sc.op("dve", lambda e: e.tensor_reduce(out=sm[:, 1:2], in_=score[:, 0:L], axis=AX.X,
                                                               op=ALU.max), [r_score], [r_sm])
                        sc.op("dve", lambda e: e.tensor_scalar(out=sm[:, 2:3], in0=sm[:, 1:2], scalar1=sm[:, 0:1],
                                                               scalar2=0.5, op0=ALU.subtract, op1=ALU.mult),
                              [r_sm], [r_sm])
                        sc.op("dve", lambda e: e.tensor_scalar(out=wk[:, :], in0=cc("pow2"), scalar1=sm[:, 2:3],
                                                               scalar2=None, op0=ALU.mult), [r_sm, r_consts], [r_wk])
                        sc.op("dve", lambda e: e.tensor_tensor(out=mid[:, :], in0=sm[:, 0:1], in1=sm[:, 2:3],
                                                               op=ALU.add), [r_sm], [r_mid])
                        Ld = L if DVE_FRAC >= 1.0 else min(L, max(256, int(round(L * DVE_FRAC / 256.0)) * 256))
                        nact = L - Ld
                        for k in range(NBIS):
                            if nact > 0:
                                for ci_, c0_ in enumerate(range(Ld, L, 1024)):
                                    w_ = min(1024, L - c0_)
                                    sc.op("act", lambda e: e.activation(
                                        out=ajunk[:, 0:w_], in_=score[:, c0_:c0_ + w_], func=AF.Sign,
                                        bias=mid[:, 0:1], scale=-1.0, accum_out=asum[:, ci_:ci_ + 1]),
                                        [r_score, r_mid], [r_ajunk, r_asum])
                                nac = ci_ + 1
                            ccol = 4
                            for ci_, c0_ in enumerate(range(0, Ld, 512)):
                                w_ = min(512, Ld - c0_)
                                hj_ = ci_ % 2
                                sc.op("dve", lambda e: e.tensor_scalar(
                                    out=cjunk[:, hj_ * 512:hj_ * 512 + w_], in0=score[:, c0_:c0_ + w_],
                                    scalar1=mid[:, 0:1], scalar2=None, op0=ALU.is_ge,
                                    op1=ALU.add, accum_out=cntp[:, ci_:ci_ + 1]),
                                    [r_score, r_mid], [r_cntp[ci_], r_cjh[hj_]] + ([r_cjunk] if hj_ == 0 else []))
                            ncnt_ = ci_ + 1
                            sc.op("dve", lambda e: e.tensor_reduce(out=sm[:, 4:5], in_=cntp[:, 0:ncnt_], axis=AX.X,
                                                                   op=ALU.add), r_cntp[0:ncnt_], [r_sm])
                            if nact > 0:
                                for a_ in range(nac):
                                    sc.op("dve", lambda e: e.scalar_tensor_tensor(
                                        out=sm[:, ccol:ccol + 1], in0=asum[:, a_:a_ + 1], scalar=-0.5,
                                        in1=sm[:, ccol:ccol + 1], op0=ALU.mult, op1=ALU.add),
                                        [r_asum, r_sm], [r_sm])
                            sc.op("dve", lambda e: e.tensor_scalar(out=sm[:, 5:6], in0=sm[:, ccol:ccol + 1],
                                                                   scalar1=256.0 - nact / 2.0,
                                                                   scalar2=0.5, op0=ALU.is_ge, op1=ALU.subtract),
                                  [r_sm], [r_sm])
                            sc.op("dve", lambda e: e.scalar_tensor_tensor(out=mid[:, :], in0=sm[:, 5:6],
                                                                          scalar=wk[:, k:k + 1], in1=mid[:, :],
                                                                          op0=ALU.mult, op1=ALU.add),
                                  [r_sm, r_wk, r_mid], [r_mid])
                            yield
                        sc.op("dve", lambda e: e.tensor_tensor(out=thr[:, :], in0=mid[:, :],
                                                               in1=wk[:, NBIS:NBIS + 1], op=ALU.subtract),
                              [r_mid, r_wk], [r_thr])
                        return
                    if phase == 1:
                        return
                    th, r_th = (thr, r_thr) if j >= 2 else (thr0, r_thr0)
                    for c0_ in range(0, L, 2048):
                        w_ = min(2048, L - c0_)
                        sc.op("dve", lambda e: e.tensor_scalar(out=MBf[:, c0_:c0_ + w_], in0=score[:, c0_:c0_ + w_],
                                                               scalar1=th[:, 0:1], scalar2=MBNEG, op0=ALU.is_lt,
                                                               op1=ALU.mult), [r_score, r_th], [r_MB])
                        yield

                def gen_att(s):
                    j = 4 * t + s
                    qb = qTb[s % 3]
                    r_qb = r_qTb[s % 3]
                    pacc = [next_ps(pin=True), next_ps(pin=True)]
                    units = [(kc, gi) for kc in range(j + 1) for gi in range(2)]
                    LA = 2

                    def emit_logits(i):
                        kc, gi = units[i]
                        pl, rpl = next_ps()
                        sc.op("pe", lambda e: e.matmul(pl[:, :], lhsT=kikT[0:64, kc * 128:(kc + 1) * 128],
                                                       rhs=qb[0:64, 0, 4 * gi:4 * gi + 4, :], start=True,
                                                       stop=False), [r_kikc[kc], r_qb], [rpl])
                        sc.op("pe", lambda e: e.matmul(pl[:, :], lhsT=MBf[:, kc * 128:(kc + 1) * 128],
                                                       rhs=E4_bf[:, :], start=False, stop=True),
                              [r_MB, r_E4], [rpl])
                        PT, r_PT = PTb[i % 3]
                        sc.op("act", lambda e: e.activation(out=PT, in_=pl[:, :], func=AF.Exp, scale=0.125),
                              [rpl], [r_PT])

                    def emit_pv(i):
                        kc, gi = units[i]
                        PT, r_PT = PTb[i % 3]
                        pa_, rpa_ = pacc[gi]
                        for hh in range(4):
                            sc.op("pe", lambda e: e.matmul(pa_[:, hh * 65:(hh + 1) * 65],
                                                           lhsT=PT[:, hh * 128:(hh + 1) * 128],
                                                           rhs=Vall[:, kc, :], start=(kc == 0 and hh == 0),
                                                           stop=(kc == j), skip_group_check=True),
                                  [r_PT, r_Vc[kc], r_V], [rpa_])

                    for i in range(min(LA, len(units))):
                        emit_logits(i)
                    for i in range(len(units)):
                        if i + LA < len(units):
                            emit_logits(i + LA)
                        emit_pv(i)
                    yield "TAIL"
                    for gi in range(2):
                        pa_, rpa_ = pacc[gi]
                        a3 = pa_[:, 0:260].rearrange("p (h d) -> p h d", h=4)
                        sc.op("dve", lambda e: e.reciprocal(out=rec8[:, 4 * gi:4 * gi + 4], in_=a3[:, :, 64]),
                              [rpa_], [r_rec8])
                        sc.op("dve", lambda e: e.tensor_tensor(
                            out=ob[:, gi * 256:(gi + 1) * 256].rearrange("p (h d) -> p h d", h=4), in0=a3[:, :, 0:64],
                            in1=rec8[:, 4 * gi:4 * gi + 4].unsqueeze(2).to_broadcast([128, 4, 64]), op=ALU.mult),
                            [rpa_, r_rec8], [r_ob])
                        unpin(pa_)
                    pt, rpt = next_ps()
                    for k4 in range(4):
                        sc.op("pe", lambda e: e.transpose(out=pt[:, k4 * 128:(k4 + 1) * 128],
                                                          in_=ob[:, k4 * 128:(k4 + 1) * 128], identity=ident_f),
                              [r_ob, r_consts], [rpt])
                    sc.op("act", lambda e: e.copy(out=yB[:, :, s * 128:(s + 1) * 128],
                                                  in_=pt[:, :].rearrange("p (k t) -> p k t", k=4)), [rpt], [r_yB])
                    yield

                def drain(gen):
                    for _ in gen:
                        pass

                def interleave_n(items):
                    prog = [0] * len(items)
                    done = [False] * len(items)
                    while not all(done):
                        best = None
                        for i_, (g_, ex_) in enumerate(items):
                            if done[i_]:
                                continue
                            r_ = prog[i_] / float(ex_)
                            if best is None or r_ < best[0]:
                                best = (r_, i_)
                        i_ = best[1]
                        try:
                            next(items[i_][0])
                            prog[i_] += 1
                        except StopIteration:
                            done[i_] = True

                def run_main(gen):
                    for v_ in gen:
                        if v_ == "TAIL":
                            break
                    return gen

                drain(gen_proj(0, 0))
                drain(gen_proj(0, 1))
                pend = None
                for s in range(4):
                    drain(gen_idx(s, 0))
                    if pend is not None:
                        drain(pend)
                        pend = None
                    if s < 3:
                        drain(gen_proj(s + 1, 0))
                    drain(gen_idx(s, 1))
                    if s > 0:
                        pend = run_main(gen_att(s - 1))
                    drain(gen_idx(s, 2))
                    if s < 3:
                        drain(gen_proj(s + 1, 1))
                if pend is not None:
                    drain(pend)
                drain(gen_att(3))
        for l in range(n_layers):
            src_d = x_d if l == 0 else x1_d
            dst_d = y_d if l == n_layers - 1 else x1_d

            if "M" in BR:
                sc.dma("sp", xt[:, 0:2, :], mem_d[:, :].rearrange("(s p) d -> p s d", p=128), [], [r_xt], "xt")
                rmsnorm_to_F(xt, r_xt, 2, "memn", memT, [r_memT, r_yA, r_yB])
                if l == 0:
                    mkT, r_mkT = T("mkT", [128, 4, 256], BF16)
                    mv, r_mv = T("mv", [128, 2, 512], BF16)
                    gkq, r_gkq = T("gkq", [128, 1])
                sc.op("dve", lambda e: e.tensor_tensor(out=gkq[:, :], in0=cc(f"mqn{l}"), in1=cc(f"mkn{l}"),
                                                       op=ALU.mult), [r_consts], [r_gkq])
                sc.op("dve", lambda e: e.tensor_scalar(out=gkq[:, :], in0=gkq[:, :], scalar1=128.0 ** -0.5,
                                                       scalar2=None, op0=ALU.mult), [r_gkq], [r_gkq])

                def cons_mk(j, p, rp):
                    sq, r_sq = g(0)
                    sc.op("act", lambda e: e.activation(out=sq[:, 0:256], in_=p[:, 0:256], func=AF.Square),
                          [rp], [r_sq])
                    sqb, r_sqb = MQB
                    sc.op("pool", lambda e: e.tensor_copy(out=sqb[:, 0:256], in_=sq[:, 0:256]), [r_sq], [r_sqb])
                    p2, rp2 = next_ps()
                    sc.op("pe", lambda e: e.matmul(p2[:, 0:256], lhsT=ones_bf[:, :], rhs=sqb[:, 0:256], start=True,
                                                   stop=True), [r_ones, r_sqb], [rp2])
                    rs, r_rs = g(1)
                    rsqrt(rs[:, 0:256], r_rs, p2[:, 0:256], rp2, 1.0 / 128)
                    sc.op("dve", lambda e: e.tensor_tensor(out=rs[:, 0:256], in0=rs[:, 0:256], in1=p[:, 0:256],
                                                           op=ALU.mult), [r_rs, rp], [r_rs])
                    sc.op("dve", lambda e: e.tensor_scalar(out=mkT[:, j, :], in0=rs[:, 0:256], scalar1=gkq[:, 0:1],
                                                           scalar2=None, op0=ALU.mult), [r_rs, r_gkq], [r_mkT])

                if l == 0:
                    MQB = PTb[2] if "B" in BR else T("mqb", [128, 512], BF16)
                proj_F(wb_kv[l], 0, 4, cons_mk, memT, r_memT, ntok=256)
                wt, rw = load_w(wb_kv[l], 512, 512, 8)
                for mc in range(2):
                    p, rp = next_ps()
                    for kc in range(8):
                        sc.op("pe", lambda e: e.matmul(p[:, :], lhsT=memT[:, kc, mc * 128:(mc + 1) * 128],
                                                       rhs=wt[:, kc, :], start=(kc == 0), stop=(kc == 7)),
                              [rw, r_memT], [rp])
                    sc.op("act", lambda e: e.copy(out=mv[:, mc, :], in_=p[:, :]), [rp], [r_mv])
                for r_ in (r_yA, r_yB):
                    r_.r.update(r_memT.r)

            if "C" in BR:
                sc.op("pool", lambda e: e.memset(Sst[:, :, :], 0.0), [], [r_S])
                if l == 0:
                    OSQ_ = rbb[0] if "B" in BR else T("osq", [128, 512], BF16)

            for t in range(NT):
                t0 = t * 512
                sc.dma("sp", xt[:, :, :], src_d[t0:t0 + 512, :].rearrange("(s p) d -> p s d", p=128),
                       [], [r_xt], "xt")
                rmsnorm_to_F(xt, r_xt, 4, f"nmix{l}", hT, r_hT)
                branches = []

                if "A" in BR:
                    U = [g(i) for i in range(4)]
                    UT = [G[i][0] for i in range(4)]
                    for c in range(4):
                        if t == 0:
                            sc.op("pool", lambda e: e.memset(UT[c][:, 0:2], 0.0), [], [G[c][1]])
                        else:
                            sc.op("pool", lambda e: e.tensor_copy(out=UT[c][:, 0:2], in_=uhalo[:, c, :]),
                                  [r_uhalo], [G[c][1]])

                    def cons_ax(j, p, rp):
                        sc.op("act", lambda e: e.copy(out=UT[j][:, 2:514], in_=p[:, :]), [rp], [G[j][1]])

                    proj_F(wb_in[l], C_AX, 4, cons_ax, hT, r_hT)

                    def cons_ac(j, p, rp):
                        sc.op("dve", lambda e: e.tensor_tensor(out=UT[j][:, 2:514], in0=UT[j][:, 2:514], in1=p[:, :],
                                                               op=ALU.mult), [rp, G[j][1]], [G[j][1]])
                        sc.op("pool", lambda e: e.tensor_copy(out=uhalo[:, j, :], in_=UT[j][:, 512:514]),
                              [G[j][1]], [r_uhalo])

                    proj_F(wb_in[l], C_AC, 4, cons_ac, hT, r_hT)

                    def cons_ab(j, p, rp):
                        cw = lay[f"convw{l}"][0]
                        ctmp, r_ctmp = g(4 + (j % 2))
                        sc.op("pool", lambda e: e.tensor_scalar(out=ctmp, in0=UT[j][:, 0:512],
                                                                scalar1=consts[:, cw + j:cw + j + 1], scalar2=None,
                                                                op0=ALU.mult), [G[j][1], r_consts], [r_ctmp])
                        sc.op("dve", lambda e: e.scalar_tensor_tensor(out=ctmp, in0=UT[j][:, 1:513],
                                                                      scalar=consts[:, cw + 4 + j:cw + 5 + j],
                                                                      in1=ctmp, op0=ALU.mult, op1=ALU.add),
                              [G[j][1], r_consts, r_ctmp], [r_ctmp])
                        sc.op("dve", lambda e: e.scalar_tensor_tensor(out=ctmp, in0=UT[j][:, 2:514],
                                                                      scalar=consts[:, cw + 8 + j:cw + 9 + j],
                                                                      in1=ctmp, op0=ALU.mult, op1=ALU.add),
                              [G[j][1], r_consts, r_ctmp], [r_ctmp])
                        sc.op("dve", lambda e: e.tensor_tensor(out=yA[:, j, :], in0=ctmp, in1=p[:, :], op=ALU.mult),
                              [rp, r_ctmp], [r_yA])

                    proj_F(wb_in[l], C_AB, 4, cons_ab, hT, r_hT)
                    branches.append((0, yA, r_yA))

                if "M" in BR:
                    def cons_mq(j, p, rp):
                        sq, r_sq = MQB
                        sc.op("act", lambda e: e.activation(out=sq, in_=p[:, :], func=AF.Square), [rp], [r_sq])
                        p2, rp2 = next_ps()
                        sc.op("pe", lambda e: e.matmul(p2[:, :], lhsT=ones_bf[:, :], rhs=sq, start=True, stop=True),
                              [r_ones, r_sq], [rp2])
                        rs, r_rs = g(4)
                        rsqrt(rs, r_rs, p2[:, :], rp2, 1.0 / 128)
                        mqn, r_mqn = MQN
                        sc.op("dve", lambda e: e.tensor_tensor(out=mqn, in0=rs, in1=p[:, :], op=ALU.mult),
                              [r_rs, rp], [r_mqn])
                        pts = []
                        for mc in range(2):
                            pl, rpl = next_ps()
                            sc.op("pe", lambda e: e.matmul(pl[:, :], lhsT=mkT[:, j, mc * 128:(mc + 1) * 128], rhs=mqn,
                                                           start=True, stop=True), [r_mkT, r_mqn], [rpl])
                            PT, r_PT = MPT[mc]
                            sc.op("act", lambda e: e.activation(out=PT, in_=pl[:, :], func=AF.Exp), [rpl], [r_PT])
                            pts.append((PT, r_PT))
                        py, rpy = next_ps()
                        psm, rpsm = next_ps()
                        for mc in range(2):
                            PT, r_PT = pts[mc]
                            sc.op("pe", lambda e: e.matmul(py[:, :], lhsT=mv[:, mc, j * 128:(j + 1) * 128], rhs=PT,
                                                           start=(mc == 0), stop=(mc == 1)), [r_mv, r_PT], [rpy])
                        for mc in range(2):
                            PT, r_PT = pts[mc]
                            sc.op("pe", lambda e: e.matmul(psm[:, :], lhsT=ones_bf[:, :], rhs=PT,
                                                           start=(mc == 0), stop=(mc == 1)), [r_ones, r_PT], [rpsm])
                        rc, r_rc = g(5)
                        sc.op("dve", lambda e: e.reciprocal(out=rc, in_=psm[:, :]), [rpsm], [r_rc])
                        sc.op("dve", lambda e: e.tensor_tensor(out=yM[:, j, :], in0=rc, in1=py[:, :], op=ALU.mult),
                              [r_rc, rpy], [r_yM])

                    if l == 0 and t == 0:
                        if "B" in BR:
                            MQN = (cjunk[:, 0:512], r_cjunk)
                            MPT = [PTb[0], PTb[1]]
                        else:
                            MQN = T("mqn", [128, 512], BF16)
                            MPT = [T(f"mpt{i}", [128, 512], BF16) for i in range(2)]
                    proj_F(wb_in[l], C_MQ, 4, cons_mq, hT, r_hT)
                    branches.append((3, yM, r_yM))

                if "C" in BR:
                    def cons_gg(j, p, rp):
                        sc.op("act", lambda e: e.activation(out=sgg[:, j, :], in_=p[:, :], func=AF.Silu), [rp], [r_sgg])

                    proj_F(wb_in[l], C_GG, 4, cons_gg, hT, r_hT)
                    for s in range(4):
                        tk = slice(s * 128, (s + 1) * 128)

                        def tproj(c0):
                            wt, rw = load_w(wb_in[l], c0, 512, 8)
                            p, rp = next_ps()
                            for kc in range(8):
                                sc.op("pe", lambda e: e.matmul(p[:, :], lhsT=hT[:, kc, tk], rhs=wt[:, kc, :],
                                                               start=(kc == 0), stop=(kc == 7)), [rw, r_hT], [rp])
                            return p, rp

                        p, rp = tproj(C_GQ)
                        sc.op("act", lambda e: e.activation(out=qTl[:, :], in_=p[:, :], func=AF.Silu), [rp], [r_qTl])
                        p, rp = tproj(C_GI)
                        sc.op("act", lambda e: e.copy(out=vT_[:, :], in_=p[:, :]), [rp], [r_vT])
                        p, rp = tproj(C_GF)
                        sc.op("act", lambda e: e.activation(out=lfT[:, :], in_=p[:, :], func=AF.Sigmoid),
                              [rp], [r_lfT])
                        if l == 1:
                            sc.op("dve", lambda e: e.tensor_tensor(out=kTl[:, :], in0=lfT[:, :], in1=lb1[:, :],
                                                                   op=ALU.mult), [r_lfT, r_lb1], [r_kTl])
                            sc.op("dve", lambda e: e.tensor_tensor(out=lfT[:, :], in0=lfT[:, :], in1=kTl[:, :],
                                                                   op=ALU.subtract), [r_lfT, r_kTl], [r_lfT])
                            sc.op("dve", lambda e: e.tensor_tensor(out=lfT[:, :], in0=lfT[:, :], in1=lb1[:, :],
                                                                   op=ALU.add), [r_lfT, r_lb1], [r_lfT])
                        sc.op("dve", lambda e: e.tensor_scalar(out=kTl[:, :], in0=lfT[:, :], scalar1=-1.0, scalar2=1.0,
                                                               op0=ALU.mult, op1=ALU.add), [r_lfT], [r_kTl])
                        sc.op("act", lambda e: e.activation(out=lfT[:, :], in_=lfT[:, :], func=AF.Ln), [r_lfT], [r_lfT])
                        for h in range(4):
                            hc = slice(h * 128, (h + 1) * 128)
                            pc, rpc = next_ps()
                            sc.op("pe", lambda e: e.matmul(pc[:, 0:128], lhsT=lfT[:, hc], rhs=cc("M1"), start=True,
                                                           stop=True), [r_lfT, r_consts], [rpc])
                            sc.op("pe", lambda e: e.matmul(pc[:, 128:256], lhsT=lfT[:, hc], rhs=cc("Ublk"), start=True,
                                                           stop=True), [r_lfT, r_consts], [rpc])
                            sc.op("pe", lambda e: e.matmul(pc[:, 256:384], lhsT=cc("SL"), rhs=lfT[:, hc], start=True,
                                                           stop=True), [r_lfT, r_consts], [rpc])
                            sc.op("act", lambda e: e.activation(out=ex[:, 0:128], in_=pc[:, 0:128], func=AF.Exp),
                                  [rpc], [r_ex])
                            sc.op("act", lambda e: e.activation(out=ex[:, 128:256], in_=pc[:, 0:128], func=AF.Exp,
                                                                scale=-1.0), [rpc], [r_ex])
                            sc.op("act", lambda e: e.activation(out=ex[:, 256:512], in_=pc[:, 128:384], func=AF.Exp),
                                  [rpc], [r_ex])
                            ptq, rptq = next_ps()
                            sc.op("pe", lambda e: e.transpose(out=ptq[:, 0:128], in_=qTl[:, hc], identity=ident_f),
                                  [r_qTl, r_consts], [rptq])
                            sc.op("pe", lambda e: e.transpose(out=ptq[:, 128:256], in_=kTl[:, hc], identity=ident_f),
                                  [r_kTl, r_consts], [rptq])
                            qd, r_qd = QD[h]
                            sc.op("dve", lambda e: e.tensor_tensor(out=qd[:, 0:128], in0=ptq[:, 0:128],
                                                                   in1=ex[:, 0:128], op=ALU.mult),
                                  [rptq, r_ex], [r_qd])
                            sc.op("dve", lambda e: e.tensor_tensor(out=qd[:, 128:256], in0=ptq[:, 128:256],
                                                                   in1=ex[:, 128:256], op=ALU.mult),
                                  [rptq, r_ex], [r_qd])
                            sc.op("dve", lambda e: e.tensor_tensor(out=qd[:, 256:384], in0=ptq[:, 0:128],
                                                                   in1=ex[:, 256:384], op=ALU.mult),
                                  [rptq, r_ex], [r_qd])
                            sc.op("pool", lambda e: e.tensor_tensor(out=qd[:, 384:512], in0=kTl[:, hc],
                                                                    in1=ex[:, 384:512], op=ALU.mult),
                                  [r_kTl, r_ex], [r_qd])
                            sc.op("pool", lambda e: e.tensor_copy(out=dl[:, h, 0:1], in_=ex[:, 256 + 63:256 + 64]),
                                  [r_ex], [r_dl])
                            sc.op("pool", lambda e: e.tensor_copy(out=dl[:, h, 1:2], in_=ex[:, 256 + 127:256 + 128]),
                                  [r_ex], [r_dl])
                        po, rpo = next_ps(pin=True)
                        for c in range(2):
                            rows = slice(64 * c, 64 * c + 64)
                            cs = slice(64 * c, 64 * c + 64)
                            pa, rpa = next_ps()
                            pd, rpd = next_ps()
                            for h in range(4):
                                qd, r_qd = QD[h]
                                hc = slice(h * 128, (h + 1) * 128)
                                sc.op("pe", lambda e: e.matmul(pa[rows, h * 64:(h + 1) * 64],
                                                               lhsT=qd[:, 128 + 64 * c:128 + 64 * c + 64],
                                                               rhs=qd[:, 64 * c:64 * c + 64], start=True, stop=True),
                                      [r_qd], [rpa])
                                sc.op("pe", lambda e: e.matmul(po[:, h * 128 + 64 * c:h * 128 + 64 * c + 64],
                                                               lhsT=Sst[:, h, :],
                                                               rhs=qd[:, 256 + 64 * c:256 + 64 * c + 64],
                                                               start=(c == 0 and h == 0), stop=False,
                                                               skip_group_check=True), [r_S, r_qd], [rpo])
                            sc.op("dve", lambda e: e.tensor_tensor(
                                out=Am[rows, :, :], in0=pa[rows, 0:256].rearrange("p (h t) -> p h t", h=4),
                                in1=cc("triu")[rows, :].unsqueeze(1).to_broadcast([64, 4, 64]), op=ALU.mult),
                                [rpa, r_consts], [r_Am])
                            for h in range(4):
                                qd, r_qd = QD[h]
                                hc = slice(h * 128, (h + 1) * 128)
                                sc.op("pe", lambda e: e.matmul(po[:, h * 128 + 64 * c:h * 128 + 64 * c + 64],
                                                               lhsT=vT_[rows, hc], rhs=Am[rows, h, :],
                                                               start=False, stop=True, skip_group_check=True),
                                      [r_vT, r_Am], [rpo])
                                sc.op("pe", lambda e: e.matmul(pd[:, hc], lhsT=qd[rows, 384:512], rhs=vT_[rows, hc],
                                                               start=True, stop=True), [r_qd, r_vT], [rpd])
                            for h in range(4):
                                hc = slice(h * 128, (h + 1) * 128)
                                sc.op("dve", lambda e: e.scalar_tensor_tensor(out=Sst[:, h, :], in0=Sst[:, h, :],
                                                                              scalar=dl[:, h, c:c + 1], in1=pd[:, hc],
                                                                              op0=ALU.mult, op1=ALU.add),
                                      [r_S, r_dl, rpd], [r_S])
                        sqb, r_sqb = OSQ_
                        sc.op("act", lambda e: e.activation(out=sqb, in_=po[:, :], func=AF.Square), [rpo], [r_sqb])
                        p2, rp2 = next_ps()
                        sc.op("pe", lambda e: e.matmul(p2[:, :], lhsT=ones_bf[:, :], rhs=sqb, start=True, stop=True),
                              [r_ones, r_sqb], [rp2])
                        rs, r_rs = g(0)
                        rsqrt(rs, r_rs, p2[:, :], rp2, 1.0 / 128)
                        sc.op("dve", lambda e: e.tensor_tensor(out=rs, in0=rs, in1=po[:, :], op=ALU.mult),
                              [r_rs, rpo], [r_rs])
                        sc.op("dve", lambda e: e.scalar_tensor_tensor(
                            out=yC[:, :, tk], in0=rs.rearrange("p (h t) -> p h t", h=4), scalar=cc(f"hgn{l}"),
                            in1=sgg[:, :, tk], op0=ALU.mult, op1=ALU.mult), [r_rs, r_consts, r_sgg], [r_yC])
                        unpin(po)
                    branches.append((2, yC, r_yC))

                if "B" in BR:
                    dsa_tile(l, t)
                    branches.append((1, yB, r_yB))

                branches.sort(key=lambda b: b[0])
                gsig, r_gsig = g(4)
                gl, r_gl = g(5)
                for mg in range(2):
                    for bi, (n, yb, ryb) in enumerate(branches):
                        wt, rw = load_w(wb_in[l], C_GATE + n * D + mg * 512, 512, 8)
                        wl, rwl = load_w(wb_lift[l], mg * 512, 512, 4, krow0=n * 512)
                        last_n = (bi == len(branches) - 1)
                        for mm in range(4):
                            m = mg * 4 + mm
                            macc, r_macc = g(mm)
                            pg, rpg = next_ps()
                            for kc in range(8):
                                sc.op("pe", lambda e: e.matmul(pg[:, :], lhsT=wt[:, kc, mm * 128:(mm + 1) * 128],
                                                               rhs=hT[:, kc, :], start=(kc == 0), stop=(kc == 7)),
                                      [rw, r_hT], [rpg])
                            pl, rpl = next_ps()
                            for kc in range(4):
                                sc.op("pe", lambda e: e.matmul(pl[:, :], lhsT=wl[:, kc, mm * 128:(mm + 1) * 128],
                                                               rhs=yb[:, kc, :], start=(kc == 0), stop=(kc == 3)),
                                      [rwl, ryb], [rpl])
                            sc.op("act", lambda e: e.activation(out=gsig, in_=pg[:, :], func=AF.Sigmoid),
                                  [rpg], [r_gsig])
                            dst, rdst = (mergedT[:, m, :], r_merged) if last_n else (macc, r_macc)
                            if bi == 0:
                                sc.op("dve", lambda e: e.tensor_tensor(out=dst, in0=gsig, in1=pl[:, :], op=ALU.mult),
                                      [r_gsig, rpl], [rdst])
                            else:
                                sc.op("dve", lambda e: e.tensor_tensor(out=gl, in0=gsig, in1=pl[:, :], op=ALU.mult),
                                      [r_gsig, rpl], [r_gl])
                                sc.op("pool", lambda e: e.tensor_tensor(out=dst, in0=gl, in1=macc, op=ALU.add),
                                      [r_gl, r_macc], [rdst])

                for half in range(2):
                    wt, rw = load_w(wb_out[l], half * 512, 512, 8)
                    for s in range(4):
                        p, rp = next_ps()
                        for kc in range(8):
                            sc.op("pe", lambda e: e.matmul(p[:, :], lhsT=mergedT[:, kc, s * 128:(s + 1) * 128],
                                                           rhs=wt[:, kc, :], start=(kc == 0), stop=(kc == 7)),
                                  [rw, r_merged], [rp])
                        sc.op("dve", lambda e: e.tensor_tensor(out=xt[:, s, half * 512:(half + 1) * 512],
                                                               in0=xt[:, s, half * 512:(half + 1) * 512], in1=p[:, :],
                                                               op=ALU.add), [rp, r_xt], [r_xt])

                rmsnorm_to_F(xt, r_xt, 4, f"nffn{l}", hT, r_hT)
                allres = [r_aT, r_yA, r_yB, r_yC, r_yM, r_merged]

                def cons_gate(j, p, rp):
                    sc.op("act", lambda e: e.activation(out=aT[:, j, :], in_=p[:, :], func=AF.Silu), [rp], allres)

                proj_F(wb_up[l], 0, 22, cons_gate, hT, r_hT)

                def cons_up(j, p, rp):
                    sc.op("dve", lambda e: e.tensor_tensor(out=aT[:, j, :], in0=aT[:, j, :], in1=p[:, :], op=ALU.mult),
                          [rp, r_aT], allres)

                proj_F(wb_up[l], FFN, 22, cons_up, hT, r_hT)

                for half in range(2):
                    pss = [next_ps() for _ in range(4)]
                    for gq in range(3):
                        nk = 8 if gq < 2 else 6
                        wt, rw = load_w(wb_dn[l], half * 512, 512, nk, krow0=gq * 1024)
                        for s in range(4):
                            p, rp = pss[s]
                            for kc in range(nk):
                                fc = gq * 8 + kc
                                sc.op("pe", lambda e: e.matmul(p[:, :], lhsT=aT[:, fc, s * 128:(s + 1) * 128],
                                                               rhs=wt[:, kc, :], start=(fc == 0), stop=(fc == 21)),
                                      [rw] + allres, [rp])
                    for s in range(4):
                        p, rp = pss[s]
                        sc.op("dve", lambda e: e.tensor_tensor(out=xt[:, s, half * 512:(half + 1) * 512],
                                                               in0=xt[:, s, half * 512:(half + 1) * 512], in1=p[:, :],
                                                               op=ALU.add), [rp, r_xt], [r_xt])
                sc.dma("sp", dst_d[t0:t0 + 512, :].rearrange("(s p) d -> p s d", p=128), xt[:, :, :],
                       [r_xt], [], "xo")
            sc.wait_tok("sp", (sc.dsem["xo"][0], sc.dsem["xo"][1], None, "d_xo"))
        sc.wait_tok("sp", (sc.dsem["xo"][0], sc.dsem["xo"][1], None, "d_xo"))
        print("ops:", sc.cnt, "waits:", sc.nwait, "sbuf bytes/partition:", sbtot[0])
    return nc


def host_consts(inp):
    lay, NCONST = _const_layout()
    c = np.zeros((128, NCONST), np.float32)

    def put(name, arr):
        o, w = lay[name]
        c[:, o:o + w] = arr

    p = np.arange(128)
    put("ident", np.eye(128, dtype=np.float32))
    same = (p[:, None] // 64) == (p[None, :] // 64)
    U = (same & (p[:, None] <= p[None, :])).astype(np.float32)
    Rb = (same & ((p[:, None] % 64) <= 31)).astype(np.float32)
    SL = (same & (p[:, None] > p[None, :])).astype(np.float32)
    put("Ublk", U)
    put("M1", U - Rb)
    put("SL", SL)
    put("triu", ((p[:, None] % 64) <= np.arange(64)[None, :]).astype(np.float32))
    put("CB", np.where(p[None, :] <= p[:, None], 0.0, NEG).astype(np.float32))
    put("pow2", np.tile((0.5 ** np.arange(32, dtype=np.float64)).astype(np.float32)[None, :], (128, 1)))
    invf = 1.0 / (500000.0 ** (np.arange(0, 16, 2, dtype=np.float32) / 16.0))
    put("invf", np.tile(invf.astype(np.float32)[None, :], (128, 1)))
    put("memn", np.asarray(inp["mem_norm"]).reshape(8, 128).T)
    for l in range(2):
        put(f"nmix{l}", np.asarray(inp["norm_mix"][l]).reshape(8, 128).T)
        put(f"nffn{l}", np.asarray(inp["norm_ffn"][l]).reshape(8, 128).T)
        cw = np.asarray(inp["conv_w"][l])
        put(f"convw{l}", np.concatenate([cw[k].reshape(4, 128).T for k in range(3)], axis=1))
        put(f"hgn{l}", np.asarray(inp["hgrn_out_norm"][l]).reshape(128, 1))
        put(f"mqn{l}", np.asarray(inp["mem_q_norm"][l]).reshape(128, 1))
        put(f"mkn{l}", np.asarray(inp["mem_k_norm"][l]).reshape(128, 1))
        put(f"dqn{l}", np.tile(np.asarray(inp["dsa_q_norm"][l])[None, :], (128, 1)))
        put(f"dkn{l}", np.tile(np.asarray(inp["dsa_k_norm"][l])[None, :], (128, 1)))
    lbr = np.asarray(inp["hgrn_lower_bounds"], dtype=np.float32)
    cbig = np.concatenate([np.tile(np.eye(128, dtype=np.float32), (1, 4)), np.tile(lbr[0][None, :], (128, 1)),
                           np.tile(lbr[1][None, :], (128, 1))], axis=1).astype(np.float32)
    return c, cbig


def make_in_maps(inp, S, nb):
    consts, cbig = host_consts(inp)
    f = lambda a: np.ascontiguousarray(np.asarray(a, dtype=np.float32))
    shared = dict(
        consts=consts,
        cbig=cbig,
        w_in=f(inp["w_in"]),
        mem_w_kv=f(inp["mem_w_kv"]),
        w_lift=f(inp["w_lift"]).reshape(2, 2048, D),
        w_out=f(inp["w_out"]),
        ffn_w_up=f(inp["ffn_w_up"]),
        ffn_w_down=f(inp["ffn_w_down"]),
    )
    maps = []
    for b in range(nb):
        m = dict(shared)
        m["x"] = f(inp["x"][b, :S])
        m["mem"] = f(inp["mem"][b])
        m["pos"] = np.ascontiguousarray(np.asarray(inp["positions"][b, :S], dtype=np.int32).reshape(S // 128, 128).T)
        maps.append(m)
    return maps


def kernel(**inputs):
    S = inputs["x"].shape[1]
    B = inputs["x"].shape[0]
    nc = build(S)
    maps = make_in_maps(inputs, S, B)
    maps = maps + maps
    res = run_bass_kernel_spmd(nc, maps, core_ids=list(range(8)))
    out = np.stack([res.results[b]["y"] for b in range(B)], axis=0)
    return out.astype(np.float32)
```
